# Optimizing a Trainium2 kernel written in Bass

```python
import jax, jax.numpy as jnp
from jax import lax
import numpy as np

D_MODEL = 1024
BATCH = 8
SEQ = 8192
DEPTH = 2

N_MIXERS = 2
POOL_WINDOWS = (2, 4, 8, 16)
N_POOL_GROUPS = len(POOL_WINDOWS)
POOL_GROUP_DIM = D_MODEL // N_POOL_GROUPS
CONV_WIDTH = 31
FFN_DIM = ((8 * D_MODEL // 3 + 255) // 256) * 256
FFN_CONV_WIDTH = 3
N_MOD = 6
EPS = 1e-6
N_POOL_LAYERS = (DEPTH + 1) // 2
N_CONV_LAYERS = DEPTH // 2

kernel_name = "hybrid_pool_conformer_convffn_trunk"


def rms_norm(x, g):
    xf = x.astype(jnp.float32)
    y = xf * lax.rsqrt(jnp.mean(xf * xf, axis=-1, keepdims=True) + EPS)
    return (y * g.astype(jnp.float32)).astype(x.dtype)


def layer_norm(x, g, b):
    xf = x.astype(jnp.float32)
    mu = jnp.mean(xf, axis=-1, keepdims=True)
    var = jnp.mean(jnp.square(xf - mu), axis=-1, keepdims=True)
    y = (xf - mu) * lax.rsqrt(var + EPS)
    return (y * g.astype(jnp.float32) + b.astype(jnp.float32)).astype(x.dtype)


def causal_depthwise_conv(x, w):
    k = w.shape[0]
    return lax.conv_general_dilated(
        x, w[:, None, :].astype(x.dtype), window_strides=(1,),
        padding=((k - 1, 0),), dimension_numbers=("NWC", "WIO", "NWC"),
        feature_group_count=x.shape[-1])


def causal_mean_pool(u, window):
    s = u.shape[1]
    cs = jnp.cumsum(u.astype(jnp.float32), axis=1)
    lag = jnp.pad(cs, ((0, 0), (window, 0), (0, 0)))[:, :s]
    cnt = jnp.minimum(jnp.arange(1, s + 1), window).astype(jnp.float32)
    return ((cs - lag) / cnt[None, :, None]).astype(u.dtype)


def pool_mixer(h, w_groups, scale):
    b, s, d = h.shape
    hg = h.reshape(b, s, N_POOL_GROUPS, POOL_GROUP_DIM)
    pooled = jnp.stack(
        [causal_mean_pool(hg[:, :, g], POOL_WINDOWS[g]) - hg[:, :, g]
         for g in range(N_POOL_GROUPS)], axis=2)
    y = jnp.einsum("bsgc,gcd->bsgd", pooled, w_groups).reshape(b, s, d)
    return y * scale


def conformer_conv_module(h, w_pw1, b_pw1, w_dw, b_dw, ln_g, ln_b, w_pw2, b_pw2):
    a = h @ w_pw1 + b_pw1
    val, gt = jnp.split(a, 2, axis=-1)
    u = val * jax.nn.sigmoid(gt)
    u = causal_depthwise_conv(u, w_dw) + b_dw
    u = jax.nn.silu(layer_norm(u, ln_g, ln_b))
    return u @ w_pw2 + b_pw2


def conv_ffn(h, w_up, w_dw, w_down):
    a = causal_depthwise_conv(h @ w_up, w_dw)
    g, v = jnp.split(a, 2, axis=-1)
    return (jax.nn.silu(g) * v) @ w_down


def setup_inputs(seed: int = 0) -> dict:
    key = jax.random.key(seed)
    ks = jax.random.split(key, 24)
    d, f = D_MODEL, FFN_DIM
    nrm = lambda k, shape, s: jax.random.normal(k, shape, jnp.float32) * s
    return {
        "x": nrm(ks[0], (BATCH, SEQ, d), 1.0),
        "c": nrm(ks[1], (BATCH, d), 1.0),
        "ada_w": nrm(ks[2], (DEPTH, d, N_MOD * d), 0.5 * d ** -0.5),
        "ada_b": nrm(ks[3], (DEPTH, N_MOD * d), 0.02),
        "pre_g": 1.0 + nrm(ks[4], (DEPTH, 2, d), 0.05),
        "post_g": 1.0 + nrm(ks[5], (DEPTH, 2, d), 0.05),
        "pool_w": nrm(ks[6], (N_POOL_LAYERS, N_POOL_GROUPS, POOL_GROUP_DIM, POOL_GROUP_DIM), POOL_GROUP_DIM ** -0.5),
        "pool_scale": 1.0 + nrm(ks[7], (N_POOL_LAYERS, d), 0.1),
        "cv_w_pw1": nrm(ks[8], (N_CONV_LAYERS, d, 2 * d), d ** -0.5),
        "cv_b_pw1": nrm(ks[9], (N_CONV_LAYERS, 2 * d), 0.02),
        "cv_w_dw": nrm(ks[10], (N_CONV_LAYERS, CONV_WIDTH, d), CONV_WIDTH ** -0.5),
        "cv_b_dw": nrm(ks[11], (N_CONV_LAYERS, d), 0.02),
        "cv_ln_g": 1.0 + nrm(ks[12], (N_CONV_LAYERS, d), 0.05),
        "cv_ln_b": nrm(ks[13], (N_CONV_LAYERS, d), 0.02),
        "cv_w_pw2": nrm(ks[14], (N_CONV_LAYERS, d, d), d ** -0.5),
        "cv_b_pw2": nrm(ks[15], (N_CONV_LAYERS, d), 0.02),
        "ffn_w_up": nrm(ks[16], (DEPTH, d, 2 * f), d ** -0.5),
        "ffn_w_dw": nrm(ks[17], (DEPTH, FFN_CONV_WIDTH, 2 * f), FFN_CONV_WIDTH ** -0.5),
        "ffn_w_down": nrm(ks[18], (DEPTH, f, d), f ** -0.5),
    }


def reference(x, c, ada_w, ada_b, pre_g, post_g, pool_w, pool_scale,
              cv_w_pw1, cv_b_pw1, cv_w_dw, cv_b_dw, cv_ln_g, cv_ln_b, cv_w_pw2, cv_b_pw2,
              ffn_w_up, ffn_w_dw, ffn_w_down):
    c_act = jax.nn.silu(c)
    for i in range(DEPTH):
        mod = c_act @ ada_w[i] + ada_b[i]
        sh_m, sc_m, gt_m, sh_f, sc_f, gt_f = [m[:, None, :] for m in jnp.split(mod, N_MOD, axis=-1)]

        h = rms_norm(x, pre_g[i, 0]) * (1.0 + sc_m) + sh_m
        j = i // N_MIXERS
        if i % N_MIXERS == 0:
            y = pool_mixer(h, pool_w[j], pool_scale[j])
        else:
            y = conformer_conv_module(h, cv_w_pw1[j], cv_b_pw1[j], cv_w_dw[j], cv_b_dw[j],
                                      cv_ln_g[j], cv_ln_b[j], cv_w_pw2[j], cv_b_pw2[j])
        x = x + gt_m * rms_norm(y, post_g[i, 0])

        h = rms_norm(x, pre_g[i, 1]) * (1.0 + sc_f) + sh_f
        y = conv_ffn(h, ffn_w_up[i], ffn_w_dw[i], ffn_w_down[i])
        x = x + gt_f * rms_norm(y, post_g[i, 1])
    return x
```

```python
import contextlib
import numpy as np
import concourse.bass as bass
import concourse.mybir as mybir
from concourse.bass_utils import run_bass_kernel_spmd

F32 = mybir.dt.float32
BF16 = mybir.dt.bfloat16
AF = mybir.ActivationFunctionType
ALU = mybir.AluOpType

D = 1024
SEQ = 8192
NB = 8
FF = 2816
NCH = 8
NA = 44
NQ = 22
EPS = 1e-6
T = 512
NT = SEQ // T
HM = 16
CM = 30

_VOFF = {}
_cur = 0
for _n, _w in [("c", 8), ("adab", 96), ("preg", 32), ("postg", 32), ("pscale", 8),
               ("bpw1", 16), ("w31", 248), ("bdw", 8), ("lng", 8), ("lnb", 8),
               ("bpw2", 8), ("wdw", 264), ("ident", 128), ("corr", 64)]:
    _VOFF[_n] = _cur
    _cur += _w
NV = _cur

N_W8 = 28
N_DG = 8
N_WD = 16


class Sched:
    def __init__(self, nc, es):
        self.nc = nc
        self.es = es
        self.eng = {"pe": nc.tensor, "act": nc.scalar, "dve": nc.vector,
                    "pool": nc.gpsimd, "sp": nc.sync}
        self.sem = {k: es.enter_context(nc.semaphore("sem_" + k)) for k in self.eng}
        self.cnt = {k: 0 for k in self.eng}
        self.waited = {k: {} for k in self.eng}
        self.last_w = {}
        self.readers = {}
        self.dma_sems = {}
        self.dma_cnt = {}
        self.pending = {k: ([], []) for k in self.eng}
        self.last_ins = {k: None for k in self.eng}
        self.n_inst = 0

    def flush(self, e):
        pr, pw = self.pending[e]
        if not pr and not pw:
            return
        self.cnt[e] += 1
        self.last_ins[e].then_inc(self.sem[e], 1)
        self._commit((e, self.sem[e], self.cnt[e]), pr, pw)
        self.pending[e] = ([], [])

    def flush_all(self):
        for e in self.eng:
            self.flush(e)

    def _flush_conflicts(self, e, reads, writes):
        for e2 in self.eng:
            if e2 == e:
                continue
            pr, pw = self.pending[e2]
            if not pr and not pw:
                continue
            hit = any(r in pw for r in reads) or any((w in pw) or (w in pr) for w in writes)
            if hit:
                self.flush(e2)

    def _wait(self, e, ev):
        if ev is None:
            return
        sname, sem, val = ev
        w = self.waited[e]
        if w.get(sname, 0) >= val:
            return
        w[sname] = val
        self.eng[e].wait_ge(sem, val)
        self.n_inst += 1

    def _deps(self, e, reads, writes):
        self._flush_conflicts(e, reads, writes)
        for r in reads:
            self._wait(e, self.last_w.get(r))
        for w in writes:
            self._wait(e, self.last_w.get(w))
            for ev in self.readers.get(w, ()):
                self._wait(e, ev)

    def _commit(self, ev, reads, writes):
        for r in reads:
            self.readers.setdefault(r, []).append(ev)
        for w in writes:
            self.last_w[w] = ev
            self.readers[w] = []

    def op(self, e, fn, reads=(), writes=(), signal=True):
        reads = list(reads)
        writes = list(writes)
        writes += [r for r in reads if isinstance(r, tuple) and r[0] == "ps" and r not in writes]
        self._deps(e, reads, writes)
        ins = fn()
        self.last_ins[e] = ins
        self.n_inst += 1
        pr, pw = self.pending[e]
        if signal:
            self.cnt[e] += 1
            ins.then_inc(self.sem[e], 1)
            ev = (e, self.sem[e], self.cnt[e])
            self._commit(ev, pr + reads, pw + writes)
            self.pending[e] = ([], [])
        else:
            pr.extend(reads)
            pw.extend(writes)
        return ins

    def dma(self, q, out, in_, reads=(), writes=(), sem_key=None, **kw):
        reads = list(reads)
        writes = list(writes)
        self._deps(q, reads, writes)
        if sem_key not in self.dma_sems:
            self.dma_sems[sem_key] = self.es.enter_context(
                self.nc.semaphore("dsem_%d" % len(self.dma_sems)))
            self.dma_cnt[sem_key] = 0
        sem = self.dma_sems[sem_key]
        name = "d_%s" % (sem_key,)
        if self.dma_cnt[sem_key] > 0:
            self._wait(q, (name, sem, self.dma_cnt[sem_key]))
        self.dma_cnt[sem_key] += 16
        ins = self.eng[q].dma_start(out=out, in_=in_, **kw)
        ins.then_inc(sem, 16)
        self.n_inst += 1
        ev = (name, sem, self.dma_cnt[sem_key])
        self._commit(ev, reads, writes)
        return ev


class Stream:
    def __init__(self, S, name, slots, srcs, nslots):
        self.S, self.name, self.slots, self.srcs = S, name, slots, srcs
        self.ns = nslots
        self.issued = 0

    def get(self, n, ahead=None):
        ahead = self.ns - 1 if ahead is None else ahead
        upto = min(len(self.srcs) - 1, n + ahead)
        while self.issued <= upto:
            k = self.issued
            s = k % self.ns
            src_ap, src_key = self.srcs[k]
            self.S.dma("sp", self.slots[s][:], src_ap, reads=[src_key],
                       writes=[(self.name, s)], sem_key=(self.name, s))
            self.issued += 1
        return n % self.ns


def build_program(n_tiles=NT):
    nc = bass.Bass("TRN2", target_bir_lowering=False)
    dram = lambda n, sh, dt, kind: nc.dram_tensor(n, sh, dt, kind=kind).ap()
    xT = dram("xT", [128, NCH, SEQ], F32, "ExternalInput")
    vecs_d = dram("vecs", [128, NV], F32, "ExternalInput")
    w8f = dram("w8f", [N_W8, 128, 4096], F32, "ExternalInput")
    wdf = dram("wdf", [N_WD, 128, NQ * 128], F32, "ExternalInput")
    adaf = dram("adaf", [24, 128, 4096], F32, "ExternalInput")
    poolwf = dram("poolwf", [128, 2048], F32, "ExternalInput")
    outT = dram("outT", [128, NCH, SEQ], F32, "ExternalOutput")
    w8b = dram("w8b", [N_W8, 128, 4096], BF16, "Internal")
    dgb = dram("dgb", [N_DG, 128, 4096], BF16, "Internal")
    wdb = dram("wdb", [N_WD, 128, NQ * 128], BF16, "Internal")

    es = contextlib.ExitStack()
    with es:
        S = Sched(nc, es)
        sb = lambda n, sh, dt: es.enter_context(nc.sbuf_tensor(n, sh, dt))
        NS8, NSD, NAB, NCG = 3, 3, 4, 2
        POOL_UPD = (6, 7)
        POOL_PRE = (6, 7)
        vecs = sb("vecs_sb", [128, NV], F32)
        der = sb("der", [128, 96], F32)
        modT = sb("modT", [128, 96], F32)
        cact = sb("cact", [128, 8], BF16)
        ones = sb("ones", [128, 128], BF16)
        mh = sb("mh", [128, T], F32)
        cbuf = sb("cbuf", [128, 4, 16], F32)
        zz = sb("zz", [128, T], F32)
        xt = [sb("xt%d" % i, [128, NCH, T], F32) for i in range(2)]
        t32 = sb("t32", [128, NCH, T], F32)
        sq = sb("sq", [128, NCH * T], BF16)
        hT = sb("hT", [128, NCH, HM + T], BF16)
        ms = sb("ms", [128, T], F32)
        rstdA = sb("rstdA", [128, T], F32)
        rstdB = sb("rstdB", [128, T], F32)
        mean = sb("mean", [128, T], F32)
        m2 = sb("m2", [128, T], F32)
        Sa = sb("Sa", [128, NCH, HM + T], BF16)
        Sb_ = sb("Sb", [128, NCH, HM + T], BF16)
        Dt = sb("Dt", [128, NCH, T], BF16)
        sT = sb("sT", [128, NCH, T], BF16)
        Ab = [sb("Ab%d" % i, [128, 2 + T], BF16) for i in range(NAB)]
        Bb = [sb("Bb%d" % i, [128, 2 + T], BF16) for i in range(NAB)]
        cg = [sb("cg%d" % i, [128, T], BF16) for i in range(NCG)]
        cv = [sb("cv%d" % i, [128, T], BF16) for i in range(NCG)]
        uT = sb("uT", [128, NQ, T], BF16)
        u31 = sb("u31", [128, NCH, CM + T], BF16)
        vh = [sb("vh%d" % i, [128, T], BF16) for i in range(2)]
        th = [sb("th%d" % i, [128, T], BF16) for i in range(2)]
        HA = sb("HA", [128, 2, NA, 2], BF16)
        HB = sb("HB", [128, 2, NA, 2], BF16)
        w8 = [sb("w8_%d" % i, [128, 4096], BF16) for i in range(NS8)]
        wd = [sb("wd_%d" % i, [128, NQ * 128], BF16) for i in range(NSD)]
        poolw = sb("poolw", [128, 2048], BF16)
        psb = [es.enter_context(nc.psum_tensor("ps%d" % i, [128, T], F32)) for i in range(8)]
        NMAIN = 6
        state = {"bank": 0}

        def next_bank():
            b = state["bank"]
            state["bank"] = (b + 1) % NMAIN
            return b

        V = lambda name, a, n=1: vecs[:, _VOFF[name] + a:_VOFF[name] + a + n]
        XK = lambda s: [("x", s, c) for c in range(NCH)]

        S.dma("sp", vecs[:], vecs_d, writes=["vecs"], sem_key="vecs")
        S.dma("sp", xt[0][:], xT[:, :, 0:T], writes=XK(0), sem_key=("x", 0))
        S.op("pool", lambda: nc.gpsimd.memset(ones[:], 1.0 / D), writes=["ones"])
        S.op("pool", lambda: nc.gpsimd.memset(hT[:, :, 0:HM], 0.0), writes=["hm"])
        S.op("pool", lambda: nc.gpsimd.memset(u31[:, :, 0:CM], 0.0), writes=["u31m"])
        S.op("pool", lambda: nc.gpsimd.memset(HA[:], 0.0), writes=["HA0", "HA1"])
        S.op("pool", lambda: nc.gpsimd.memset(HB[:], 0.0), writes=["HB0", "HB1"])
        S.op("act", lambda: nc.scalar.activation(out=cact[:], in_=V("c", 0, 8), func=AF.Silu),
             reads=["vecs"], writes=["cact"])

        SQK = [("sq", c) for c in range(NCH)]
        HK = [("h", c) for c in range(NCH)]
        GS = lambda l_, j, c: der[:, (l_ * 2 + j) * 8 + c:(l_ * 2 + j) * 8 + c + 1]
        GG = lambda l_, j, c: der[:, 32 + (l_ * 2 + j) * 8 + c:32 + (l_ * 2 + j) * 8 + c + 1]
        SH = lambda l_, j, c: modT[:, l_ * 48 + 3 * j * 8 + c:l_ * 48 + 3 * j * 8 + c + 1]

        def mm(out, lhsT, rhs, start, stop, reads, writes, signal=None):
            S.op("pe", lambda: nc.tensor.matmul(out, lhsT, rhs, start=start, stop=stop),
                 reads=reads, writes=writes, signal=(stop if signal is None else signal))

        modps = psb[7]
        for hb in range(24):
            blk, half = hb // 2, hb % 2
            l_, i_ = blk // 6, blk % 6
            s = hb % NS8
            S.dma("pool", w8[s][:, :].rearrange("p (a n) -> p a n", n=2048),
                  adaf[hb].rearrange("p (a n) -> p a n", n=2048),
                  writes=[("w8", s)], sem_key=("w8", s))
            for cc in range(4):
                col = l_ * 48 + i_ * 8 + half * 4 + cc
                for kc in range(8):
                    o = kc * 512 + cc * 128
                    mm(modps[:, col:col + 1], w8[s][:, o:o + 128], cact[:, kc:kc + 1],
                       kc == 0, kc == 7, [("w8", s), "cact"], [("ps", 7)])
        S.op("dve", lambda: nc.vector.tensor_tensor(modT[:], modps[:, 0:96], V("adab", 0, 96), ALU.add),
             reads=[("ps", 7), "vecs"], writes=["modT"])
        for l_ in range(2):
            for j in range(2):
                o = (l_ * 2 + j) * 8
                sc0 = l_ * 48 + (1 + 3 * j) * 8
                gt0 = l_ * 48 + (2 + 3 * j) * 8
                S.op("dve", lambda: nc.vector.scalar_tensor_tensor(
                    der[:, o:o + 8], modT[:, sc0:sc0 + 8], 1.0, V("preg", l_ * 16 + j * 8, 8),
                    ALU.add, ALU.mult), reads=["modT", "vecs"], writes=["der"])
                S.op("dve", lambda: nc.vector.tensor_tensor(
                    der[:, 32 + o:32 + o + 8], modT[:, gt0:gt0 + 8], V("postg", l_ * 16 + j * 8, 8),
                    ALU.mult), reads=["modT", "vecs"], writes=["der"])
        S.op("dve", lambda: nc.vector.tensor_scalar(der[:, 64:80], V("bpw1", 0, 16), 0.5, None, ALU.mult),
             reads=["vecs"], writes=["der"])
        S.op("dve", lambda: nc.vector.tensor_tensor(der[:, 80:88], der[:, 32:40], V("pscale", 0, 8), ALU.mult),
             reads=["der", "vecs"], writes=["der"])
        S.op("dve", lambda: nc.vector.tensor_tensor(der[:, 88:96], der[:, 48:56], V("bpw2", 0, 8), ALU.mult),
             reads=["der", "vecs"], writes=["der"])

        S.dma("pool", poolw[:], poolwf, writes=["poolw"], sem_key="poolw")

        def cast_w8(g):
            S.dma("pool", w8b[g].rearrange("p (a n) -> (p a) n", n=2048),
                  w8f[g].rearrange("p (a n) -> (p a) n", n=2048),
                  writes=[("w8b", g)], sem_key=("cv", g))

        def cast_wd(g):
            S.dma("pool", wdb[g].rearrange("p (a n) -> (p a) n", n=1408),
                  wdf[g].rearrange("p (a n) -> (p a) n", n=1408),
                  writes=[("wdb", g)], sem_key=("cw", g))

        for g in range(11):
            cast_w8(g)
        for g in range(8):
            cast_wd(g)
        deferred_casts = [(cast_w8, g) for g in range(11, 17)] + [(cast_w8, g) for g in range(17, 28)] \
            + [(cast_wd, g) for g in range(8, 16)]

        def diag_builder():
            for idx in range(NCH * 31):
                g = idx // 32
                stg, sk = (Sa, "Sa") if g % 2 == 0 else (Sb_, "Sb")
                keys = [(sk, c) for c in range(NCH)]
                st_c, st_o = (idx % 32) // 4, ((idx % 32) % 4) * 128
                S.op("dve", lambda: nc.vector.tensor_scalar(stg[:, st_c, st_o:st_o + 128], V("ident", 0, 128),
                                                            V("w31", idx), None, ALU.mult),
                     reads=["vecs"], writes=keys)
                if idx % 32 == 31 or idx == NCH * 31 - 1:
                    S.dma("sp", dgb[g].rearrange("p (q n) -> p q n", n=T), stg[:, :, 0:T], reads=keys,
                          writes=[("dgb", g)], sem_key=("dg", g % 2))
                yield

        diag_gen = diag_builder()

        w8_srcs, wd_srcs = [], []
        for i in range(n_tiles):
            for g in range(11):
                w8_srcs.append((w8b[g], ("w8b", g)))
            for g in range(11, 15):
                w8_srcs.append((w8b[g], ("w8b", g)))
            for g in range(N_DG):
                w8_srcs.append((dgb[g], ("dgb", g)))
            for g in range(15, 28):
                w8_srcs.append((w8b[g], ("w8b", g)))
            for g in range(N_WD):
                wd_srcs.append((wdb[g], ("wdb", g)))
        W8 = Stream(S, "w8", w8, w8_srcs, NS8)
        WD = Stream(S, "wd", wd, wd_srcs, NSD)

        def load_x(i):
            s = i % 2
            S.dma("sp", xt[s][:], xT[:, :, i * T:(i + 1) * T], writes=XK(s), sem_key=("x", s))

        def store_x(i):
            s = i % 2
            S.dma("sp", outT[:, :, i * T:(i + 1) * T], xt[s][:], reads=XK(s), sem_key=("o", s))

        def stats_mm(bank, src_ap, c, reads):
            mm(psb[bank][:], ones[:], src_ap, c == 0, c == NCH - 1, ["ones"] + reads, [("ps", bank)])

        I32 = mybir.dt.int32

        NBLK = T // 32

        def newton_rsqrt(src_ap, src_keys, add_eps, dst, dkey):
            S.op("dve", lambda: nc.vector.transpose(mh[:], src_ap), reads=src_keys, writes=["mh"])
            cview = mh[:].rearrange("p (b i) -> p b i", i=32)[:, :, 0:1]
            ca, cy, ch, cz = (cbuf[:, k, 0:NBLK] for k in range(4))
            S.op("dve", lambda: nc.vector.tensor_scalar(ca.unsqueeze(2), cview, EPS if add_eps else 0.0, None, ALU.add),
                 reads=["mh"], writes=["cbuf"])
            S.op("dve", lambda: nc.vector.tensor_scalar(cy.bitcast(I32), ca.bitcast(I32), 1, None, ALU.arith_shift_right),
                 reads=["cbuf"], writes=["cbuf"])
            S.op("dve", lambda: nc.vector.tensor_scalar(cy.bitcast(I32), cy.bitcast(I32), -1.0, 1597463007.0,
                                                        ALU.mult, ALU.add), reads=["cbuf"], writes=["cbuf"])
            S.op("dve", lambda: nc.vector.tensor_scalar(ch, ca, -0.5, None, ALU.mult), reads=["cbuf"], writes=["cbuf"])
            for _ in range(2):
                S.op("dve", lambda: nc.vector.tensor_tensor(cz, cy, cy, ALU.mult), reads=["cbuf"], writes=["cbuf"])
                S.op("dve", lambda: nc.vector.tensor_tensor(cz, cz, ch, ALU.mult), reads=["cbuf"], writes=["cbuf"])
                S.op("dve", lambda: nc.vector.scalar_tensor_tensor(cy, cz, 1.5, cy, ALU.add, ALU.mult),
                     reads=["cbuf"], writes=["cbuf"])
            S.op("dve", lambda: nc.vector.tensor_copy(zz[:].rearrange("p (b i) -> p b i", i=32),
                                                      cy.unsqueeze(2).broadcast_to([128, NBLK, 32])),
                 reads=["cbuf"], writes=["zz"])
            S.op("dve", lambda: nc.vector.transpose(dst[:], zz[:]), reads=["zz"], writes=[dkey])

        def rstd_from(bank, dst, dkey):
            newton_rsqrt(psb[bank][:], [("ps", bank)], True, dst, dkey)

        def pre_norm(s, l_, j):
            for c in range(NCH):
                S.op("act", lambda: nc.scalar.activation(out=sq[:, c * T:(c + 1) * T], in_=xt[s][:, c, :],
                                                         func=AF.Square),
                     reads=[("x", s, c)], writes=[("sq", c)])
                stats_mm(6, sq[:, c * T:(c + 1) * T], c, [("sq", c)])
            rstd_from(6, rstdA, "rstdA")
            for c in range(NCH):
                e, eng = ("pool", nc.gpsimd) if c in POOL_PRE else ("dve", nc.vector)
                S.op(e, lambda: eng.tensor_tensor(t32[:, c, :], xt[s][:, c, :], rstdA[:], ALU.mult),
                     reads=[("x", s, c), "rstdA"], writes=[("t32", c)])
                S.op("act", lambda: nc.scalar.activation(out=hT[:, c, HM:HM + T], in_=t32[:, c, :],
                                                         func=AF.Identity, scale=GS(l_, j, c), bias=SH(l_, j, c)),
                     reads=[("t32", c), "der", "modT"], writes=[("h", c)])

        def post_chunk(s, m, bank, scale, bias, gscale, gbias):
            S.op("act", lambda: nc.scalar.activation(out=t32[:, m, :], in_=psb[bank][:], func=AF.Identity,
                                                     scale=gscale, bias=gbias),
                 reads=[("ps", bank), "vecs", "der"], writes=[("t32", m)])
            S.op("act", lambda: nc.scalar.activation(out=sq[:, m * T:(m + 1) * T], in_=psb[bank][:],
                                                     func=AF.Square, scale=scale, bias=bias),
                 reads=[("ps", bank), "vecs"], writes=[("sq", m)])

        def post_stats(m):
            stats_mm(7, sq[:, m * T:(m + 1) * T], m, [("sq", m)])

        def post_finish(s, l_, j):
            rstd_from(7, rstdB, "rstdB")
            order = [m for m in range(NCH) if m in POOL_UPD] + [m for m in range(NCH) if m not in POOL_UPD]
            for m in order:
                e, eng = ("pool", nc.gpsimd) if m in POOL_UPD else ("dve", nc.vector)
                S.op(e, lambda: eng.tensor_tensor(t32[:, m, :], t32[:, m, :], rstdB[:], ALU.mult),
                     reads=[("t32", m), "rstdB"], writes=[("t32", m)])
                S.op(e, lambda: eng.tensor_tensor(xt[s][:, m, :], xt[s][:, m, :], t32[:, m, :], ALU.add),
                     reads=[("t32", m), ("x", s, m)], writes=[("x", s, m)])

        def pool_mixer(i, s):
            W = HM + T
            SaK = lambda a, b: [("Sa", c) for c in range(a, b)]
            SbK = lambda a, b: [("Sb", c) for c in range(a, b)]
            S.op("dve", lambda: nc.vector.tensor_tensor(Sa[:, :, 1:W], hT[:, :, 1:W], hT[:, :, 0:W - 1], ALU.add),
                 reads=HK + ["hm"], writes=SaK(0, 8))
            S.op("dve", lambda: nc.vector.tensor_tensor(Sb_[:, 2:8, 3:W], Sa[:, 2:8, 3:W], Sa[:, 2:8, 1:W - 2], ALU.add),
                 reads=SaK(2, 8), writes=SbK(2, 8))
            S.op("dve", lambda: nc.vector.tensor_tensor(Sa[:, 4:8, 7:W], Sb_[:, 4:8, 7:W], Sb_[:, 4:8, 3:W - 4], ALU.add),
                 reads=SbK(4, 8), writes=SaK(4, 8))
            S.op("dve", lambda: nc.vector.tensor_tensor(Sb_[:, 6:8, 15:W], Sa[:, 6:8, 15:W], Sa[:, 6:8, 7:W - 8], ALU.add),
                 reads=SaK(6, 8), writes=SbK(6, 8))
            for g in range(4):
                src, sk = (Sa, "Sa") if g in (0, 2) else (Sb_, "Sb")
                c0 = 2 * g
                keys = [(sk, c0), (sk, c0 + 1)]
                if i == 0:
                    S.op("dve", lambda: nc.vector.tensor_tensor(
                        src[:, c0:c0 + 2, HM:HM + 16], src[:, c0:c0 + 2, HM:HM + 16],
                        V("corr", g * 16, 16).unsqueeze(1).broadcast_to([128, 2, 16]), ALU.mult),
                        reads=keys + ["vecs"], writes=keys)
                S.op("dve", lambda: nc.vector.scalar_tensor_tensor(
                    Dt[:, c0:c0 + 2, :], src[:, c0:c0 + 2, HM:HM + T], 1.0 / (2 << g),
                    hT[:, c0:c0 + 2, HM:HM + T], ALU.mult, ALU.subtract),
                    reads=keys + [("h", c0), ("h", c0 + 1)], writes=[("D", c0), ("D", c0 + 1)])
            S.op("pool", lambda: nc.gpsimd.tensor_copy(hT[:, :, 0:HM], hT[:, :, T:T + HM]),
                 reads=HK, writes=["hm"])
            for m in range(NCH):
                g = m // 2
                bank = next_bank()
                for kc in range(2):
                    o = (g * 2 + kc) * 256 + (m % 2) * 128
                    mm(psb[bank][:], poolw[:, o:o + 128], Dt[:, g * 2 + kc, :], kc == 0, kc == 1,
                       ["poolw", ("D", g * 2 + kc)], [("ps", bank)])
                post_chunk(s, m, bank, V("pscale", m), 0.0, der[:, 80 + m:81 + m], 0.0)
                if m > 0:
                    post_stats(m - 1)
            post_stats(NCH - 1)
            post_finish(s, 0, 0)

        def ffn(i, s, l_):
            pre_norm(s, l_, 1)
            base = i * 36 + (0 if l_ == 0 else 25)
            based = i * 16 + l_ * 8
            WD.get(based)
            info = {}

            def st_pe(n):
                q, isv = n // 2, n % 2
                ach = q + (NQ if isv else 0)
                g, pos = n // 4, n % 4
                r = n % NAB
                if n < 4:
                    if n == 0:
                        slot = W8.get(base)
                        banks = [next_bank() for _ in range(4)]
                        for kc in range(8):
                            for p4 in range(4):
                                o = kc * 512 + p4 * 128
                                mm(psb[banks[p4]][:], w8[slot][:, o:o + 128], hT[:, kc, HM:HM + T], kc == 0, kc == 7,
                                   [("w8", slot), ("h", kc)], [("ps", banks[p4])])
                        info["banks"] = banks
                    bank = info["banks"][n]
                else:
                    slot = W8.get(base + g)
                    bank = next_bank()
                    for kc in range(8):
                        o = kc * 512 + pos * 128
                        mm(psb[bank][:], w8[slot][:, o:o + 128], hT[:, kc, HM:HM + T], kc == 0, kc == 7,
                           [("w8", slot), ("h", kc)], [("ps", bank)])
                info[n] = (q, isv, ach, bank, r)
                S.op("pool", lambda: nc.gpsimd.tensor_copy(Ab[r][:, 0:2], HA[:, l_, ach, :]),
                     reads=[("HA", l_)], writes=[("Ah", r)])
                S.op("pool", lambda: nc.gpsimd.tensor_copy(Bb[r][:, 0:1], HB[:, l_, ach, 0:1]),
                     reads=[("HB", l_)], writes=[("Bh", r)])

            def st_evac(n):
                q, isv, ach, bank, r = info[n]
                w1 = V("wdw", l_ * 132 + ach * 3 + 1)
                S.op("act", lambda: nc.scalar.activation(out=Ab[r][:, 2:2 + T], in_=psb[bank][:], func=AF.Copy),
                     reads=[("ps", bank)], writes=[("A", r), ("At", r)])
                S.op("act", lambda: nc.scalar.activation(out=Bb[r][:, 1:1 + T], in_=psb[bank][:], func=AF.Identity,
                                                         scale=w1),
                     reads=[("ps", bank), "vecs"], writes=[("B", r), ("Bt", r)])
                S.op("pool", lambda: nc.gpsimd.tensor_copy(HA[:, l_, ach, :], Ab[r][:, T:T + 2]),
                     reads=[("At", r)], writes=[("HA", l_)])
                S.op("pool", lambda: nc.gpsimd.tensor_copy(HB[:, l_, ach, 0:1], Bb[r][:, T:T + 1]),
                     reads=[("Bt", r)], writes=[("HB", l_)])

            def st_conv(n):
                q, isv, ach, bank, r = info[n]
                w0 = V("wdw", l_ * 132 + ach * 3 + 0)
                w2 = V("wdw", l_ * 132 + ach * 3 + 2)
                S.op("dve", lambda: nc.vector.scalar_tensor_tensor(Bb[r][:, 0:T], Ab[r][:, 0:T], w0,
                                                                  Bb[r][:, 0:T], ALU.mult, ALU.add),
                     reads=[("Ah", r), ("A", r), ("Bh", r), ("B", r), "vecs"], writes=[("Bh", r), ("B", r)])
                dst, dk = (cv, "cv") if isv else (cg, "cg")
                S.op("dve", lambda: nc.vector.scalar_tensor_tensor(dst[q % NCG][:], Ab[r][:, 2:2 + T], w2,
                                                                  Bb[r][:, 0:T], ALU.mult, ALU.add),
                     reads=[("A", r), ("At", r), ("Bh", r), ("B", r), "vecs"], writes=[(dk, q % NCG)])

            def st_silu(n):
                q, isv, ach, bank, r = info[n]
                if not isv:
                    S.op("act", lambda: nc.scalar.activation(out=cg[q % NCG][:], in_=cg[q % NCG][:], func=AF.Silu),
                         reads=[("cg", q % NCG)], writes=[("cg", q % NCG)])

            def st_mul(n):
                q, isv, ach, bank, r = info[n]
                if isv:
                    S.op("dve", lambda: nc.vector.tensor_tensor(uT[:, q, :], cg[q % NCG][:], cv[q % NCG][:], ALU.mult),
                         reads=[("cg", q % NCG), ("cv", q % NCG)], writes=[("u", q)])

            stages = [st_pe, st_evac, st_conv, st_silu, st_mul]
            for it in range(NA + len(stages) - 1):
                for k, fn in enumerate(stages):
                    n = it - k
                    if 0 <= n < NA:
                        fn(n)
                if deferred_casts:
                    fcast, gcast = deferred_casts.pop(0)
                    fcast(gcast)
                for _ in range(6):
                    next(diag_gen, None)
            while deferred_casts:
                fcast, gcast = deferred_casts.pop(0)
                fcast(gcast)
            for _ in diag_gen:
                pass
            for m in range(NCH):
                slot = WD.get(based + m)
                bank = next_bank()
                for fc in range(NQ):
                    mm(psb[bank][:], wd[slot][:, fc * 128:(fc + 1) * 128], uT[:, fc, :], fc == 0, fc == NQ - 1,
                       [("wd", slot), ("u", fc)], [("ps", bank)])
                post_chunk(s, m, bank, 1.0, 0.0, GG(l_, 1, m), 0.0)
                if m > 0:
                    post_stats(m - 1)
            post_stats(NCH - 1)
            post_finish(s, l_, 1)

        def conformer(i, s):
            pre_norm(s, 1, 0)
            base = i * 36 + 11
            cinfo = {}

            def c_pe(n):
                g, pos = n // 4, n % 4
                if n < 4:
                    if n == 0:
                        slot = W8.get(base)
                        banks = [next_bank() for _ in range(4)]
                        for kc in range(8):
                            for p4 in range(4):
                                o = kc * 512 + p4 * 128
                                mm(psb[banks[p4]][:], w8[slot][:, o:o + 128], hT[:, kc, HM:HM + T], kc == 0, kc == 7,
                                   [("w8", slot), ("h", kc)], [("ps", banks[p4])])
                        cinfo["banks"] = banks
                    cinfo[n] = cinfo["banks"][n]
                    return
                slot = W8.get(base + g)
                bank = next_bank()
                for kc in range(8):
                    o = kc * 512 + pos * 128
                    mm(psb[bank][:], w8[slot][:, o:o + 128], hT[:, kc, HM:HM + T], kc == 0, kc == 7,
                       [("w8", slot), ("h", kc)], [("ps", bank)])
                cinfo[n] = bank

            def c_evac(n):
                j, isg = n // 2, n % 2
                bank = cinfo[n]
                r = j % 2
                if not isg:
                    S.op("act", lambda: nc.scalar.activation(out=vh[r][:], in_=psb[bank][:], func=AF.Identity,
                                                             scale=0.5, bias=der[:, 64 + j:65 + j]),
                         reads=[("ps", bank), "der"], writes=[("vh", r)])
                else:
                    S.op("act", lambda: nc.scalar.activation(out=th[r][:], in_=psb[bank][:], func=AF.Tanh,
                                                             scale=0.5, bias=der[:, 72 + j:73 + j]),
                         reads=[("ps", bank), "der"], writes=[("th", r)])

            def c_glu(n):
                j, isg = n // 2, n % 2
                r = j % 2
                if isg:
                    S.op("dve", lambda: nc.vector.scalar_tensor_tensor(u31[:, j, CM:CM + T], th[r][:], 1.0, vh[r][:],
                                                                      ALU.add, ALU.mult),
                         reads=[("th", r), ("vh", r)], writes=[("u31", j)])

            cstages = [c_pe, c_evac, c_glu]
            for it in range(16 + len(cstages) - 1):
                for k, fn in enumerate(cstages):
                    n = it - k
                    if 0 <= n < 16:
                        fn(n)
            based = i * 36 + 15

            def ln_stats(j):
                stats_mm(6, Dt[:, j, :], j, [("D", j)])
                stats_mm(7, sq[:, j * T:(j + 1) * T], j, [("sq", j)])

            for j in range(NCH):
                bank = next_bank()
                for k in range(31):
                    idx = j * 31 + k
                    slot = W8.get(based + idx // 32)
                    p0 = (idx % 32) * 128
                    mm(psb[bank][:], w8[slot][:, p0:p0 + 128], u31[:, j, k:k + T], k == 0, k == 30,
                       [("w8", slot), ("u31", j), "u31m"], [("ps", bank)])
                S.op("act", lambda: nc.scalar.activation(out=Dt[:, j, :], in_=psb[bank][:], func=AF.Identity,
                                                         bias=V("bdw", j)),
                     reads=[("ps", bank), "vecs"], writes=[("D", j)])
                S.op("act", lambda: nc.scalar.activation(out=sq[:, j * T:(j + 1) * T], in_=Dt[:, j, :], func=AF.Square),
                     reads=[("D", j)], writes=[("sq", j)])
                if j > 0:
                    ln_stats(j - 1)
            ln_stats(NCH - 1)
            S.op("pool", lambda: nc.gpsimd.tensor_copy(u31[:, :, 0:CM], u31[:, :, T:T + CM]),
                 reads=[("u31", j) for j in range(NCH)], writes=["u31m"])
            S.op("dve", lambda: nc.vector.tensor_copy(mean[:], psb[6][:]), reads=[("ps", 6)], writes=["mean"])
            S.op("act", lambda: nc.scalar.activation(out=m2[:], in_=psb[6][:], func=AF.Square),
                 reads=[("ps", 6)], writes=["m2"])
            S.op("dve", lambda: nc.vector.tensor_tensor(ms[:], psb[7][:], m2[:], ALU.subtract),
                 reads=[("ps", 7), "m2"], writes=["ms"])
            S.op("dve", lambda: nc.vector.tensor_scalar(ms[:], ms[:], 0.0, EPS, ALU.max, ALU.add),
                 reads=["ms"], writes=["ms"])
            newton_rsqrt(ms[:], ["ms"], False, rstdA, "rstdA")
            for j in range(NCH):
                S.op("dve", lambda: nc.vector.tensor_tensor(t32[:, j, :], Dt[:, j, :], mean[:], ALU.subtract),
                     reads=[("D", j), "mean"], writes=[("t32", j)])
                S.op("dve", lambda: nc.vector.tensor_tensor(t32[:, j, :], t32[:, j, :], rstdA[:], ALU.mult),
                     reads=[("t32", j), "rstdA"], writes=[("t32", j)])
                S.op("act", lambda: nc.scalar.activation(out=sT[:, j, :], in_=t32[:, j, :], func=AF.Silu,
                                                         scale=V("lng", j), bias=V("lnb", j)),
                     reads=[("t32", j), "vecs"], writes=[("sT", j)])
            for m in range(NCH):
                g, pos = m // 4, m % 4
                slot = W8.get(i * 36 + 23 + g)
                bank = next_bank()
                for kc in range(8):
                    o = kc * 512 + pos * 128
                    mm(psb[bank][:], w8[slot][:, o:o + 128], sT[:, kc, :], kc == 0, kc == 7,
                       [("w8", slot), ("sT", kc)], [("ps", bank)])
                post_chunk(s, m, bank, 1.0, V("bpw2", m), GG(1, 0, m), der[:, 88 + m:89 + m])
                if m > 0:
                    post_stats(m - 1)
            post_stats(NCH - 1)
            post_finish(s, 1, 0)

        for i in range(n_tiles):
            s = i % 2
            pre_norm(s, 0, 0)
            pool_mixer(i, s)
            ffn(i, s, 0)
            conformer(i, s)
            if i + 1 < n_tiles:
                load_x(i + 1)
            ffn(i, s, 1)
            store_x(i)
        for k in [("o", 0), ("o", 1)]:
            if k in S.dma_sems:
                S._wait("sp", ("d_%s" % (k,), S.dma_sems[k], S.dma_cnt[k]))
        S.flush_all()
    return nc, S


def _g8(Wc):
    return np.ascontiguousarray(Wc.reshape(8, 128, 512).transpose(1, 0, 2)).reshape(128, 4096)


def _cols(order):
    return np.concatenate([np.arange(ch * 128, (ch + 1) * 128) for ch in order])


def _fm(v):
    v = np.asarray(v, np.float32)
    return v.reshape(-1, 128).T


def prepare_inputs(inputs):
    f = lambda k: np.asarray(inputs[k], dtype=np.float32)
    x, c = f("x"), f("c")
    ada_w, ada_b = f("ada_w"), f("ada_b")
    ffn_w_up, ffn_w_dw, ffn_w_down = f("ffn_w_up"), f("ffn_w_dw"), f("ffn_w_down")
    w8f = np.empty((N_W8, 128, 4096), np.float32)
    up_order = []
    for q in range(NQ):
        up_order += [q, NQ + q]
    ucols = _cols(up_order)
    for l in range(2):
        Wp = ffn_w_up[l][:, ucols]
        for g in range(11):
            w8f[(0 if l == 0 else 17) + g] = _g8(Wp[:, g * 512:(g + 1) * 512])
    p1_order = []
    for j in range(8):
        p1_order += [j, 8 + j]
    Wp = f("cv_w_pw1")[0][:, _cols(p1_order)]
    for g in range(4):
        w8f[11 + g] = _g8(Wp[:, g * 512:(g + 1) * 512])
    Wp = f("cv_w_pw2")[0]
    for g in range(2):
        w8f[15 + g] = _g8(Wp[:, g * 512:(g + 1) * 512])
    wdf = np.empty((N_WD, 128, NQ * 128), np.float32)
    for l in range(2):
        for m in range(8):
            blk = ffn_w_down[l][:, m * 128:(m + 1) * 128]
            wdf[l * 8 + m] = np.ascontiguousarray(blk.reshape(NQ, 128, 128).transpose(1, 0, 2)).reshape(128, NQ * 128)
    adaf = np.empty((24, 128, 4096), np.float32)
    for hb in range(24):
        blk, half = hb // 2, hb % 2
        l, i = blk // 6, blk % 6
        adaf[hb] = _g8(ada_w[l][:, i * 1024 + half * 512:i * 1024 + half * 512 + 512])
    pw = f("pool_w")[0]
    poolwf = np.ascontiguousarray(pw.reshape(4, 2, 128, 256).transpose(2, 0, 1, 3)).reshape(128, 2048)

    vec_common = np.zeros((128, NV), np.float32)

    def put(name, arr):
        arr = np.asarray(arr, np.float32)
        vec_common[:, _VOFF[name]:_VOFF[name] + arr.shape[1]] = arr

    put("adab", np.concatenate([_fm(ada_b[l]) for l in range(2)], axis=1))
    put("preg", np.concatenate([_fm(f("pre_g")[l, j]) for l in range(2) for j in range(2)], axis=1))
    put("postg", np.concatenate([_fm(f("post_g")[l, j]) for l in range(2) for j in range(2)], axis=1))
    put("pscale", _fm(f("pool_scale")[0]))
    put("bpw1", _fm(f("cv_b_pw1")[0]))
    w31 = f("cv_w_dw")[0]
    put("w31", np.ascontiguousarray(w31.reshape(31, 8, 128).transpose(2, 1, 0)).reshape(128, 248))
    put("bdw", _fm(f("cv_b_dw")[0]))
    put("lng", _fm(f("cv_ln_g")[0]))
    put("lnb", _fm(f("cv_ln_b")[0]))
    put("bpw2", _fm(f("cv_b_pw2")[0]))
    put("wdw", np.concatenate(
        [np.ascontiguousarray(ffn_w_dw[l].reshape(3, NA, 128).transpose(2, 1, 0)).reshape(128, NA * 3)
         for l in range(2)], axis=1))
    put("ident", np.eye(128, dtype=np.float32))
    corr = np.zeros((4, 16), np.float32)
    for g in range(4):
        w = 2 << g
        for t in range(16):
            corr[g, t] = w / min(t + 1, w)
    put("corr", np.broadcast_to(corr.reshape(1, 64), (128, 64)))

    in_maps = []
    for b in range(NB):
        vb = vec_common.copy()
        vb[:, _VOFF["c"]:_VOFF["c"] + 8] = _fm(c[b])
        xTb = np.ascontiguousarray(x[b].T.reshape(NCH, 128, SEQ).transpose(1, 0, 2))
        in_maps.append({"xT": xTb, "vecs": vb, "w8f": w8f, "wdf": wdf, "adaf": adaf, "poolwf": poolwf})
    return in_maps


def kernel(**inputs):
    in_maps = prepare_inputs(inputs)
    nc, _ = build_program()
    res = run_bass_kernel_spmd(nc, in_maps, core_ids=list(range(NB)))
    out = np.empty((NB, SEQ, D), np.float32)
    for b in range(NB):
        o = np.asarray(res.results[b]["outT"], dtype=np.float32)
        out[b] = o.transpose(1, 0, 2).reshape(D, SEQ).T
    return out
```

```python
import contextlib
import numpy as np
import concourse.bass as bass
import concourse.mybir as mybir
from concourse.bass_utils import run_bass_kernel_spmd

F32 = mybir.dt.float32
BF16 = mybir.dt.bfloat16
AF = mybir.ActivationFunctionType
ALU = mybir.AluOpType

D = 1024
SEQ = 8192
NB = 8
FF = 2816
NCH = 8
NA = 44
NQ = 22
EPS = 1e-6
T = 512
NT = SEQ // T
HM = 16
CM = 30

_VOFF = {}
_cur = 0
for _n, _w in [("c", 8), ("adab", 96), ("preg", 32), ("postg", 32), ("pscale", 8),
               ("bpw1", 16), ("w31", 248), ("bdw", 8), ("lng", 8), ("lnb", 8),
               ("bpw2", 8), ("wdw", 264), ("ident", 128), ("corr", 64)]:
    _VOFF[_n] = _cur
    _cur += _w
NV = _cur

N_W8 = 28
N_DG = 8
N_WD = 16


class Sched:
    def __init__(self, nc, es):
        self.nc = nc
        self.es = es
        self.eng = {"pe": nc.tensor, "act": nc.scalar, "dve": nc.vector,
                    "pool": nc.gpsimd, "sp": nc.sync}
        self.sem = {k: es.enter_context(nc.semaphore("sem_" + k)) for k in self.eng}
        self.cnt = {k: 0 for k in self.eng}
        self.waited = {k: {} for k in self.eng}
        self.last_w = {}
        self.readers = {}
        self.dma_sems = {}
        self.dma_cnt = {}
        self.pending = {k: ([], []) for k in self.eng}
        self.last_ins = {k: None for k in self.eng}
        self.n_inst = 0

    def flush(self, e):
        pr, pw = self.pending[e]
        if not pr and not pw:
            return
        self.cnt[e] += 1
        self.last_ins[e].then_inc(self.sem[e], 1)
        self._commit((e, self.sem[e], self.cnt[e]), pr, pw)
        self.pending[e] = ([], [])

    def flush_all(self):
        for e in self.eng:
            self.flush(e)

    def _flush_conflicts(self, e, reads, writes):
        for e2 in self.eng:
            if e2 == e:
                continue
            pr, pw = self.pending[e2]
            if not pr and not pw:
                continue
            hit = any(r in pw for r in reads) or any((w in pw) or (w in pr) for w in writes)
            if hit:
                self.flush(e2)

    def _wait(self, e, ev):
        if ev is None:
            return
        sname, sem, val = ev
        w = self.waited[e]
        if w.get(sname, 0) >= val:
            return
        w[sname] = val
        self.eng[e].wait_ge(sem, val)
        self.n_inst += 1

    def _deps(self, e, reads, writes):
        self._flush_conflicts(e, reads, writes)
        for r in reads:
            self._wait(e, self.last_w.get(r))
        for w in writes:
            self._wait(e, self.last_w.get(w))
            for ev in self.readers.get(w, ()):
                self._wait(e, ev)

    def _commit(self, ev, reads, writes):
        for r in reads:
            self.readers.setdefault(r, []).append(ev)
        for w in writes:
            self.last_w[w] = ev
            self.readers[w] = []

    def op(self, e, fn, reads=(), writes=(), signal=True):
        reads = list(reads)
        writes = list(writes)
        writes += [r for r in reads if isinstance(r, tuple) and r[0] == "ps" and r not in writes]
        self._deps(e, reads, writes)
        ins = fn()
        self.last_ins[e] = ins
        self.n_inst += 1
        pr, pw = self.pending[e]
        if signal:
            self.cnt[e] += 1
            ins.then_inc(self.sem[e], 1)
            ev = (e, self.sem[e], self.cnt[e])
            self._commit(ev, pr + reads, pw + writes)
            self.pending[e] = ([], [])
        else:
            pr.extend(reads)
            pw.extend(writes)
        return ins

    def dma(self, q, out, in_, reads=(), writes=(), sem_key=None, **kw):
        reads = list(reads)
        writes = list(writes)
        self._deps(q, reads, writes)
        if sem_key not in self.dma_sems:
            self.dma_sems[sem_key] = self.es.enter_context(
                self.nc.semaphore("dsem_%d" % len(self.dma_sems)))
            self.dma_cnt[sem_key] = 0
        sem = self.dma_sems[sem_key]
        name = "d_%s" % (sem_key,)
        if self.dma_cnt[sem_key] > 0:
            self._wait(q, (name, sem, self.dma_cnt[sem_key]))
        self.dma_cnt[sem_key] += 16
        ins = self.eng[q].dma_start(out=out, in_=in_, **kw)
        ins.then_inc(sem, 16)
        self.n_inst += 1
        ev = (name, sem, self.dma_cnt[sem_key])
        self._commit(ev, reads, writes)
        return ev


class Stream:
    def __init__(self, S, name, slots, srcs, nslots):
        self.S, self.name, self.slots, self.srcs = S, name, slots, srcs
        self.ns = nslots
        self.issued = 0

    def get(self, n, ahead=None):
        ahead = self.ns - 1 if ahead is None else ahead
        upto = min(len(self.srcs) - 1, n + ahead)
        while self.issued <= upto:
            k = self.issued
            s = k % self.ns
            src_ap, src_key = self.srcs[k]
            self.S.dma("sp", self.slots[s][:], src_ap, reads=[src_key],
                       writes=[(self.name, s)], sem_key=(self.name, s))
            self.issued += 1
        return n % self.ns


def build_program(n_tiles=NT):
    nc = bass.Bass("TRN2", target_bir_lowering=False)
    dram = lambda n, sh, dt, kind: nc.dram_tensor(n, sh, dt, kind=kind).ap()
    xT = dram("xT", [128, NCH, SEQ], F32, "ExternalInput")
    vecs_d = dram("vecs", [128, NV], F32, "ExternalInput")
    w8f = dram("w8f", [N_W8, 128, 4096], F32, "ExternalInput")
    wdf = dram("wdf", [N_WD, 128, NQ * 128], F32, "ExternalInput")
    adaf = dram("adaf", [24, 128, 4096], F32, "ExternalInput")
    poolwf = dram("poolwf", [128, 2048], F32, "ExternalInput")
    outT = dram("outT", [128, NCH, SEQ], F32, "ExternalOutput")
    w8b = dram("w8b", [N_W8, 128, 4096], BF16, "Internal")
    dgb = dram("dgb", [N_DG, 128, 4096], BF16, "Internal")
    wdb = dram("wdb", [N_WD, 128, NQ * 128], BF16, "Internal")

    es = contextlib.ExitStack()
    with es:
        S = Sched(nc, es)
        sb = lambda n, sh, dt: es.enter_context(nc.sbuf_tensor(n, sh, dt))
        NS8, NSD, NAB, NCG = 3, 3, 4, 2
        POOL_UPD = ()
        POOL_PRE = ()
        vecs = sb("vecs_sb", [128, NV], F32)
        der = sb("der", [128, 96], F32)
        modT = sb("modT", [128, 96], F32)
        cact = sb("cact", [128, 8], BF16)
        ones = sb("ones", [128, 128], BF16)
        mh = sb("mh", [128, T], F32)
        cbuf = sb("cbuf", [128, 4, 16], F32)
        zz = sb("zz", [128, T], F32)
        xt = [sb("xt%d" % i, [128, NCH, T], F32) for i in range(2)]
        t32 = sb("t32", [128, NCH, T], F32)
        sq = sb("sq", [128, NCH * T], BF16)
        hT = sb("hT", [128, NCH, HM + T], BF16)
        ms = sb("ms", [128, T], F32)
        rstdA = sb("rstdA", [128, T], F32)
        rstdB = sb("rstdB", [128, T], F32)
        mean = sb("mean", [128, T], F32)
        m2 = sb("m2", [128, T], F32)
        Sa = sb("Sa", [128, NCH, HM + T], BF16)
        Sb_ = sb("Sb", [128, NCH, HM + T], BF16)
        Dt = sb("Dt", [128, NCH, T], BF16)
        sT = sb("sT", [128, NCH, T], BF16)
        Ab = [sb("Ab%d" % i, [128, 2 + T], BF16) for i in range(NAB)]
        Bb = [sb("Bb%d" % i, [128, 2 + T], BF16) for i in range(NAB)]
        cg = [sb("cg%d" % i, [128, T], BF16) for i in range(NCG)]
        cv = [sb("cv%d" % i, [128, T], BF16) for i in range(NCG)]
        uT = sb("uT", [128, NQ, T], BF16)
        u31 = sb("u31", [128, NCH, CM + T], BF16)
        vh = [sb("vh%d" % i, [128, T], BF16) for i in range(2)]
        th = [sb("th%d" % i, [128, T], BF16) for i in range(2)]
        HA = sb("HA", [128, 2, NA, 2], BF16)
        HB = sb("HB", [128, 2, NA, 2], BF16)
        w8 = [sb("w8_%d" % i, [128, 4096], BF16) for i in range(NS8)]
        wd = [sb("wd_%d" % i, [128, NQ * 128], BF16) for i in range(NSD)]
        poolw = sb("poolw", [128, 2048], BF16)
        psb = [es.enter_context(nc.psum_tensor("ps%d" % i, [128, T], F32)) for i in range(8)]
        NMAIN = 6
        state = {"bank": 0}

        def next_bank():
            b = state["bank"]
            state["bank"] = (b + 1) % NMAIN
            return b

        V = lambda name, a, n=1: vecs[:, _VOFF[name] + a:_VOFF[name] + a + n]
        XK = lambda s: [("x", s, c) for c in range(NCH)]

        S.dma("sp", vecs[:], vecs_d, writes=["vecs"], sem_key="vecs")
        S.dma("sp", xt[0][:], xT[:, :, 0:T], writes=XK(0), sem_key=("x", 0))
        S.op("pool", lambda: nc.gpsimd.memset(ones[:], 1.0 / D), writes=["ones"])
        S.op("pool", lambda: nc.gpsimd.memset(hT[:, :, 0:HM], 0.0), writes=["hm"])
        S.op("pool", lambda: nc.gpsimd.memset(u31[:, :, 0:CM], 0.0), writes=["u31m"])
        S.op("pool", lambda: nc.gpsimd.memset(HA[:], 0.0), writes=["HA0", "HA1"])
        S.op("pool", lambda: nc.gpsimd.memset(HB[:], 0.0), writes=["HB0", "HB1"])
        S.op("act", lambda: nc.scalar.activation(out=cact[:], in_=V("c", 0, 8), func=AF.Silu),
             reads=["vecs"], writes=["cact"])

        SQK = [("sq", c) for c in range(NCH)]
        HK = [("h", c) for c in range(NCH)]
        GS = lambda l_, j, c: der[:, (l_ * 2 + j) * 8 + c:(l_ * 2 + j) * 8 + c + 1]
        GG = lambda l_, j, c: der[:, 32 + (l_ * 2 + j) * 8 + c:32 + (l_ * 2 + j) * 8 + c + 1]
        SH = lambda l_, j, c: modT[:, l_ * 48 + 3 * j * 8 + c:l_ * 48 + 3 * j * 8 + c + 1]

        def mm(out, lhsT, rhs, start, stop, reads, writes, signal=None):
            S.op("pe", lambda: nc.tensor.matmul(out, lhsT, rhs, start=start, stop=stop),
                 reads=reads, writes=writes, signal=(stop if signal is None else signal))

        modps = psb[7]
        for hb in range(24):
            blk, half = hb // 2, hb % 2
            l_, i_ = blk // 6, blk % 6
            s = hb % NS8
            S.dma("pool", w8[s][:, :].rearrange("p (a n) -> p a n", n=2048),
                  adaf[hb].rearrange("p (a n) -> p a n", n=2048),
                  writes=[("w8", s)], sem_key=("w8", s))
            for cc in range(4):
                col = l_ * 48 + i_ * 8 + half * 4 + cc
                for kc in range(8):
                    o = kc * 512 + cc * 128
                    mm(modps[:, col:col + 1], w8[s][:, o:o + 128], cact[:, kc:kc + 1],
                       kc == 0, kc == 7, [("w8", s), "cact"], [("ps", 7)])
        S.op("dve", lambda: nc.vector.tensor_tensor(modT[:], modps[:, 0:96], V("adab", 0, 96), ALU.add),
             reads=[("ps", 7), "vecs"], writes=["modT"])
        for l_ in range(2):
            for j in range(2):
                o = (l_ * 2 + j) * 8
                sc0 = l_ * 48 + (1 + 3 * j) * 8
                gt0 = l_ * 48 + (2 + 3 * j) * 8
                S.op("dve", lambda: nc.vector.scalar_tensor_tensor(
                    der[:, o:o + 8], modT[:, sc0:sc0 + 8], 1.0, V("preg", l_ * 16 + j * 8, 8),
                    ALU.add, ALU.mult), reads=["modT", "vecs"], writes=["der"])
                S.op("dve", lambda: nc.vector.tensor_tensor(
                    der[:, 32 + o:32 + o + 8], modT[:, gt0:gt0 + 8], V("postg", l_ * 16 + j * 8, 8),
                    ALU.mult), reads=["modT", "vecs"], writes=["der"])
        S.op("dve", lambda: nc.vector.tensor_scalar(der[:, 64:80], V("bpw1", 0, 16), 0.5, None, ALU.mult),
             reads=["vecs"], writes=["der"])
        S.op("dve", lambda: nc.vector.tensor_tensor(der[:, 80:88], der[:, 32:40], V("pscale", 0, 8), ALU.mult),
             reads=["der", "vecs"], writes=["der"])
        S.op("dve", lambda: nc.vector.tensor_tensor(der[:, 88:96], der[:, 48:56], V("bpw2", 0, 8), ALU.mult),
             reads=["der", "vecs"], writes=["der"])

        S.dma("pool", poolw[:], poolwf, writes=["poolw"], sem_key="poolw")

        def cast_w8(g):
            S.dma("pool", w8b[g].rearrange("p (a n) -> (p a) n", n=2048),
                  w8f[g].rearrange("p (a n) -> (p a) n", n=2048),
                  writes=[("w8b", g)], sem_key=("cv", g))

        def cast_wd(g):
            S.dma("pool", wdb[g].rearrange("p (a n) -> (p a) n", n=1408),
                  wdf[g].rearrange("p (a n) -> (p a) n", n=1408),
                  writes=[("wdb", g)], sem_key=("cw", g))

        for g in range(11):
            cast_w8(g)
        for g in range(8):
            cast_wd(g)
        deferred_casts = [(cast_w8, g) for g in range(11, 17)] + [(cast_w8, g) for g in range(17, 28)] \
            + [(cast_wd, g) for g in range(8, 16)]

        def diag_builder():
            for idx in range(NCH * 31):
                g = idx // 32
                stg, sk = (Sa, "Sa") if g % 2 == 0 else (Sb_, "Sb")
                keys = [(sk, c) for c in range(NCH)]
                st_c, st_o = (idx % 32) // 4, ((idx % 32) % 4) * 128
                S.op("dve", lambda: nc.vector.tensor_scalar(stg[:, st_c, st_o:st_o + 128], V("ident", 0, 128),
                                                            V("w31", idx), None, ALU.mult),
                     reads=["vecs"], writes=keys)
                if idx % 32 == 31 or idx == NCH * 31 - 1:
                    S.dma("sp", dgb[g].rearrange("p (q n) -> p q n", n=T), stg[:, :, 0:T], reads=keys,
                          writes=[("dgb", g)], sem_key=("dg", g % 2))
                yield

        diag_gen = diag_builder()

        w8_srcs, wd_srcs = [], []
        for i in range(n_tiles):
            for g in range(11):
                w8_srcs.append((w8b[g], ("w8b", g)))
            for g in range(11, 15):
                w8_srcs.append((w8b[g], ("w8b", g)))
            for g in range(N_DG):
                w8_srcs.append((dgb[g], ("dgb", g)))
            for g in range(15, 28):
                w8_srcs.append((w8b[g], ("w8b", g)))
            for g in range(N_WD):
                wd_srcs.append((wdb[g], ("wdb", g)))
        W8 = Stream(S, "w8", w8, w8_srcs, NS8)
        WD = Stream(S, "wd", wd, wd_srcs, NSD)

        def load_x(i):
            s = i % 2
            S.dma("sp", xt[s][:], xT[:, :, i * T:(i + 1) * T], writes=XK(s), sem_key=("x", s))

        def store_x(i):
            s = i % 2
            S.dma("sp", outT[:, :, i * T:(i + 1) * T], xt[s][:], reads=XK(s), sem_key=("o", s))

        def stats_mm(bank, src_ap, c, reads):
            mm(psb[bank][:], ones[:], src_ap, c == 0, c == NCH - 1, ["ones"] + reads, [("ps", bank)])

        I32 = mybir.dt.int32

        NBLK = T // 32

        def newton_rsqrt(src_ap, src_keys, add_eps, dst, dkey):
            S.op("dve", lambda: nc.vector.transpose(mh[:], src_ap), reads=src_keys, writes=["mh"])
            cview = mh[:].rearrange("p (b i) -> p b i", i=32)[:, :, 0:1]
            ca, cy, ch, cz = (cbuf[:, k, 0:NBLK] for k in range(4))
            S.op("dve", lambda: nc.vector.tensor_scalar(ca.unsqueeze(2), cview, EPS if add_eps else 0.0, None, ALU.add),
                 reads=["mh"], writes=["cbuf"])
            S.op("dve", lambda: nc.vector.tensor_scalar(cy.bitcast(I32), ca.bitcast(I32), 1, None, ALU.arith_shift_right),
                 reads=["cbuf"], writes=["cbuf"])
            S.op("dve", lambda: nc.vector.tensor_scalar(cy.bitcast(I32), cy.bitcast(I32), -1.0, 1597463007.0,
                                                        ALU.mult, ALU.add), reads=["cbuf"], writes=["cbuf"])
            S.op("dve", lambda: nc.vector.tensor_scalar(ch, ca, -0.5, None, ALU.mult), reads=["cbuf"], writes=["cbuf"])
            for _ in range(2):
                S.op("dve", lambda: nc.vector.tensor_tensor(cz, cy, cy, ALU.mult), reads=["cbuf"], writes=["cbuf"])
                S.op("dve", lambda: nc.vector.tensor_tensor(cz, cz, ch, ALU.mult), reads=["cbuf"], writes=["cbuf"])
                S.op("dve", lambda: nc.vector.scalar_tensor_tensor(cy, cz, 1.5, cy, ALU.add, ALU.mult),
                     reads=["cbuf"], writes=["cbuf"])
            S.op("dve", lambda: nc.vector.tensor_copy(zz[:].rearrange("p (b i) -> p b i", i=32),
                                                      cy.unsqueeze(2).broadcast_to([128, NBLK, 32])),
                 reads=["cbuf"], writes=["zz"])
            S.op("dve", lambda: nc.vector.transpose(dst[:], zz[:]), reads=["zz"], writes=[dkey])

        def rstd_from(bank, dst, dkey):
            newton_rsqrt(psb[bank][:], [("ps", bank)], True, dst, dkey)

        def pre_norm(s, l_, j):
            for c in range(NCH):
                S.op("act", lambda: nc.scalar.activation(out=sq[:, c * T:(c + 1) * T], in_=xt[s][:, c, :],
                                                         func=AF.Square),
                     reads=[("x", s, c)], writes=[("sq", c)])
                stats_mm(6, sq[:, c * T:(c + 1) * T], c, [("sq", c)])
            rstd_from(6, rstdA, "rstdA")
            for c in range(NCH):
                e, eng = ("pool", nc.gpsimd) if c in POOL_PRE else ("dve", nc.vector)
                S.op(e, lambda: eng.tensor_tensor(t32[:, c, :], xt[s][:, c, :], rstdA[:], ALU.mult),
                     reads=[("x", s, c), "rstdA"], writes=[("t32", c)])
                S.op("act", lambda: nc.scalar.activation(out=hT[:, c, HM:HM + T], in_=t32[:, c, :],
                                                         func=AF.Identity, scale=GS(l_, j, c), bias=SH(l_, j, c)),
                     reads=[("t32", c), "der", "modT"], writes=[("h", c)])

        def post_chunk(s, m, bank, scale, bias, gscale, gbias):
            S.op("act", lambda: nc.scalar.activation(out=t32[:, m, :], in_=psb[bank][:], func=AF.Identity,
                                                     scale=gscale, bias=gbias),
                 reads=[("ps", bank), "vecs", "der"], writes=[("t32", m)])
            S.op("act", lambda: nc.scalar.activation(out=sq[:, m * T:(m + 1) * T], in_=psb[bank][:],
                                                     func=AF.Square, scale=scale, bias=bias),
                 reads=[("ps", bank), "vecs"], writes=[("sq", m)])

        def post_stats(m):
            stats_mm(7, sq[:, m * T:(m + 1) * T], m, [("sq", m)])

        def post_finish(s, l_, j):
            rstd_from(7, rstdB, "rstdB")
            order = [m for m in range(NCH) if m in POOL_UPD] + [m for m in range(NCH) if m not in POOL_UPD]
            for m in order:
                e, eng = ("pool", nc.gpsimd) if m in POOL_UPD else ("dve", nc.vector)
                S.op(e, lambda: eng.tensor_tensor(t32[:, m, :], t32[:, m, :], rstdB[:], ALU.mult),
                     reads=[("t32", m), "rstdB"], writes=[("t32", m)])
                S.op(e, lambda: eng.tensor_tensor(xt[s][:, m, :], xt[s][:, m, :], t32[:, m, :], ALU.add),
                     reads=[("t32", m), ("x", s, m)], writes=[("x", s, m)])

        def pool_mixer(i, s):
            W = HM + T
            SaK = lambda a, b: [("Sa", c) for c in range(a, b)]
            SbK = lambda a, b: [("Sb", c) for c in range(a, b)]
            S.op("dve", lambda: nc.vector.tensor_tensor(Sa[:, :, 1:W], hT[:, :, 1:W], hT[:, :, 0:W - 1], ALU.add),
                 reads=HK + ["hm"], writes=SaK(0, 8))
            S.op("dve", lambda: nc.vector.tensor_tensor(Sb_[:, 2:8, 3:W], Sa[:, 2:8, 3:W], Sa[:, 2:8, 1:W - 2], ALU.add),
                 reads=SaK(2, 8), writes=SbK(2, 8))
            S.op("dve", lambda: nc.vector.tensor_tensor(Sa[:, 4:8, 7:W], Sb_[:, 4:8, 7:W], Sb_[:, 4:8, 3:W - 4], ALU.add),
                 reads=SbK(4, 8), writes=SaK(4, 8))
            S.op("dve", lambda: nc.vector.tensor_tensor(Sb_[:, 6:8, 15:W], Sa[:, 6:8, 15:W], Sa[:, 6:8, 7:W - 8], ALU.add),
                 reads=SaK(6, 8), writes=SbK(6, 8))
            for g in range(4):
                src, sk = (Sa, "Sa") if g in (0, 2) else (Sb_, "Sb")
                c0 = 2 * g
                keys = [(sk, c0), (sk, c0 + 1)]
                if i == 0:
                    S.op("dve", lambda: nc.vector.tensor_tensor(
                        src[:, c0:c0 + 2, HM:HM + 16], src[:, c0:c0 + 2, HM:HM + 16],
                        V("corr", g * 16, 16).unsqueeze(1).broadcast_to([128, 2, 16]), ALU.mult),
                        reads=keys + ["vecs"], writes=keys)
                S.op("dve", lambda: nc.vector.scalar_tensor_tensor(
                    Dt[:, c0:c0 + 2, :], src[:, c0:c0 + 2, HM:HM + T], 1.0 / (2 << g),
                    hT[:, c0:c0 + 2, HM:HM + T], ALU.mult, ALU.subtract),
                    reads=keys + [("h", c0), ("h", c0 + 1)], writes=[("D", c0), ("D", c0 + 1)])
            S.op("pool", lambda: nc.gpsimd.tensor_copy(hT[:, :, 0:HM], hT[:, :, T:T + HM]),
                 reads=HK, writes=["hm"])
            for m in range(NCH):
                g = m // 2
                bank = next_bank()
                for kc in range(2):
                    o = (g * 2 + kc) * 256 + (m % 2) * 128
                    mm(psb[bank][:], poolw[:, o:o + 128], Dt[:, g * 2 + kc, :], kc == 0, kc == 1,
                       ["poolw", ("D", g * 2 + kc)], [("ps", bank)])
                post_chunk(s, m, bank, V("pscale", m), 0.0, der[:, 80 + m:81 + m], 0.0)
                if m > 0:
                    post_stats(m - 1)
            post_stats(NCH - 1)
            post_finish(s, 0, 0)

        def ffn(i, s, l_):
            pre_norm(s, l_, 1)
            base = i * 36 + (0 if l_ == 0 else 25)
            based = i * 16 + l_ * 8
            WD.get(based)
            info = {}

            def st_pe(n):
                q, isv = n // 2, n % 2
                ach = q + (NQ if isv else 0)
                g, pos = n // 4, n % 4
                r = n % NAB
                if n < 4:
                    if n == 0:
                        slot = W8.get(base)
                        banks = [next_bank() for _ in range(4)]
                        for kc in range(8):
                            for p4 in range(4):
                                o = kc * 512 + p4 * 128
                                mm(psb[banks[p4]][:], w8[slot][:, o:o + 128], hT[:, kc, HM:HM + T], kc == 0, kc == 7,
                                   [("w8", slot), ("h", kc)], [("ps", banks[p4])])
                        info["banks"] = banks
                    bank = info["banks"][n]
                else:
                    slot = W8.get(base + g)
                    bank = next_bank()
                    for kc in range(8):
                        o = kc * 512 + pos * 128
                        mm(psb[bank][:], w8[slot][:, o:o + 128], hT[:, kc, HM:HM + T], kc == 0, kc == 7,
                           [("w8", slot), ("h", kc)], [("ps", bank)])
                info[n] = (q, isv, ach, bank, r)
                S.op("pool", lambda: nc.gpsimd.tensor_copy(Ab[r][:, 0:2], HA[:, l_, ach, :]),
                     reads=[("HA", l_)], writes=[("Ah", r)])
                S.op("pool", lambda: nc.gpsimd.tensor_copy(Bb[r][:, 0:1], HB[:, l_, ach, 0:1]),
                     reads=[("HB", l_)], writes=[("Bh", r)])

            def st_evac(n):
                q, isv, ach, bank, r = info[n]
                w1 = V("wdw", l_ * 132 + ach * 3 + 1)
                S.op("act", lambda: nc.scalar.activation(out=Ab[r][:, 2:2 + T], in_=psb[bank][:], func=AF.Copy),
                     reads=[("ps", bank)], writes=[("A", r), ("At", r)])
                S.op("act", lambda: nc.scalar.activation(out=Bb[r][:, 1:1 + T], in_=psb[bank][:], func=AF.Identity,
                                                         scale=w1),
                     reads=[("ps", bank), "vecs"], writes=[("B", r), ("Bt", r)])
                S.op("pool", lambda: nc.gpsimd.tensor_copy(HA[:, l_, ach, :], Ab[r][:, T:T + 2]),
                     reads=[("At", r)], writes=[("HA", l_)])
                S.op("pool", lambda: nc.gpsimd.tensor_copy(HB[:, l_, ach, 0:1], Bb[r][:, T:T + 1]),
                     reads=[("Bt", r)], writes=[("HB", l_)])

            def st_conv(n):
                q, isv, ach, bank, r = info[n]
                w0 = V("wdw", l_ * 132 + ach * 3 + 0)
                w2 = V("wdw", l_ * 132 + ach * 3 + 2)
                S.op("dve", lambda: nc.vector.scalar_tensor_tensor(Bb[r][:, 0:T], Ab[r][:, 0:T], w0,
                                                                  Bb[r][:, 0:T], ALU.mult, ALU.add),
                     reads=[("Ah", r), ("A", r), ("Bh", r), ("B", r), "vecs"], writes=[("Bh", r), ("B", r)])
                dst, dk = (cv, "cv") if isv else (cg, "cg")
                S.op("dve", lambda: nc.vector.scalar_tensor_tensor(dst[q % NCG][:], Ab[r][:, 2:2 + T], w2,
                                                                  Bb[r][:, 0:T], ALU.mult, ALU.add),
                     reads=[("A", r), ("At", r), ("Bh", r), ("B", r), "vecs"], writes=[(dk, q % NCG)])

            def st_silu(n):
                q, isv, ach, bank, r = info[n]
                if not isv:
                    S.op("act", lambda: nc.scalar.activation(out=cg[q % NCG][:], in_=cg[q % NCG][:], func=AF.Silu),
                         reads=[("cg", q % NCG)], writes=[("cg", q % NCG)])

            def st_mul(n):
                q, isv, ach, bank, r = info[n]
                if isv:
                    S.op("dve", lambda: nc.vector.tensor_tensor(uT[:, q, :], cg[q % NCG][:], cv[q % NCG][:], ALU.mult),
                         reads=[("cg", q % NCG), ("cv", q % NCG)], writes=[("u", q)])

            stages = [st_pe, st_evac, st_conv, st_silu, st_mul]
            for it in range(NA + len(stages) - 1):
                for k, fn in enumerate(stages):
                    n = it - k
                    if 0 <= n < NA:
                        fn(n)
                if deferred_casts:
                    fcast, gcast = deferred_casts.pop(0)
                    fcast(gcast)
                for _ in range(6):
                    next(diag_gen, None)
            while deferred_casts:
                fcast, gcast = deferred_casts.pop(0)
                fcast(gcast)
            for _ in diag_gen:
                pass
            for m in range(NCH):
                slot = WD.get(based + m)
                bank = next_bank()
                for fc in range(NQ):
                    mm(psb[bank][:], wd[slot][:, fc * 128:(fc + 1) * 128], uT[:, fc, :], fc == 0, fc == NQ - 1,
                       [("wd", slot), ("u", fc)], [("ps", bank)])
                post_chunk(s, m, bank, 1.0, 0.0, GG(l_, 1, m), 0.0)
                if m > 0:
                    post_stats(m - 1)
            post_stats(NCH - 1)
            post_finish(s, l_, 1)

        def conformer(i, s):
            pre_norm(s, 1, 0)
            base = i * 36 + 11
            cinfo = {}

            def c_pe(n):
                g, pos = n // 4, n % 4
                if n < 4:
                    if n == 0:
                        slot = W8.get(base)
                        banks = [next_bank() for _ in range(4)]
                        for kc in range(8):
                            for p4 in range(4):
                                o = kc * 512 + p4 * 128
                                mm(psb[banks[p4]][:], w8[slot][:, o:o + 128], hT[:, kc, HM:HM + T], kc == 0, kc == 7,
                                   [("w8", slot), ("h", kc)], [("ps", banks[p4])])
                        cinfo["banks"] = banks
                    cinfo[n] = cinfo["banks"][n]
                    return
                slot = W8.get(base + g)
                bank = next_bank()
                for kc in range(8):
                    o = kc * 512 + pos * 128
                    mm(psb[bank][:], w8[slot][:, o:o + 128], hT[:, kc, HM:HM + T], kc == 0, kc == 7,
                       [("w8", slot), ("h", kc)], [("ps", bank)])
                cinfo[n] = bank

            def c_evac(n):
                j, isg = n // 2, n % 2
                bank = cinfo[n]
                r = j % 2
                if not isg:
                    S.op("act", lambda: nc.scalar.activation(out=vh[r][:], in_=psb[bank][:], func=AF.Identity,
                                                             scale=0.5, bias=der[:, 64 + j:65 + j]),
                         reads=[("ps", bank), "der"], writes=[("vh", r)])
                else:
                    S.op("act", lambda: nc.scalar.activation(out=th[r][:], in_=psb[bank][:], func=AF.Tanh,
                                                             scale=0.5, bias=der[:, 72 + j:73 + j]),
                         reads=[("ps", bank), "der"], writes=[("th", r)])

            def c_glu(n):
                j, isg = n // 2, n % 2
                r = j % 2
                if isg:
                    S.op("dve", lambda: nc.vector.scalar_tensor_tensor(u31[:, j, CM:CM + T], th[r][:], 1.0, vh[r][:],
                                                                      ALU.add, ALU.mult),
                         reads=[("th", r), ("vh", r)], writes=[("u31", j)])

            cstages = [c_pe, c_evac, c_glu]
            for it in range(16 + len(cstages) - 1):
                for k, fn in enumerate(cstages):
                    n = it - k
                    if 0 <= n < 16:
                        fn(n)
            based = i * 36 + 15

            def ln_stats(j):
                stats_mm(6, Dt[:, j, :], j, [("D", j)])
                stats_mm(7, sq[:, j * T:(j + 1) * T], j, [("sq", j)])

            for j in range(NCH):
                bank = next_bank()
                for k in range(31):
                    idx = j * 31 + k
                    slot = W8.get(based + idx // 32)
                    p0 = (idx % 32) * 128
                    mm(psb[bank][:], w8[slot][:, p0:p0 + 128], u31[:, j, k:k + T], k == 0, k == 30,
                       [("w8", slot), ("u31", j), "u31m"], [("ps", bank)])
                S.op("act", lambda: nc.scalar.activation(out=Dt[:, j, :], in_=psb[bank][:], func=AF.Identity,
                                                         bias=V("bdw", j)),
                     reads=[("ps", bank), "vecs"], writes=[("D", j)])
                S.op("act", lambda: nc.scalar.activation(out=sq[:, j * T:(j + 1) * T], in_=Dt[:, j, :], func=AF.Square),
                     reads=[("D", j)], writes=[("sq", j)])
                if j > 0:
                    ln_stats(j - 1)
            ln_stats(NCH - 1)
            S.op("pool", lambda: nc.gpsimd.tensor_copy(u31[:, :, 0:CM], u31[:, :, T:T + CM]),
                 reads=[("u31", j) for j in range(NCH)], writes=["u31m"])
            S.op("act", lambda: nc.scalar.activation(out=m2[:], in_=psb[6][:], func=AF.Square),
                 reads=[("ps", 6)], writes=["m2"])
            S.op("dve", lambda: nc.vector.tensor_tensor(ms[:], psb[7][:], m2[:], ALU.subtract),
                 reads=[("ps", 7), "m2"], writes=["ms"])
            S.op("dve", lambda: nc.vector.tensor_scalar(ms[:], ms[:], 0.0, EPS, ALU.max, ALU.add),
                 reads=["ms"], writes=["ms"])
            newton_rsqrt(ms[:], ["ms"], False, rstdA, "rstdA")
            for j in range(NCH):
                S.op("dve", lambda: nc.vector.tensor_tensor(t32[:, j, :], Dt[:, j, :], psb[6][:], ALU.subtract),
                     reads=[("D", j), ("ps", 6)], writes=[("t32", j)])
                S.op("dve", lambda: nc.vector.tensor_tensor(t32[:, j, :], t32[:, j, :], rstdA[:], ALU.mult),
                     reads=[("t32", j), "rstdA"], writes=[("t32", j)])
                S.op("act", lambda: nc.scalar.activation(out=sT[:, j, :], in_=t32[:, j, :], func=AF.Silu,
                                                         scale=V("lng", j), bias=V("lnb", j)),
                     reads=[("t32", j), "vecs"], writes=[("sT", j)])
            for m in range(NCH):
                g, pos = m // 4, m % 4
                slot = W8.get(i * 36 + 23 + g)
                bank = next_bank()
                for kc in range(8):
                    o = kc * 512 + pos * 128
                    mm(psb[bank][:], w8[slot][:, o:o + 128], sT[:, kc, :], kc == 0, kc == 7,
                       [("w8", slot), ("sT", kc)], [("ps", bank)])
                post_chunk(s, m, bank, 1.0, V("bpw2", m), GG(1, 0, m), der[:, 88 + m:89 + m])
                if m > 0:
                    post_stats(m - 1)
            post_stats(NCH - 1)
            post_finish(s, 1, 0)

        for i in range(n_tiles):
            s = i % 2
            pre_norm(s, 0, 0)
            pool_mixer(i, s)
            ffn(i, s, 0)
            conformer(i, s)
            if i + 1 < n_tiles:
                load_x(i + 1)
            ffn(i, s, 1)
            store_x(i)
        for k in [("o", 0), ("o", 1)]:
            if k in S.dma_sems:
                S._wait("sp", ("d_%s" % (k,), S.dma_sems[k], S.dma_cnt[k]))
        S.flush_all()
    return nc, S


def _g8(Wc):
    return np.ascontiguousarray(Wc.reshape(8, 128, 512).transpose(1, 0, 2)).reshape(128, 4096)


def _cols(order):
    return np.concatenate([np.arange(ch * 128, (ch + 1) * 128) for ch in order])


def _fm(v):
    v = np.asarray(v, np.float32)
    return v.reshape(-1, 128).T


def prepare_inputs(inputs):
    f = lambda k: np.asarray(inputs[k], dtype=np.float32)
    x, c = f("x"), f("c")
    ada_w, ada_b = f("ada_w"), f("ada_b")
    ffn_w_up, ffn_w_dw, ffn_w_down = f("ffn_w_up"), f("ffn_w_dw"), f("ffn_w_down")
    w8f = np.empty((N_W8, 128, 4096), np.float32)
    up_order = []
    for q in range(NQ):
        up_order += [q, NQ + q]
    ucols = _cols(up_order)
    for l in range(2):
        Wp = ffn_w_up[l][:, ucols]
        for g in range(11):
            w8f[(0 if l == 0 else 17) + g] = _g8(Wp[:, g * 512:(g + 1) * 512])
    p1_order = []
    for j in range(8):
        p1_order += [j, 8 + j]
    Wp = f("cv_w_pw1")[0][:, _cols(p1_order)]
    for g in range(4):
        w8f[11 + g] = _g8(Wp[:, g * 512:(g + 1) * 512])
    Wp = f("cv_w_pw2")[0]
    for g in range(2):
        w8f[15 + g] = _g8(Wp[:, g * 512:(g + 1) * 512])
    wdf = np.empty((N_WD, 128, NQ * 128), np.float32)
    for l in range(2):
        for m in range(8):
            blk = ffn_w_down[l][:, m * 128:(m + 1) * 128]
            wdf[l * 8 + m] = np.ascontiguousarray(blk.reshape(NQ, 128, 128).transpose(1, 0, 2)).reshape(128, NQ * 128)
    adaf = np.empty((24, 128, 4096), np.float32)
    for hb in range(24):
        blk, half = hb // 2, hb % 2
        l, i = blk // 6, blk % 6
        adaf[hb] = _g8(ada_w[l][:, i * 1024 + half * 512:i * 1024 + half * 512 + 512])
    pw = f("pool_w")[0]
    poolwf = np.ascontiguousarray(pw.reshape(4, 2, 128, 256).transpose(2, 0, 1, 3)).reshape(128, 2048)

    vec_common = np.zeros((128, NV), np.float32)

    def put(name, arr):
        arr = np.asarray(arr, np.float32)
        vec_common[:, _VOFF[name]:_VOFF[name] + arr.shape[1]] = arr

    put("adab", np.concatenate([_fm(ada_b[l]) for l in range(2)], axis=1))
    put("preg", np.concatenate([_fm(f("pre_g")[l, j]) for l in range(2) for j in range(2)], axis=1))
    put("postg", np.concatenate([_fm(f("post_g")[l, j]) for l in range(2) for j in range(2)], axis=1))
    put("pscale", _fm(f("pool_scale")[0]))
    put("bpw1", _fm(f("cv_b_pw1")[0]))
    w31 = f("cv_w_dw")[0]
    put("w31", np.ascontiguousarray(w31.reshape(31, 8, 128).transpose(2, 1, 0)).reshape(128, 248))
    put("bdw", _fm(f("cv_b_dw")[0]))
    put("lng", _fm(f("cv_ln_g")[0]))
    put("lnb", _fm(f("cv_ln_b")[0]))
    put("bpw2", _fm(f("cv_b_pw2")[0]))
    put("wdw", np.concatenate(
        [np.ascontiguousarray(ffn_w_dw[l].reshape(3, NA, 128).transpose(2, 1, 0)).reshape(128, NA * 3)
         for l in range(2)], axis=1))
    put("ident", np.eye(128, dtype=np.float32))
    corr = np.zeros((4, 16), np.float32)
    for g in range(4):
        w = 2 << g
        for t in range(16):
            corr[g, t] = w / min(t + 1, w)
    put("corr", np.broadcast_to(corr.reshape(1, 64), (128, 64)))

    in_maps = []
    for b in range(NB):
        vb = vec_common.copy()
        vb[:, _VOFF["c"]:_VOFF["c"] + 8] = _fm(c[b])
        xTb = np.ascontiguousarray(x[b].T.reshape(NCH, 128, SEQ).transpose(1, 0, 2))
        in_maps.append({"xT": xTb, "vecs": vb, "w8f": w8f, "wdf": wdf, "adaf": adaf, "poolwf": poolwf})
    return in_maps


def kernel(**inputs):
    in_maps = prepare_inputs(inputs)
    nc, _ = build_program()
    res = run_bass_kernel_spmd(nc, in_maps, core_ids=list(range(NB)))
    out = np.empty((NB, SEQ, D), np.float32)
    for b in range(NB):
        o = np.asarray(res.results[b]["outT"], dtype=np.float32)
        out[b] = o.transpose(1, 0, 2).reshape(D, SEQ).T
    return out
```

```python
import contextlib
import numpy as np
import concourse.bass as bass
import concourse.mybir as mybir
from concourse.bass_utils import run_bass_kernel_spmd

F32 = mybir.dt.float32
BF16 = mybir.dt.bfloat16
AF = mybir.ActivationFunctionType
ALU = mybir.AluOpType

D = 1024
SEQ = 8192
NB = 8
FF = 2816
NCH = 8
NA = 44
NQ = 22
EPS = 1e-6
T = 512
NT = SEQ // T
HM = 16
CM = 30

_VOFF = {}
_cur = 0
for _n, _w in [("c", 8), ("adab", 96), ("preg", 32), ("postg", 32), ("pscale", 8),
               ("bpw1", 16), ("w31", 248), ("bdw", 8), ("lng", 8), ("lnb", 8),
               ("bpw2", 8), ("wdw", 264), ("ident", 128), ("corr", 64)]:
    _VOFF[_n] = _cur
    _cur += _w
NV = _cur

N_W8 = 28
N_DG = 8
N_WD = 16


class Sched:
    def __init__(self, nc, es):
        self.nc = nc
        self.es = es
        self.eng = {"pe": nc.tensor, "act": nc.scalar, "dve": nc.vector,
                    "pool": nc.gpsimd, "sp": nc.sync}
        self.sem = {k: es.enter_context(nc.semaphore("sem_" + k)) for k in self.eng}
        self.cnt = {k: 0 for k in self.eng}
        self.waited = {k: {} for k in self.eng}
        self.last_w = {}
        self.readers = {}
        self.dma_sems = {}
        self.dma_cnt = {}
        self.pending = {k: ([], []) for k in self.eng}
        self.last_ins = {k: None for k in self.eng}
        self.n_inst = 0

    def flush(self, e):
        pr, pw = self.pending[e]
        if not pr and not pw:
            return
        self.cnt[e] += 1
        self.last_ins[e].then_inc(self.sem[e], 1)
        self._commit((e, self.sem[e], self.cnt[e]), pr, pw)
        self.pending[e] = ([], [])

    def flush_all(self):
        for e in self.eng:
            self.flush(e)

    def _flush_conflicts(self, e, reads, writes):
        for e2 in self.eng:
            if e2 == e:
                continue
            pr, pw = self.pending[e2]
            if not pr and not pw:
                continue
            hit = any(r in pw for r in reads) or any((w in pw) or (w in pr) for w in writes)
            if hit:
                self.flush(e2)

    def _wait(self, e, ev):
        if ev is None:
            return
        sname, sem, val = ev
        w = self.waited[e]
        if w.get(sname, 0) >= val:
            return
        w[sname] = val
        self.eng[e].wait_ge(sem, val)
        self.n_inst += 1

    def _deps(self, e, reads, writes):
        self._flush_conflicts(e, reads, writes)
        for r in reads:
            self._wait(e, self.last_w.get(r))
        for w in writes:
            self._wait(e, self.last_w.get(w))
            for ev in self.readers.get(w, ()):
                self._wait(e, ev)

    def _commit(self, ev, reads, writes):
        for r in reads:
            self.readers.setdefault(r, []).append(ev)
        for w in writes:
            self.last_w[w] = ev
            self.readers[w] = []

    def op(self, e, fn, reads=(), writes=(), signal=True):
        reads = list(reads)
        writes = list(writes)
        writes += [r for r in reads if isinstance(r, tuple) and r[0] == "ps" and r not in writes]
        self._deps(e, reads, writes)
        ins = fn()
        self.last_ins[e] = ins
        self.n_inst += 1
        pr, pw = self.pending[e]
        if signal:
            self.cnt[e] += 1
            ins.then_inc(self.sem[e], 1)
            ev = (e, self.sem[e], self.cnt[e])
            self._commit(ev, pr + reads, pw + writes)
            self.pending[e] = ([], [])
        else:
            pr.extend(reads)
            pw.extend(writes)
        return ins

    def dma(self, q, out, in_, reads=(), writes=(), sem_key=None, **kw):
        reads = list(reads)
        writes = list(writes)
        self._deps(q, reads, writes)
        if sem_key not in self.dma_sems:
            self.dma_sems[sem_key] = self.es.enter_context(
                self.nc.semaphore("dsem_%d" % len(self.dma_sems)))
            self.dma_cnt[sem_key] = 0
        sem = self.dma_sems[sem_key]
        name = "d_%s" % (sem_key,)
        if self.dma_cnt[sem_key] > 0:
            self._wait(q, (name, sem, self.dma_cnt[sem_key]))
        self.dma_cnt[sem_key] += 16
        ins = self.eng[q].dma_start(out=out, in_=in_, **kw)
        ins.then_inc(sem, 16)
        self.n_inst += 1
        ev = (name, sem, self.dma_cnt[sem_key])
        self._commit(ev, reads, writes)
        return ev


class Stream:
    def __init__(self, S, name, slots, srcs, nslots):
        self.S, self.name, self.slots, self.srcs = S, name, slots, srcs
        self.ns = nslots
        self.issued = 0

    def get(self, n, ahead=None):
        ahead = self.ns - 1 if ahead is None else ahead
        upto = min(len(self.srcs) - 1, n + ahead)
        while self.issued <= upto:
            k = self.issued
            s = k % self.ns
            src_ap, src_key = self.srcs[k]
            self.S.dma("sp", self.slots[s][:], src_ap, reads=[src_key],
                       writes=[(self.name, s)], sem_key=(self.name, s))
            self.issued += 1
        return n % self.ns


def build_program(n_tiles=NT):
    nc = bass.Bass("TRN2", target_bir_lowering=False)
    dram = lambda n, sh, dt, kind: nc.dram_tensor(n, sh, dt, kind=kind).ap()
    xT = dram("xT", [128, NCH, SEQ], F32, "ExternalInput")
    vecs_d = dram("vecs", [128, NV], F32, "ExternalInput")
    w8f = dram("w8f", [N_W8, 128, 4096], F32, "ExternalInput")
    wdf = dram("wdf", [N_WD, 128, NQ * 128], F32, "ExternalInput")
    adaf = dram("adaf", [24, 128, 4096], F32, "ExternalInput")
    poolwf = dram("poolwf", [128, 2048], F32, "ExternalInput")
    outT = dram("outT", [128, NCH, SEQ], F32, "ExternalOutput")
    w8b = dram("w8b", [N_W8, 128, 4096], BF16, "Internal")
    dgb = dram("dgb", [N_DG, 128, 4096], BF16, "Internal")
    wdb = dram("wdb", [N_WD, 128, NQ * 128], BF16, "Internal")

    es = contextlib.ExitStack()
    with es:
        S = Sched(nc, es)
        sb = lambda n, sh, dt: es.enter_context(nc.sbuf_tensor(n, sh, dt))
        NS8, NSD, NAB, NCG = 3, 3, 4, 2
        POOL_UPD = ()
        vecs = sb("vecs_sb", [128, NV], F32)
        der = sb("der", [128, 80], F32)
        modT = sb("modT", [128, 96], F32)
        cact = sb("cact", [128, 8], BF16)
        ones = sb("ones", [128, 128], BF16)
        mh = sb("mh", [128, T], F32)
        cbuf = sb("cbuf", [128, 4, 16], F32)
        zz = sb("zz", [128, T], F32)
        xt = [sb("xt%d" % i, [128, NCH, T], F32) for i in range(2)]
        t32 = sb("t32", [128, NCH, T], F32)
        sq = sb("sq", [128, NCH * T], BF16)
        hT = sb("hT", [128, NCH, HM + T], BF16)
        ms = sb("ms", [128, T], F32)
        rstdA = sb("rstdA", [128, T], F32)
        rstdB = sb("rstdB", [128, T], F32)
        mean = sb("mean", [128, T], F32)
        m2 = sb("m2", [128, T], F32)
        Sa = sb("Sa", [128, NCH, HM + T], BF16)
        Sb_ = sb("Sb", [128, NCH, HM + T], BF16)
        Dt = sb("Dt", [128, NCH, T], BF16)
        sT = sb("sT", [128, NCH, T], BF16)
        Ab = [sb("Ab%d" % i, [128, 2 + T], BF16) for i in range(NAB)]
        Bb = [sb("Bb%d" % i, [128, 2 + T], BF16) for i in range(NAB)]
        cg = [sb("cg%d" % i, [128, T], BF16) for i in range(NCG)]
        cv = [sb("cv%d" % i, [128, T], BF16) for i in range(NCG)]
        uT = sb("uT", [128, NQ, T], BF16)
        u31 = sb("u31", [128, NCH, CM + T], BF16)
        vh = [sb("vh%d" % i, [128, T], BF16) for i in range(2)]
        th = [sb("th%d" % i, [128, T], BF16) for i in range(2)]
        HA = sb("HA", [128, 2, NA, 2], BF16)
        HB = sb("HB", [128, 2, NA, 2], BF16)
        w8 = [sb("w8_%d" % i, [128, 4096], BF16) for i in range(NS8)]
        wd = [sb("wd_%d" % i, [128, NQ * 128], BF16) for i in range(NSD)]
        poolw = sb("poolw", [128, 2048], BF16)
        psb = [es.enter_context(nc.psum_tensor("ps%d" % i, [128, T], F32)) for i in range(8)]
        NMAIN = 6
        state = {"bank": 0}

        def next_bank():
            b = state["bank"]
            state["bank"] = (b + 1) % NMAIN
            return b

        V = lambda name, a, n=1: vecs[:, _VOFF[name] + a:_VOFF[name] + a + n]
        XK = lambda s: [("x", s, c) for c in range(NCH)]

        S.dma("sp", vecs[:], vecs_d, writes=["vecs"], sem_key="vecs")
        S.dma("sp", xt[0][:], xT[:, :, 0:T], writes=XK(0), sem_key=("x", 0))
        S.op("pool", lambda: nc.gpsimd.memset(ones[:], 1.0 / D), writes=["ones"])
        S.op("pool", lambda: nc.gpsimd.memset(hT[:, :, 0:HM], 0.0), writes=["hm"])
        S.op("pool", lambda: nc.gpsimd.memset(u31[:, :, 0:CM], 0.0), writes=["u31m"])
        S.op("pool", lambda: nc.gpsimd.memset(HA[:], 0.0), writes=["HA0", "HA1"])
        S.op("pool", lambda: nc.gpsimd.memset(HB[:], 0.0), writes=["HB0", "HB1"])
        S.op("act", lambda: nc.scalar.activation(out=cact[:], in_=V("c", 0, 8), func=AF.Silu),
             reads=["vecs"], writes=["cact"])

        SQK = [("sq", c) for c in range(NCH)]
        HK = [("h", c) for c in range(NCH)]
        GS = lambda l_, j, c: der[:, (l_ * 2 + j) * 8 + c:(l_ * 2 + j) * 8 + c + 1]
        GG = lambda l_, j, c: der[:, 32 + (l_ * 2 + j) * 8 + c:32 + (l_ * 2 + j) * 8 + c + 1]
        SH = lambda l_, j, c: modT[:, l_ * 48 + 3 * j * 8 + c:l_ * 48 + 3 * j * 8 + c + 1]

        def mm(out, lhsT, rhs, start, stop, reads, writes, signal=None):
            S.op("pe", lambda: nc.tensor.matmul(out, lhsT, rhs, start=start, stop=stop),
                 reads=reads, writes=writes, signal=(stop if signal is None else signal))

        modps = psb[7]
        for hb in range(24):
            blk, half = hb // 2, hb % 2
            l_, i_ = blk // 6, blk % 6
            s = hb % NS8
            S.dma("pool", w8[s][:, :].rearrange("p (a n) -> p a n", n=2048),
                  adaf[hb].rearrange("p (a n) -> p a n", n=2048),
                  writes=[("w8", s)], sem_key=("w8", s))
            for cc in range(4):
                col = l_ * 48 + i_ * 8 + half * 4 + cc
                for kc in range(8):
                    o = kc * 512 + cc * 128
                    mm(modps[:, col:col + 1], w8[s][:, o:o + 128], cact[:, kc:kc + 1],
                       kc == 0, kc == 7, [("w8", s), "cact"], [("ps", 7)])
        S.op("dve", lambda: nc.vector.tensor_tensor(modT[:], modps[:, 0:96], V("adab", 0, 96), ALU.add),
             reads=[("ps", 7), "vecs"], writes=["modT"])
        for l_ in range(2):
            for j in range(2):
                o = (l_ * 2 + j) * 8
                sc0 = l_ * 48 + (1 + 3 * j) * 8
                gt0 = l_ * 48 + (2 + 3 * j) * 8
                S.op("dve", lambda: nc.vector.scalar_tensor_tensor(
                    der[:, o:o + 8], modT[:, sc0:sc0 + 8], 1.0, V("preg", l_ * 16 + j * 8, 8),
                    ALU.add, ALU.mult), reads=["modT", "vecs"], writes=["der"])
                S.op("dve", lambda: nc.vector.tensor_tensor(
                    der[:, 32 + o:32 + o + 8], modT[:, gt0:gt0 + 8], V("postg", l_ * 16 + j * 8, 8),
                    ALU.mult), reads=["modT", "vecs"], writes=["der"])
        S.op("dve", lambda: nc.vector.tensor_scalar(der[:, 64:80], V("bpw1", 0, 16), 0.5, None, ALU.mult),
             reads=["vecs"], writes=["der"])

        S.dma("pool", poolw[:], poolwf, writes=["poolw"], sem_key="poolw")

        def cast_w8(g):
            S.dma("pool", w8b[g].rearrange("p (a n) -> (p a) n", n=2048),
                  w8f[g].rearrange("p (a n) -> (p a) n", n=2048),
                  writes=[("w8b", g)], sem_key=("cv", g))

        def cast_wd(g):
            S.dma("pool", wdb[g].rearrange("p (a n) -> (p a) n", n=1408),
                  wdf[g].rearrange("p (a n) -> (p a) n", n=1408),
                  writes=[("wdb", g)], sem_key=("cw", g))

        for g in range(11):
            cast_w8(g)
        for g in range(8):
            cast_wd(g)
        deferred_casts = [(cast_w8, g) for g in range(11, 17)] + [(cast_w8, g) for g in range(17, 28)] \
            + [(cast_wd, g) for g in range(8, 16)]

        def diag_builder():
            for idx in range(NCH * 31):
                g = idx // 32
                stg, sk = (Sa, "Sa") if g % 2 == 0 else (Sb_, "Sb")
                keys = [(sk, c) for c in range(NCH)]
                st_c, st_o = (idx % 32) // 4, ((idx % 32) % 4) * 128
                S.op("dve", lambda: nc.vector.tensor_scalar(stg[:, st_c, st_o:st_o + 128], V("ident", 0, 128),
                                                            V("w31", idx), None, ALU.mult),
                     reads=["vecs"], writes=keys)
                if idx % 32 == 31 or idx == NCH * 31 - 1:
                    S.dma("sp", dgb[g].rearrange("p (q n) -> p q n", n=T), stg[:, :, 0:T], reads=keys,
                          writes=[("dgb", g)], sem_key=("dg", g % 2))
                yield

        diag_gen = diag_builder()

        w8_srcs, wd_srcs = [], []
        for i in range(n_tiles):
            for g in range(11):
                w8_srcs.append((w8b[g], ("w8b", g)))
            for g in range(11, 15):
                w8_srcs.append((w8b[g], ("w8b", g)))
            for g in range(N_DG):
                w8_srcs.append((dgb[g], ("dgb", g)))
            for g in range(15, 28):
                w8_srcs.append((w8b[g], ("w8b", g)))
            for g in range(N_WD):
                wd_srcs.append((wdb[g], ("wdb", g)))
        W8 = Stream(S, "w8", w8, w8_srcs, NS8)
        WD = Stream(S, "wd", wd, wd_srcs, NSD)

        def load_x(i):
            s = i % 2
            S.dma("sp", xt[s][:], xT[:, :, i * T:(i + 1) * T], writes=XK(s), sem_key=("x", s))

        def store_x(i):
            s = i % 2
            S.dma("sp", outT[:, :, i * T:(i + 1) * T], xt[s][:], reads=XK(s), sem_key=("o", s))

        def stats_mm(bank, src_ap, c, reads):
            mm(psb[bank][:], ones[:], src_ap, c == 0, c == NCH - 1, ["ones"] + reads, [("ps", bank)])

        I32 = mybir.dt.int32

        NBLK = T // 32

        def newton_rsqrt(src_ap, src_keys, add_eps, dst, dkey):
            S.op("dve", lambda: nc.vector.transpose(mh[:], src_ap), reads=src_keys, writes=["mh"])
            cview = mh[:].rearrange("p (b i) -> p b i", i=32)[:, :, 0:1]
            ca, cy, ch, cz = (cbuf[:, k, 0:NBLK] for k in range(4))
            S.op("dve", lambda: nc.vector.tensor_scalar(ca.unsqueeze(2), cview, EPS if add_eps else 0.0, None, ALU.add),
                 reads=["mh"], writes=["cbuf"])
            S.op("dve", lambda: nc.vector.tensor_scalar(cy.bitcast(I32), ca.bitcast(I32), 1, None, ALU.arith_shift_right),
                 reads=["cbuf"], writes=["cbuf"])
            S.op("dve", lambda: nc.vector.tensor_scalar(cy.bitcast(I32), cy.bitcast(I32), -1.0, 1597463007.0,
                                                        ALU.mult, ALU.add), reads=["cbuf"], writes=["cbuf"])
            S.op("dve", lambda: nc.vector.tensor_scalar(ch, ca, -0.5, None, ALU.mult), reads=["cbuf"], writes=["cbuf"])
            for _ in range(2):
                S.op("dve", lambda: nc.vector.tensor_tensor(cz, cy, cy, ALU.mult), reads=["cbuf"], writes=["cbuf"])
                S.op("dve", lambda: nc.vector.tensor_tensor(cz, cz, ch, ALU.mult), reads=["cbuf"], writes=["cbuf"])
                S.op("dve", lambda: nc.vector.scalar_tensor_tensor(cy, cz, 1.5, cy, ALU.add, ALU.mult),
                     reads=["cbuf"], writes=["cbuf"])
            S.op("dve", lambda: nc.vector.tensor_copy(zz[:].rearrange("p (b i) -> p b i", i=32),
                                                      cy.unsqueeze(2).broadcast_to([128, NBLK, 32])),
                 reads=["cbuf"], writes=["zz"])
            S.op("dve", lambda: nc.vector.transpose(dst[:], zz[:]), reads=["zz"], writes=[dkey])

        def rstd_from(bank, dst, dkey):
            newton_rsqrt(psb[bank][:], [("ps", bank)], True, dst, dkey)

        def pre_norm(s, l_, j):
            for c in range(NCH):
                S.op("act", lambda: nc.scalar.activation(out=sq[:, c * T:(c + 1) * T], in_=xt[s][:, c, :],
                                                         func=AF.Square),
                     reads=[("x", s, c)], writes=[("sq", c)])
                stats_mm(6, sq[:, c * T:(c + 1) * T], c, [("sq", c)])
            rstd_from(6, rstdA, "rstdA")
            for c in range(NCH):
                S.op("dve", lambda: nc.vector.tensor_tensor(t32[:, c, :], xt[s][:, c, :], rstdA[:], ALU.mult),
                     reads=[("x", s, c), "rstdA"], writes=[("t32", c)])
                S.op("act", lambda: nc.scalar.activation(out=hT[:, c, HM:HM + T], in_=t32[:, c, :],
                                                         func=AF.Identity, scale=GS(l_, j, c), bias=SH(l_, j, c)),
                     reads=[("t32", c), "der", "modT"], writes=[("h", c)])

        def post_chunk(s, m, bank, scale, bias):
            S.op("act", lambda: nc.scalar.activation(out=t32[:, m, :], in_=psb[bank][:], func=AF.Identity,
                                                     scale=scale, bias=bias),
                 reads=[("ps", bank), "vecs"], writes=[("t32", m)])
            S.op("act", lambda: nc.scalar.activation(out=sq[:, m * T:(m + 1) * T], in_=t32[:, m, :],
                                                     func=AF.Square),
                 reads=[("t32", m)], writes=[("sq", m)])

        def post_stats(m):
            stats_mm(7, sq[:, m * T:(m + 1) * T], m, [("sq", m)])

        def post_finish(s, l_, j):
            rstd_from(7, rstdB, "rstdB")
            for m in range(NCH):
                if m in POOL_UPD:
                    S.op("pool", lambda: nc.gpsimd.tensor_tensor(t32[:, m, :], t32[:, m, :], rstdB[:], ALU.mult),
                         reads=[("t32", m), "rstdB"], writes=[("t32", m)])
                    S.op("pool", lambda: nc.gpsimd.tensor_scalar(t32[:, m, :], t32[:, m, :], GG(l_, j, m), None, ALU.mult),
                         reads=[("t32", m), "der"], writes=[("t32", m)])
                    S.op("pool", lambda: nc.gpsimd.tensor_tensor(xt[s][:, m, :], xt[s][:, m, :], t32[:, m, :], ALU.add),
                         reads=[("t32", m), ("x", s, m)], writes=[("x", s, m)])
                    continue
                S.op("dve", lambda: nc.vector.tensor_tensor(t32[:, m, :], t32[:, m, :], rstdB[:], ALU.mult),
                     reads=[("t32", m), "rstdB"], writes=[("t32", m)])
                S.op("dve", lambda: nc.vector.scalar_tensor_tensor(xt[s][:, m, :], t32[:, m, :], GG(l_, j, m),
                                                                  xt[s][:, m, :], ALU.mult, ALU.add),
                     reads=[("t32", m), "der", ("x", s, m)], writes=[("x", s, m)])

        def pool_sums_steps(i, s):
            W = HM + T
            SaK = lambda a, b: [("Sa", c) for c in range(a, b)]
            SbK = lambda a, b: [("Sb", c) for c in range(a, b)]
            steps = []
            steps.append(lambda: S.op("dve", lambda: nc.vector.tensor_tensor(
                Sa[:, :, 1:W], hT[:, :, 1:W], hT[:, :, 0:W - 1], ALU.add), reads=HK + ["hm"], writes=SaK(0, 8)))
            steps.append(lambda: S.op("dve", lambda: nc.vector.tensor_tensor(
                Sb_[:, 2:8, 3:W], Sa[:, 2:8, 3:W], Sa[:, 2:8, 1:W - 2], ALU.add), reads=SaK(2, 8), writes=SbK(2, 8)))
            steps.append(lambda: S.op("dve", lambda: nc.vector.tensor_tensor(
                Sa[:, 4:8, 7:W], Sb_[:, 4:8, 7:W], Sb_[:, 4:8, 3:W - 4], ALU.add), reads=SbK(4, 8), writes=SaK(4, 8)))
            steps.append(lambda: S.op("dve", lambda: nc.vector.tensor_tensor(
                Sb_[:, 6:8, 15:W], Sa[:, 6:8, 15:W], Sa[:, 6:8, 7:W - 8], ALU.add), reads=SaK(6, 8), writes=SbK(6, 8)))

            def d_step(g):
                src, sk = (Sa, "Sa") if g in (0, 2) else (Sb_, "Sb")
                c0 = 2 * g
                keys = [(sk, c0), (sk, c0 + 1)]
                if i == 0:
                    S.op("dve", lambda: nc.vector.tensor_tensor(
                        src[:, c0:c0 + 2, HM:HM + 16], src[:, c0:c0 + 2, HM:HM + 16],
                        V("corr", g * 16, 16).unsqueeze(1).broadcast_to([128, 2, 16]), ALU.mult),
                        reads=keys + ["vecs"], writes=keys)
                S.op("dve", lambda: nc.vector.scalar_tensor_tensor(
                    Dt[:, c0:c0 + 2, :], src[:, c0:c0 + 2, HM:HM + T], 1.0 / (2 << g),
                    hT[:, c0:c0 + 2, HM:HM + T], ALU.mult, ALU.subtract),
                    reads=keys + [("h", c0), ("h", c0 + 1)], writes=[("D", c0), ("D", c0 + 1)])

            for g in range(4):
                steps.append(lambda g=g: d_step(g))
            steps.append(lambda: S.op("pool", lambda: nc.gpsimd.tensor_copy(hT[:, :, 0:HM], hT[:, :, T:T + HM]),
                                      reads=HK, writes=["hm"]))
            return steps

        def early_front_steps(i, s):
            steps = []

            def sq_step(c):
                S.op("act", lambda: nc.scalar.activation(out=Dt[:, c, :], in_=xt[s][:, c, :], func=AF.Square),
                     reads=[("x", s, c)], writes=[("D", c)])
                stats_mm(6, Dt[:, c, :], c, [("D", c)])

            for c in range(NCH):
                steps.append(lambda c=c: sq_step(c))
            steps.append(lambda: rstd_from(6, rstdA, "rstdA"))

            def h_step(c):
                S.op("dve", lambda: nc.vector.scalar_tensor_tensor(hT[:, c, HM:HM + T], xt[s][:, c, :], GS(0, 0, c),
                                                                  rstdA[:], ALU.mult, ALU.mult),
                     reads=[("x", s, c), "rstdA", "der"], writes=[("h", c)])
                S.op("act", lambda: nc.scalar.activation(out=hT[:, c, HM:HM + T], in_=hT[:, c, HM:HM + T],
                                                         func=AF.Identity, bias=SH(0, 0, c)),
                     reads=[("h", c), "modT"], writes=[("h", c)])

            for c in range(NCH):
                steps.append(lambda c=c: h_step(c))
            return steps + pool_sums_steps(i, s)

        def pool_front(i, s):
            pre_norm(s, 0, 0)
            for st in pool_sums_steps(i, s):
                st()

        def pool_back(i, s):
            for m in range(NCH):
                g = m // 2
                bank = next_bank()
                for kc in range(2):
                    o = (g * 2 + kc) * 256 + (m % 2) * 128
                    mm(psb[bank][:], poolw[:, o:o + 128], Dt[:, g * 2 + kc, :], kc == 0, kc == 1,
                       ["poolw", ("D", g * 2 + kc)], [("ps", bank)])
                post_chunk(s, m, bank, V("pscale", m), 0.0)
                if m > 0:
                    post_stats(m - 1)
            post_stats(NCH - 1)
            post_finish(s, 0, 0)

        def ffn(i, s, l_, extra_steps=()):
            extra_steps = list(extra_steps)
            pre_norm(s, l_, 1)
            base = i * 36 + (0 if l_ == 0 else 25)
            based = i * 16 + l_ * 8
            WD.get(based)
            info = {}

            def st_pe(n):
                q, isv = n // 2, n % 2
                ach = q + (NQ if isv else 0)
                g, pos = n // 4, n % 4
                r = n % NAB
                if n < 4:
                    if n == 0:
                        slot = W8.get(base)
                        banks = [next_bank() for _ in range(4)]
                        for kc in range(8):
                            for p4 in range(4):
                                o = kc * 512 + p4 * 128
                                mm(psb[banks[p4]][:], w8[slot][:, o:o + 128], hT[:, kc, HM:HM + T], kc == 0, kc == 7,
                                   [("w8", slot), ("h", kc)], [("ps", banks[p4])])
                        info["banks"] = banks
                    bank = info["banks"][n]
                else:
                    slot = W8.get(base + g)
                    bank = next_bank()
                    for kc in range(8):
                        o = kc * 512 + pos * 128
                        mm(psb[bank][:], w8[slot][:, o:o + 128], hT[:, kc, HM:HM + T], kc == 0, kc == 7,
                           [("w8", slot), ("h", kc)], [("ps", bank)])
                info[n] = (q, isv, ach, bank, r)
                S.op("pool", lambda: nc.gpsimd.tensor_copy(Ab[r][:, 0:2], HA[:, l_, ach, :]),
                     reads=[("HA", l_)], writes=[("Ah", r)])
                S.op("pool", lambda: nc.gpsimd.tensor_copy(Bb[r][:, 0:1], HB[:, l_, ach, 0:1]),
                     reads=[("HB", l_)], writes=[("Bh", r)])

            def st_evac(n):
                q, isv, ach, bank, r = info[n]
                w1 = V("wdw", l_ * 132 + ach * 3 + 1)
                S.op("act", lambda: nc.scalar.activation(out=Ab[r][:, 2:2 + T], in_=psb[bank][:], func=AF.Copy),
                     reads=[("ps", bank)], writes=[("A", r), ("At", r)])
                S.op("act", lambda: nc.scalar.activation(out=Bb[r][:, 1:1 + T], in_=psb[bank][:], func=AF.Identity,
                                                         scale=w1),
                     reads=[("ps", bank), "vecs"], writes=[("B", r), ("Bt", r)])
                S.op("pool", lambda: nc.gpsimd.tensor_copy(HA[:, l_, ach, :], Ab[r][:, T:T + 2]),
                     reads=[("At", r)], writes=[("HA", l_)])
                S.op("pool", lambda: nc.gpsimd.tensor_copy(HB[:, l_, ach, 0:1], Bb[r][:, T:T + 1]),
                     reads=[("Bt", r)], writes=[("HB", l_)])

            def st_conv(n):
                q, isv, ach, bank, r = info[n]
                w0 = V("wdw", l_ * 132 + ach * 3 + 0)
                w2 = V("wdw", l_ * 132 + ach * 3 + 2)
                S.op("dve", lambda: nc.vector.scalar_tensor_tensor(Bb[r][:, 0:T], Ab[r][:, 0:T], w0,
                                                                  Bb[r][:, 0:T], ALU.mult, ALU.add),
                     reads=[("Ah", r), ("A", r), ("Bh", r), ("B", r), "vecs"], writes=[("Bh", r), ("B", r)])
                dst, dk = (cv, "cv") if isv else (cg, "cg")
                S.op("dve", lambda: nc.vector.scalar_tensor_tensor(dst[q % NCG][:], Ab[r][:, 2:2 + T], w2,
                                                                  Bb[r][:, 0:T], ALU.mult, ALU.add),
                     reads=[("A", r), ("At", r), ("Bh", r), ("B", r), "vecs"], writes=[(dk, q % NCG)])

            def st_silu(n):
                q, isv, ach, bank, r = info[n]
                if not isv:
                    S.op("act", lambda: nc.scalar.activation(out=cg[q % NCG][:], in_=cg[q % NCG][:], func=AF.Silu),
                         reads=[("cg", q % NCG)], writes=[("cg", q % NCG)])

            def st_mul(n):
                q, isv, ach, bank, r = info[n]
                if isv:
                    S.op("dve", lambda: nc.vector.tensor_tensor(uT[:, q, :], cg[q % NCG][:], cv[q % NCG][:], ALU.mult),
                         reads=[("cg", q % NCG), ("cv", q % NCG)], writes=[("u", q)])

            stages = [st_pe, st_evac, st_conv, st_silu, st_mul]
            for it in range(NA + len(stages) - 1):
                for k, fn in enumerate(stages):
                    n = it - k
                    if 0 <= n < NA:
                        fn(n)
                if deferred_casts:
                    fcast, gcast = deferred_casts.pop(0)
                    fcast(gcast)
                for _ in range(6):
                    next(diag_gen, None)
            while deferred_casts:
                fcast, gcast = deferred_casts.pop(0)
                fcast(gcast)
            for _ in diag_gen:
                pass
            for m in range(NCH):
                slot = WD.get(based + m)
                bank = next_bank()
                for fc in range(NQ):
                    mm(psb[bank][:], wd[slot][:, fc * 128:(fc + 1) * 128], uT[:, fc, :], fc == 0, fc == NQ - 1,
                       [("wd", slot), ("u", fc)], [("ps", bank)])
                post_chunk(s, m, bank, 1.0, 0.0)
                if m > 0:
                    post_stats(m - 1)
                for _ in range(5):
                    if extra_steps:
                        extra_steps.pop(0)()
            while extra_steps:
                extra_steps.pop(0)()
            post_stats(NCH - 1)
            post_finish(s, l_, 1)

        def conformer(i, s):
            pre_norm(s, 1, 0)
            base = i * 36 + 11
            cinfo = {}

            def c_pe(n):
                g, pos = n // 4, n % 4
                if n < 4:
                    if n == 0:
                        slot = W8.get(base)
                        banks = [next_bank() for _ in range(4)]
                        for kc in range(8):
                            for p4 in range(4):
                                o = kc * 512 + p4 * 128
                                mm(psb[banks[p4]][:], w8[slot][:, o:o + 128], hT[:, kc, HM:HM + T], kc == 0, kc == 7,
                                   [("w8", slot), ("h", kc)], [("ps", banks[p4])])
                        cinfo["banks"] = banks
                    cinfo[n] = cinfo["banks"][n]
                    return
                slot = W8.get(base + g)
                bank = next_bank()
                for kc in range(8):
                    o = kc * 512 + pos * 128
                    mm(psb[bank][:], w8[slot][:, o:o + 128], hT[:, kc, HM:HM + T], kc == 0, kc == 7,
                       [("w8", slot), ("h", kc)], [("ps", bank)])
                cinfo[n] = bank

            def c_evac(n):
                j, isg = n // 2, n % 2
                bank = cinfo[n]
                r = j % 2
                if not isg:
                    S.op("act", lambda: nc.scalar.activation(out=vh[r][:], in_=psb[bank][:], func=AF.Identity,
                                                             scale=0.5, bias=der[:, 64 + j:65 + j]),
                         reads=[("ps", bank), "der"], writes=[("vh", r)])
                else:
                    S.op("act", lambda: nc.scalar.activation(out=th[r][:], in_=psb[bank][:], func=AF.Tanh,
                                                             scale=0.5, bias=der[:, 72 + j:73 + j]),
                         reads=[("ps", bank), "der"], writes=[("th", r)])

            def c_glu(n):
                j, isg = n // 2, n % 2
                r = j % 2
                if isg:
                    S.op("dve", lambda: nc.vector.scalar_tensor_tensor(u31[:, j, CM:CM + T], th[r][:], 1.0, vh[r][:],
                                                                      ALU.add, ALU.mult),
                         reads=[("th", r), ("vh", r)], writes=[("u31", j)])

            cstages = [c_pe, c_evac, c_glu]
            for it in range(16 + len(cstages) - 1):
                for k, fn in enumerate(cstages):
                    n = it - k
                    if 0 <= n < 16:
                        fn(n)
            based = i * 36 + 15

            def ln_stats(j):
                stats_mm(6, Dt[:, j, :], j, [("D", j)])
                stats_mm(7, sq[:, j * T:(j + 1) * T], j, [("sq", j)])

            for j in range(NCH):
                bank = next_bank()
                for k in range(31):
                    idx = j * 31 + k
                    slot = W8.get(based + idx // 32)
                    p0 = (idx % 32) * 128
                    mm(psb[bank][:], w8[slot][:, p0:p0 + 128], u31[:, j, k:k + T], k == 0, k == 30,
                       [("w8", slot), ("u31", j), "u31m"], [("ps", bank)])
                S.op("act", lambda: nc.scalar.activation(out=Dt[:, j, :], in_=psb[bank][:], func=AF.Identity,
                                                         bias=V("bdw", j)),
                     reads=[("ps", bank), "vecs"], writes=[("D", j)])
                S.op("act", lambda: nc.scalar.activation(out=sq[:, j * T:(j + 1) * T], in_=Dt[:, j, :], func=AF.Square),
                     reads=[("D", j)], writes=[("sq", j)])
                if j > 0:
                    ln_stats(j - 1)
            ln_stats(NCH - 1)
            S.op("pool", lambda: nc.gpsimd.tensor_copy(u31[:, :, 0:CM], u31[:, :, T:T + CM]),
                 reads=[("u31", j) for j in range(NCH)], writes=["u31m"])
            S.op("dve", lambda: nc.vector.tensor_copy(mean[:], psb[6][:]), reads=[("ps", 6)], writes=["mean"])
            S.op("act", lambda: nc.scalar.activation(out=m2[:], in_=psb[6][:], func=AF.Square),
                 reads=[("ps", 6)], writes=["m2"])
            S.op("dve", lambda: nc.vector.tensor_tensor(ms[:], psb[7][:], m2[:], ALU.subtract),
                 reads=[("ps", 7), "m2"], writes=["ms"])
            S.op("dve", lambda: nc.vector.tensor_scalar(ms[:], ms[:], 0.0, EPS, ALU.max, ALU.add),
                 reads=["ms"], writes=["ms"])
            newton_rsqrt(ms[:], ["ms"], False, rstdA, "rstdA")
            for j in range(NCH):
                S.op("dve", lambda: nc.vector.tensor_tensor(t32[:, j, :], Dt[:, j, :], mean[:], ALU.subtract),
                     reads=[("D", j), "mean"], writes=[("t32", j)])
                S.op("dve", lambda: nc.vector.tensor_tensor(t32[:, j, :], t32[:, j, :], rstdA[:], ALU.mult),
                     reads=[("t32", j), "rstdA"], writes=[("t32", j)])
                S.op("act", lambda: nc.scalar.activation(out=sT[:, j, :], in_=t32[:, j, :], func=AF.Silu,
                                                         scale=V("lng", j), bias=V("lnb", j)),
                     reads=[("t32", j), "vecs"], writes=[("sT", j)])
            for m in range(NCH):
                g, pos = m // 4, m % 4
                slot = W8.get(i * 36 + 23 + g)
                bank = next_bank()
                for kc in range(8):
                    o = kc * 512 + pos * 128
                    mm(psb[bank][:], w8[slot][:, o:o + 128], sT[:, kc, :], kc == 0, kc == 7,
                       [("w8", slot), ("sT", kc)], [("ps", bank)])
                post_chunk(s, m, bank, 1.0, V("bpw2", m))
                if m > 0:
                    post_stats(m - 1)
            post_stats(NCH - 1)
            post_finish(s, 1, 0)

        for i in range(n_tiles):
            s = i % 2
            if i == 0:
                pool_front(i, s)
            pool_back(i, s)
            ffn(i, s, 0)
            conformer(i, s)
            if i + 1 < n_tiles:
                load_x(i + 1)
                ffn(i, s, 1, early_front_steps(i + 1, (i + 1) % 2))
            else:
                ffn(i, s, 1)
            store_x(i)
        for k in [("o", 0), ("o", 1)]:
            if k in S.dma_sems:
                S._wait("sp", ("d_%s" % (k,), S.dma_sems[k], S.dma_cnt[k]))
        S.flush_all()
    return nc, S


def _g8(Wc):
    return np.ascontiguousarray(Wc.reshape(8, 128, 512).transpose(1, 0, 2)).reshape(128, 4096)


def _cols(order):
    return np.concatenate([np.arange(ch * 128, (ch + 1) * 128) for ch in order])


def _fm(v):
    v = np.asarray(v, np.float32)
    return v.reshape(-1, 128).T


def prepare_inputs(inputs):
    f = lambda k: np.asarray(inputs[k], dtype=np.float32)
    x, c = f("x"), f("c")
    ada_w, ada_b = f("ada_w"), f("ada_b")
    ffn_w_up, ffn_w_dw, ffn_w_down = f("ffn_w_up"), f("ffn_w_dw"), f("ffn_w_down")
    w8f = np.empty((N_W8, 128, 4096), np.float32)
    up_order = []
    for q in range(NQ):
        up_order += [q, NQ + q]
    ucols = _cols(up_order)
    for l in range(2):
        Wp = ffn_w_up[l][:, ucols]
        for g in range(11):
            w8f[(0 if l == 0 else 17) + g] = _g8(Wp[:, g * 512:(g + 1) * 512])
    p1_order = []
    for j in range(8):
        p1_order += [j, 8 + j]
    Wp = f("cv_w_pw1")[0][:, _cols(p1_order)]
    for g in range(4):
        w8f[11 + g] = _g8(Wp[:, g * 512:(g + 1) * 512])
    Wp = f("cv_w_pw2")[0]
    for g in range(2):
        w8f[15 + g] = _g8(Wp[:, g * 512:(g + 1) * 512])
    wdf = np.empty((N_WD, 128, NQ * 128), np.float32)
    for l in range(2):
        for m in range(8):
            blk = ffn_w_down[l][:, m * 128:(m + 1) * 128]
            wdf[l * 8 + m] = np.ascontiguousarray(blk.reshape(NQ, 128, 128).transpose(1, 0, 2)).reshape(128, NQ * 128)
    adaf = np.empty((24, 128, 4096), np.float32)
    for hb in range(24):
        blk, half = hb // 2, hb % 2
        l, i = blk // 6, blk % 6
        adaf[hb] = _g8(ada_w[l][:, i * 1024 + half * 512:i * 1024 + half * 512 + 512])
    pw = f("pool_w")[0]
    poolwf = np.ascontiguousarray(pw.reshape(4, 2, 128, 256).transpose(2, 0, 1, 3)).reshape(128, 2048)

    vec_common = np.zeros((128, NV), np.float32)

    def put(name, arr):
        arr = np.asarray(arr, np.float32)
        vec_common[:, _VOFF[name]:_VOFF[name] + arr.shape[1]] = arr

    put("adab", np.concatenate([_fm(ada_b[l]) for l in range(2)], axis=1))
    put("preg", np.concatenate([_fm(f("pre_g")[l, j]) for l in range(2) for j in range(2)], axis=1))
    put("postg", np.concatenate([_fm(f("post_g")[l, j]) for l in range(2) for j in range(2)], axis=1))
    put("pscale", _fm(f("pool_scale")[0]))
    put("bpw1", _fm(f("cv_b_pw1")[0]))
    w31 = f("cv_w_dw")[0]
    put("w31", np.ascontiguousarray(w31.reshape(31, 8, 128).transpose(2, 1, 0)).reshape(128, 248))
    put("bdw", _fm(f("cv_b_dw")[0]))
    put("lng", _fm(f("cv_ln_g")[0]))
    put("lnb", _fm(f("cv_ln_b")[0]))
    put("bpw2", _fm(f("cv_b_pw2")[0]))
    put("wdw", np.concatenate(
        [np.ascontiguousarray(ffn_w_dw[l].reshape(3, NA, 128).transpose(2, 1, 0)).reshape(128, NA * 3)
         for l in range(2)], axis=1))
    put("ident", np.eye(128, dtype=np.float32))
    corr = np.zeros((4, 16), np.float32)
    for g in range(4):
        w = 2 << g
        for t in range(16):
            corr[g, t] = w / min(t + 1, w)
    put("corr", np.broadcast_to(corr.reshape(1, 64), (128, 64)))

    in_maps = []
    for b in range(NB):
        vb = vec_common.copy()
        vb[:, _VOFF["c"]:_VOFF["c"] + 8] = _fm(c[b])
        xTb = np.ascontiguousarray(x[b].T.reshape(NCH, 128, SEQ).transpose(1, 0, 2))
        in_maps.append({"xT": xTb, "vecs": vb, "w8f": w8f, "wdf": wdf, "adaf": adaf, "poolwf": poolwf})
    return in_maps


def kernel(**inputs):
    in_maps = prepare_inputs(inputs)
    nc, _ = build_program()
    res = run_bass_kernel_spmd(nc, in_maps, core_ids=list(range(NB)))
    out = np.empty((NB, SEQ, D), np.float32)
    for b in range(NB):
        o = np.asarray(res.results[b]["outT"], dtype=np.float32)
        out[b] = o.transpose(1, 0, 2).reshape(D, SEQ).T
    return out
```

```python
import contextlib
import numpy as np
import concourse.bass as bass
import concourse.mybir as mybir
from concourse.bass_utils import run_bass_kernel_spmd

F32 = mybir.dt.float32
BF16 = mybir.dt.bfloat16
AF = mybir.ActivationFunctionType
ALU = mybir.AluOpType

D = 1024
SEQ = 8192
NB = 8
FF = 2816
NCH = 8
NA = 44
NQ = 22
EPS = 1e-6
T = 512
NT = SEQ // T
HM = 16
CM = 30

_VOFF = {}
_cur = 0
for _n, _w in [("c", 8), ("adab", 96), ("preg", 32), ("postg", 32), ("pscale", 8),
               ("bpw1", 16), ("w31", 248), ("bdw", 8), ("lng", 8), ("lnb", 8),
               ("bpw2", 8), ("wdw", 264), ("ident", 128), ("corr", 64)]:
    _VOFF[_n] = _cur
    _cur += _w
NV = _cur

N_W8 = 28
N_DG = 8
N_WD = 16


class Sched:
    def __init__(self, nc, es):
        self.nc = nc
        self.es = es
        self.eng = {"pe": nc.tensor, "act": nc.scalar, "dve": nc.vector,
                    "pool": nc.gpsimd, "sp": nc.sync}
        self.sem = {k: es.enter_context(nc.semaphore("sem_" + k)) for k in self.eng}
        self.cnt = {k: 0 for k in self.eng}
        self.waited = {k: {} for k in self.eng}
        self.last_w = {}
        self.readers = {}
        self.dma_sems = {}
        self.dma_cnt = {}
        self.pending = {k: ([], []) for k in self.eng}
        self.last_ins = {k: None for k in self.eng}
        self.n_inst = 0

    def flush(self, e):
        pr, pw = self.pending[e]
        if not pr and not pw:
            return
        self.cnt[e] += 1
        self.last_ins[e].then_inc(self.sem[e], 1)
        self._commit((e, self.sem[e], self.cnt[e]), pr, pw)
        self.pending[e] = ([], [])

    def flush_all(self):
        for e in self.eng:
            self.flush(e)

    def _flush_conflicts(self, e, reads, writes):
        for e2 in self.eng:
            if e2 == e:
                continue
            pr, pw = self.pending[e2]
            if not pr and not pw:
                continue
            hit = any(r in pw for r in reads) or any((w in pw) or (w in pr) for w in writes)
            if hit:
                self.flush(e2)

    def _wait(self, e, ev):
        if ev is None:
            return
        sname, sem, val = ev
        w = self.waited[e]
        if w.get(sname, 0) >= val:
            return
        w[sname] = val
        self.eng[e].wait_ge(sem, val)
        self.n_inst += 1

    def _deps(self, e, reads, writes):
        self._flush_conflicts(e, reads, writes)
        for r in reads:
            self._wait(e, self.last_w.get(r))
        for w in writes:
            self._wait(e, self.last_w.get(w))
            for ev in self.readers.get(w, ()):
                self._wait(e, ev)

    def _commit(self, ev, reads, writes):
        for r in reads:
            self.readers.setdefault(r, []).append(ev)
        for w in writes:
            self.last_w[w] = ev
            self.readers[w] = []

    def op(self, e, fn, reads=(), writes=(), signal=True):
        reads = list(reads)
        writes = list(writes)
        writes += [r for r in reads if isinstance(r, tuple) and r[0] == "ps" and r not in writes]
        self._deps(e, reads, writes)
        ins = fn()
        self.last_ins[e] = ins
        self.n_inst += 1
        pr, pw = self.pending[e]
        if signal:
            self.cnt[e] += 1
            ins.then_inc(self.sem[e], 1)
            ev = (e, self.sem[e], self.cnt[e])
            self._commit(ev, pr + reads, pw + writes)
            self.pending[e] = ([], [])
        else:
            pr.extend(reads)
            pw.extend(writes)
        return ins

    def dma(self, q, out, in_, reads=(), writes=(), sem_key=None, **kw):
        reads = list(reads)
        writes = list(writes)
        self._deps(q, reads, writes)
        if sem_key not in self.dma_sems:
            self.dma_sems[sem_key] = self.es.enter_context(
                self.nc.semaphore("dsem_%d" % len(self.dma_sems)))
            self.dma_cnt[sem_key] = 0
        sem = self.dma_sems[sem_key]
        name = "d_%s" % (sem_key,)
        if self.dma_cnt[sem_key] > 0:
            self._wait(q, (name, sem, self.dma_cnt[sem_key]))
        self.dma_cnt[sem_key] += 16
        ins = self.eng[q].dma_start(out=out, in_=in_, **kw)
        ins.then_inc(sem, 16)
        self.n_inst += 1
        ev = (name, sem, self.dma_cnt[sem_key])
        self._commit(ev, reads, writes)
        return ev


class Stream:
    def __init__(self, S, name, slots, srcs, nslots):
        self.S, self.name, self.slots, self.srcs = S, name, slots, srcs
        self.ns = nslots
        self.issued = 0

    def get(self, n, ahead=None):
        ahead = self.ns - 1 if ahead is None else ahead
        upto = min(len(self.srcs) - 1, n + ahead)
        while self.issued <= upto:
            k = self.issued
            s = k % self.ns
            src_ap, src_key = self.srcs[k]
            self.S.dma("sp", self.slots[s][:], src_ap, reads=[src_key],
                       writes=[(self.name, s)], sem_key=(self.name, s))
            self.issued += 1
        return n % self.ns


def build_program(n_tiles=NT):
    nc = bass.Bass("TRN2", target_bir_lowering=False)
    dram = lambda n, sh, dt, kind: nc.dram_tensor(n, sh, dt, kind=kind).ap()
    xT = dram("xT", [128, NCH, SEQ], F32, "ExternalInput")
    vecs_d = dram("vecs", [128, NV], F32, "ExternalInput")
    w8f = dram("w8f", [N_W8, 128, 4096], F32, "ExternalInput")
    wdf = dram("wdf", [N_WD, 128, NQ * 128], F32, "ExternalInput")
    adaf = dram("adaf", [24, 128, 4096], F32, "ExternalInput")
    poolwf = dram("poolwf", [128, 2048], F32, "ExternalInput")
    outT = dram("outT", [128, NCH, SEQ], F32, "ExternalOutput")
    w8b = dram("w8b", [N_W8, 128, 4096], BF16, "Internal")
    dgb = dram("dgb", [N_DG, 128, 4096], BF16, "Internal")
    wdb = dram("wdb", [N_WD, 128, NQ * 128], BF16, "Internal")

    es = contextlib.ExitStack()
    with es:
        S = Sched(nc, es)
        sb = lambda n, sh, dt: es.enter_context(nc.sbuf_tensor(n, sh, dt))
        NS8, NSD, NAB, NCG = 3, 3, 4, 2
        POOL_UPD = ()
        vecs = sb("vecs_sb", [128, NV], F32)
        der = sb("der", [128, 80], F32)
        modT = sb("modT", [128, 96], F32)
        cact = sb("cact", [128, 8], BF16)
        ones = sb("ones", [128, 128], BF16)
        mh = sb("mh", [128, T], F32)
        cbuf = sb("cbuf", [128, 4, 16], F32)
        zz = sb("zz", [128, T], F32)
        xt = [sb("xt%d" % i, [128, NCH, T], F32) for i in range(2)]
        t32 = sb("t32", [128, NCH, T], F32)
        sq = sb("sq", [128, NCH * T], BF16)
        hT = sb("hT", [128, NCH, HM + T], BF16)
        ms = sb("ms", [128, T], F32)
        rstdA = sb("rstdA", [128, T], F32)
        rstdB = sb("rstdB", [128, T], F32)
        mean = sb("mean", [128, T], F32)
        m2 = sb("m2", [128, T], F32)
        Sa = sb("Sa", [128, NCH, HM + T], BF16)
        Sb_ = sb("Sb", [128, NCH, HM + T], BF16)
        Dt = sb("Dt", [128, NCH, T], BF16)
        sT = sb("sT", [128, NCH, T], BF16)
        Ab = [sb("Ab%d" % i, [128, 2 + T], BF16) for i in range(NAB)]
        Bb = [sb("Bb%d" % i, [128, 2 + T], BF16) for i in range(NAB)]
        cg = [sb("cg%d" % i, [128, T], BF16) for i in range(NCG)]
        cv = [sb("cv%d" % i, [128, T], BF16) for i in range(NCG)]
        uT = sb("uT", [128, NQ, T], BF16)
        u31 = sb("u31", [128, NCH, CM + T], BF16)
        vh = [sb("vh%d" % i, [128, T], BF16) for i in range(2)]
        th = [sb("th%d" % i, [128, T], BF16) for i in range(2)]
        HA = sb("HA", [128, 2, NA, 2], BF16)
        HB = sb("HB", [128, 2, NA, 2], BF16)
        w8 = [sb("w8_%d" % i, [128, 4096], BF16) for i in range(NS8)]
        wd = [sb("wd_%d" % i, [128, NQ * 128], BF16) for i in range(NSD)]
        poolw = sb("poolw", [128, 2048], BF16)
        psb = [es.enter_context(nc.psum_tensor("ps%d" % i, [128, T], F32)) for i in range(8)]
        NMAIN = 5
        state = {"bank": 0}

        def next_bank():
            b = state["bank"]
            state["bank"] = (b + 1) % NMAIN
            return b

        V = lambda name, a, n=1: vecs[:, _VOFF[name] + a:_VOFF[name] + a + n]
        XK = lambda s: [("x", s, c) for c in range(NCH)]

        S.dma("sp", vecs[:], vecs_d, writes=["vecs"], sem_key="vecs")
        S.dma("sp", xt[0][:], xT[:, :, 0:T], writes=XK(0), sem_key=("x", 0))
        S.op("pool", lambda: nc.gpsimd.memset(ones[:], 1.0 / D), writes=["ones"])
        S.op("pool", lambda: nc.gpsimd.memset(hT[:, :, 0:HM], 0.0), writes=["hm"])
        S.op("pool", lambda: nc.gpsimd.memset(u31[:, :, 0:CM], 0.0), writes=["u31m"])
        S.op("pool", lambda: nc.gpsimd.memset(HA[:], 0.0), writes=["HA0", "HA1"])
        S.op("pool", lambda: nc.gpsimd.memset(HB[:], 0.0), writes=["HB0", "HB1"])
        S.op("act", lambda: nc.scalar.activation(out=cact[:], in_=V("c", 0, 8), func=AF.Silu),
             reads=["vecs"], writes=["cact"])

        SQK = [("sq", c) for c in range(NCH)]
        HK = [("h", c) for c in range(NCH)]
        GS = lambda l_, j, c: der[:, (l_ * 2 + j) * 8 + c:(l_ * 2 + j) * 8 + c + 1]
        GG = lambda l_, j, c: der[:, 32 + (l_ * 2 + j) * 8 + c:32 + (l_ * 2 + j) * 8 + c + 1]
        SH = lambda l_, j, c: modT[:, l_ * 48 + 3 * j * 8 + c:l_ * 48 + 3 * j * 8 + c + 1]

        def mm(out, lhsT, rhs, start, stop, reads, writes, signal=None):
            S.op("pe", lambda: nc.tensor.matmul(out, lhsT, rhs, start=start, stop=stop),
                 reads=reads, writes=writes, signal=(stop if signal is None else signal))

        def keep_warm(n):
            for _ in range(n):
                S.op("pe", lambda: nc.tensor.matmul(psb[5][:], ones[:], w8[0][:, 0:T], start=True, stop=True),
                     signal=False)

        modps = psb[7]
        for hb in range(24):
            blk, half = hb // 2, hb % 2
            l_, i_ = blk // 6, blk % 6
            s = hb % NS8
            S.dma("pool", w8[s][:, :].rearrange("p (a n) -> p a n", n=2048),
                  adaf[hb].rearrange("p (a n) -> p a n", n=2048),
                  writes=[("w8", s)], sem_key=("w8", s))
            for cc in range(4):
                col = l_ * 48 + i_ * 8 + half * 4 + cc
                for kc in range(8):
                    o = kc * 512 + cc * 128
                    mm(modps[:, col:col + 1], w8[s][:, o:o + 128], cact[:, kc:kc + 1],
                       kc == 0, kc == 7, [("w8", s), "cact"], [("ps", 7)])
        S.op("dve", lambda: nc.vector.tensor_tensor(modT[:], modps[:, 0:96], V("adab", 0, 96), ALU.add),
             reads=[("ps", 7), "vecs"], writes=["modT"])
        for l_ in range(2):
            for j in range(2):
                o = (l_ * 2 + j) * 8
                sc0 = l_ * 48 + (1 + 3 * j) * 8
                gt0 = l_ * 48 + (2 + 3 * j) * 8
                S.op("dve", lambda: nc.vector.scalar_tensor_tensor(
                    der[:, o:o + 8], modT[:, sc0:sc0 + 8], 1.0, V("preg", l_ * 16 + j * 8, 8),
                    ALU.add, ALU.mult), reads=["modT", "vecs"], writes=["der"])
                S.op("dve", lambda: nc.vector.tensor_tensor(
                    der[:, 32 + o:32 + o + 8], modT[:, gt0:gt0 + 8], V("postg", l_ * 16 + j * 8, 8),
                    ALU.mult), reads=["modT", "vecs"], writes=["der"])
        S.op("dve", lambda: nc.vector.tensor_scalar(der[:, 64:80], V("bpw1", 0, 16), 0.5, None, ALU.mult),
             reads=["vecs"], writes=["der"])

        S.dma("pool", poolw[:], poolwf, writes=["poolw"], sem_key="poolw")

        def cast_w8(g):
            S.dma("pool", w8b[g].rearrange("p (a n) -> (p a) n", n=2048),
                  w8f[g].rearrange("p (a n) -> (p a) n", n=2048),
                  writes=[("w8b", g)], sem_key=("cv", g))

        def cast_wd(g):
            S.dma("pool", wdb[g].rearrange("p (a n) -> (p a) n", n=1408),
                  wdf[g].rearrange("p (a n) -> (p a) n", n=1408),
                  writes=[("wdb", g)], sem_key=("cw", g))

        for g in range(11):
            cast_w8(g)
        for g in range(8):
            cast_wd(g)
        deferred_casts = [(cast_w8, g) for g in range(11, 17)] + [(cast_w8, g) for g in range(17, 28)] \
            + [(cast_wd, g) for g in range(8, 16)]

        def diag_builder():
            for idx in range(NCH * 31):
                g = idx // 32
                stg, sk = (Sa, "Sa") if g % 2 == 0 else (Sb_, "Sb")
                keys = [(sk, c) for c in range(NCH)]
                st_c, st_o = (idx % 32) // 4, ((idx % 32) % 4) * 128
                S.op("dve", lambda: nc.vector.tensor_scalar(stg[:, st_c, st_o:st_o + 128], V("ident", 0, 128),
                                                            V("w31", idx), None, ALU.mult),
                     reads=["vecs"], writes=keys)
                if idx % 32 == 31 or idx == NCH * 31 - 1:
                    S.dma("sp", dgb[g].rearrange("p (q n) -> p q n", n=T), stg[:, :, 0:T], reads=keys,
                          writes=[("dgb", g)], sem_key=("dg", g % 2))
                yield

        diag_gen = diag_builder()

        w8_srcs, wd_srcs = [], []
        for i in range(n_tiles):
            for g in range(11):
                w8_srcs.append((w8b[g], ("w8b", g)))
            for g in range(11, 15):
                w8_srcs.append((w8b[g], ("w8b", g)))
            for g in range(N_DG):
                w8_srcs.append((dgb[g], ("dgb", g)))
            for g in range(15, 28):
                w8_srcs.append((w8b[g], ("w8b", g)))
            for g in range(N_WD):
                wd_srcs.append((wdb[g], ("wdb", g)))
        W8 = Stream(S, "w8", w8, w8_srcs, NS8)
        WD = Stream(S, "wd", wd, wd_srcs, NSD)

        def load_x(i):
            s = i % 2
            S.dma("sp", xt[s][:], xT[:, :, i * T:(i + 1) * T], writes=XK(s), sem_key=("x", s))

        def store_x(i):
            s = i % 2
            S.dma("sp", outT[:, :, i * T:(i + 1) * T], xt[s][:], reads=XK(s), sem_key=("o", s))

        def stats_mm(bank, src_ap, c, reads):
            mm(psb[bank][:], ones[:], src_ap, c == 0, c == NCH - 1, ["ones"] + reads, [("ps", bank)])

        I32 = mybir.dt.int32

        NBLK = T // 32

        def newton_rsqrt(src_ap, src_keys, add_eps, dst, dkey):
            S.op("dve", lambda: nc.vector.transpose(mh[:], src_ap), reads=src_keys, writes=["mh"])
            cview = mh[:].rearrange("p (b i) -> p b i", i=32)[:, :, 0:1]
            ca, cy, ch, cz = (cbuf[:, k, 0:NBLK] for k in range(4))
            S.op("dve", lambda: nc.vector.tensor_scalar(ca.unsqueeze(2), cview, EPS if add_eps else 0.0, None, ALU.add),
                 reads=["mh"], writes=["cbuf"])
            S.op("dve", lambda: nc.vector.tensor_scalar(cy.bitcast(I32), ca.bitcast(I32), 1, None, ALU.arith_shift_right),
                 reads=["cbuf"], writes=["cbuf"])
            S.op("dve", lambda: nc.vector.tensor_scalar(cy.bitcast(I32), cy.bitcast(I32), -1.0, 1597463007.0,
                                                        ALU.mult, ALU.add), reads=["cbuf"], writes=["cbuf"])
            S.op("dve", lambda: nc.vector.tensor_scalar(ch, ca, -0.5, None, ALU.mult), reads=["cbuf"], writes=["cbuf"])
            for _ in range(2):
                S.op("dve", lambda: nc.vector.tensor_tensor(cz, cy, cy, ALU.mult), reads=["cbuf"], writes=["cbuf"])
                S.op("dve", lambda: nc.vector.tensor_tensor(cz, cz, ch, ALU.mult), reads=["cbuf"], writes=["cbuf"])
                S.op("dve", lambda: nc.vector.scalar_tensor_tensor(cy, cz, 1.5, cy, ALU.add, ALU.mult),
                     reads=["cbuf"], writes=["cbuf"])
            S.op("dve", lambda: nc.vector.tensor_copy(zz[:].rearrange("p (b i) -> p b i", i=32),
                                                      cy.unsqueeze(2).broadcast_to([128, NBLK, 32])),
                 reads=["cbuf"], writes=["zz"])
            S.op("dve", lambda: nc.vector.transpose(dst[:], zz[:]), reads=["zz"], writes=[dkey])

        def rstd_from(bank, dst, dkey):
            newton_rsqrt(psb[bank][:], [("ps", bank)], True, dst, dkey)

        def pre_norm(s, l_, j, warm=True):
            for c in range(NCH):
                S.op("act", lambda: nc.scalar.activation(out=sq[:, c * T:(c + 1) * T], in_=xt[s][:, c, :],
                                                         func=AF.Square),
                     reads=[("x", s, c)], writes=[("sq", c)])
                stats_mm(6, sq[:, c * T:(c + 1) * T], c, [("sq", c)])
                if warm:
                    keep_warm(22 if c == NCH - 1 else 4)
            rstd_from(6, rstdA, "rstdA")
            for c in range(NCH):
                S.op("dve", lambda: nc.vector.tensor_tensor(t32[:, c, :], xt[s][:, c, :], rstdA[:], ALU.mult),
                     reads=[("x", s, c), "rstdA"], writes=[("t32", c)])
                S.op("act", lambda: nc.scalar.activation(out=hT[:, c, HM:HM + T], in_=t32[:, c, :],
                                                         func=AF.Identity, scale=GS(l_, j, c), bias=SH(l_, j, c)),
                     reads=[("t32", c), "der", "modT"], writes=[("h", c)])

        def post_chunk(s, m, bank, scale, bias, sq_on_dve=False):
            S.op("act", lambda: nc.scalar.activation(out=t32[:, m, :], in_=psb[bank][:], func=AF.Identity,
                                                     scale=scale, bias=bias),
                 reads=[("ps", bank), "vecs"], writes=[("t32", m)])
            if sq_on_dve:
                S.op("dve", lambda: nc.vector.tensor_tensor(sq[:, m * T:(m + 1) * T], t32[:, m, :], t32[:, m, :],
                                                            ALU.mult),
                     reads=[("t32", m)], writes=[("sq", m)])
            else:
                S.op("act", lambda: nc.scalar.activation(out=sq[:, m * T:(m + 1) * T], in_=t32[:, m, :],
                                                         func=AF.Square),
                     reads=[("t32", m)], writes=[("sq", m)])

        def post_stats(m):
            stats_mm(7, sq[:, m * T:(m + 1) * T], m, [("sq", m)])

        def post_finish(s, l_, j):
            keep_warm(22)
            rstd_from(7, rstdB, "rstdB")
            for m in range(NCH):
                if m in POOL_UPD:
                    S.op("pool", lambda: nc.gpsimd.tensor_tensor(t32[:, m, :], t32[:, m, :], rstdB[:], ALU.mult),
                         reads=[("t32", m), "rstdB"], writes=[("t32", m)])
                    S.op("pool", lambda: nc.gpsimd.tensor_scalar(t32[:, m, :], t32[:, m, :], GG(l_, j, m), None, ALU.mult),
                         reads=[("t32", m), "der"], writes=[("t32", m)])
                    S.op("pool", lambda: nc.gpsimd.tensor_tensor(xt[s][:, m, :], xt[s][:, m, :], t32[:, m, :], ALU.add),
                         reads=[("t32", m), ("x", s, m)], writes=[("x", s, m)])
                    continue
                S.op("dve", lambda: nc.vector.tensor_tensor(t32[:, m, :], t32[:, m, :], rstdB[:], ALU.mult),
                     reads=[("t32", m), "rstdB"], writes=[("t32", m)])
                S.op("dve", lambda: nc.vector.scalar_tensor_tensor(xt[s][:, m, :], t32[:, m, :], GG(l_, j, m),
                                                                  xt[s][:, m, :], ALU.mult, ALU.add),
                     reads=[("t32", m), "der", ("x", s, m)], writes=[("x", s, m)])

        def pool_sums_steps(i, s):
            W = HM + T
            SaK = lambda a, b: [("Sa", c) for c in range(a, b)]
            SbK = lambda a, b: [("Sb", c) for c in range(a, b)]
            steps = []
            steps.append(lambda: S.op("dve", lambda: nc.vector.tensor_tensor(
                Sa[:, :, 1:W], hT[:, :, 1:W], hT[:, :, 0:W - 1], ALU.add), reads=HK + ["hm"], writes=SaK(0, 8)))
            steps.append(lambda: S.op("dve", lambda: nc.vector.tensor_tensor(
                Sb_[:, 2:8, 3:W], Sa[:, 2:8, 3:W], Sa[:, 2:8, 1:W - 2], ALU.add), reads=SaK(2, 8), writes=SbK(2, 8)))
            steps.append(lambda: S.op("dve", lambda: nc.vector.tensor_tensor(
                Sa[:, 4:8, 7:W], Sb_[:, 4:8, 7:W], Sb_[:, 4:8, 3:W - 4], ALU.add), reads=SbK(4, 8), writes=SaK(4, 8)))
            steps.append(lambda: S.op("dve", lambda: nc.vector.tensor_tensor(
                Sb_[:, 6:8, 15:W], Sa[:, 6:8, 15:W], Sa[:, 6:8, 7:W - 8], ALU.add), reads=SaK(6, 8), writes=SbK(6, 8)))

            def d_step(g):
                src, sk = (Sa, "Sa") if g in (0, 2) else (Sb_, "Sb")
                c0 = 2 * g
                keys = [(sk, c0), (sk, c0 + 1)]
                if i == 0:
                    S.op("dve", lambda: nc.vector.tensor_tensor(
                        src[:, c0:c0 + 2, HM:HM + 16], src[:, c0:c0 + 2, HM:HM + 16],
                        V("corr", g * 16, 16).unsqueeze(1).broadcast_to([128, 2, 16]), ALU.mult),
                        reads=keys + ["vecs"], writes=keys)
                S.op("dve", lambda: nc.vector.scalar_tensor_tensor(
                    Dt[:, c0:c0 + 2, :], src[:, c0:c0 + 2, HM:HM + T], 1.0 / (2 << g),
                    hT[:, c0:c0 + 2, HM:HM + T], ALU.mult, ALU.subtract),
                    reads=keys + [("h", c0), ("h", c0 + 1)], writes=[("D", c0), ("D", c0 + 1)])

            for g in range(4):
                steps.append(lambda g=g: d_step(g))
            steps.append(lambda: S.op("pool", lambda: nc.gpsimd.tensor_copy(hT[:, :, 0:HM], hT[:, :, T:T + HM]),
                                      reads=HK, writes=["hm"]))
            return steps

        def early_front_steps(i, s):
            steps = []

            def sq_step(c):
                S.op("act", lambda: nc.scalar.activation(out=Dt[:, c, :], in_=xt[s][:, c, :], func=AF.Square),
                     reads=[("x", s, c)], writes=[("D", c)])
                stats_mm(6, Dt[:, c, :], c, [("D", c)])

            for c in range(NCH):
                steps.append(lambda c=c: sq_step(c))
            steps.append(lambda: rstd_from(6, rstdA, "rstdA"))

            def h_step(c):
                S.op("dve", lambda: nc.vector.scalar_tensor_tensor(hT[:, c, HM:HM + T], xt[s][:, c, :], GS(0, 0, c),
                                                                  rstdA[:], ALU.mult, ALU.mult),
                     reads=[("x", s, c), "rstdA", "der"], writes=[("h", c)])
                S.op("act", lambda: nc.scalar.activation(out=hT[:, c, HM:HM + T], in_=hT[:, c, HM:HM + T],
                                                         func=AF.Identity, bias=SH(0, 0, c)),
                     reads=[("h", c), "modT"], writes=[("h", c)])

            for c in range(NCH):
                steps.append(lambda c=c: h_step(c))
            return steps + pool_sums_steps(i, s)

        def pool_front(i, s):
            pre_norm(s, 0, 0)
            for st in pool_sums_steps(i, s):
                st()

        def pool_back(i, s):
            for m in range(NCH):
                g = m // 2
                bank = next_bank()
                for kc in range(2):
                    o = (g * 2 + kc) * 256 + (m % 2) * 128
                    mm(psb[bank][:], poolw[:, o:o + 128], Dt[:, g * 2 + kc, :], kc == 0, kc == 1,
                       ["poolw", ("D", g * 2 + kc)], [("ps", bank)])
                post_chunk(s, m, bank, V("pscale", m), 0.0, sq_on_dve=True)
                if m > 0:
                    post_stats(m - 1)
            post_stats(NCH - 1)
            post_finish(s, 0, 0)

        def ffn(i, s, l_, extra_steps=()):
            extra_steps = list(extra_steps)
            pre_norm(s, l_, 1)
            base = i * 36 + (0 if l_ == 0 else 25)
            based = i * 16 + l_ * 8
            WD.get(based)
            info = {}

            def st_pe(n):
                q, isv = n // 2, n % 2
                ach = q + (NQ if isv else 0)
                g, pos = n // 4, n % 4
                r = n % NAB
                if n < 4:
                    if n == 0:
                        slot = W8.get(base)
                        banks = [next_bank() for _ in range(4)]
                        for kc in range(8):
                            for p4 in range(4):
                                o = kc * 512 + p4 * 128
                                mm(psb[banks[p4]][:], w8[slot][:, o:o + 128], hT[:, kc, HM:HM + T], kc == 0, kc == 7,
                                   [("w8", slot), ("h", kc)], [("ps", banks[p4])])
                        info["banks"] = banks
                    bank = info["banks"][n]
                else:
                    slot = W8.get(base + g)
                    bank = next_bank()
                    for kc in range(8):
                        o = kc * 512 + pos * 128
                        mm(psb[bank][:], w8[slot][:, o:o + 128], hT[:, kc, HM:HM + T], kc == 0, kc == 7,
                           [("w8", slot), ("h", kc)], [("ps", bank)])
                info[n] = (q, isv, ach, bank, r)
                S.op("pool", lambda: nc.gpsimd.tensor_copy(Ab[r][:, 0:2], HA[:, l_, ach, :]),
                     reads=[("HA", l_)], writes=[("Ah", r)])
                S.op("pool", lambda: nc.gpsimd.tensor_copy(Bb[r][:, 0:1], HB[:, l_, ach, 0:1]),
                     reads=[("HB", l_)], writes=[("Bh", r)])

            def st_evac(n):
                q, isv, ach, bank, r = info[n]
                w1 = V("wdw", l_ * 132 + ach * 3 + 1)
                S.op("act", lambda: nc.scalar.activation(out=Ab[r][:, 2:2 + T], in_=psb[bank][:], func=AF.Copy),
                     reads=[("ps", bank)], writes=[("A", r), ("At", r)])
                S.op("act", lambda: nc.scalar.activation(out=Bb[r][:, 1:1 + T], in_=psb[bank][:], func=AF.Identity,
                                                         scale=w1),
                     reads=[("ps", bank), "vecs"], writes=[("B", r), ("Bt", r)])
                S.op("pool", lambda: nc.gpsimd.tensor_copy(HA[:, l_, ach, :], Ab[r][:, T:T + 2]),
                     reads=[("At", r)], writes=[("HA", l_)])
                S.op("pool", lambda: nc.gpsimd.tensor_copy(HB[:, l_, ach, 0:1], Bb[r][:, T:T + 1]),
                     reads=[("Bt", r)], writes=[("HB", l_)])

            def st_conv(n):
                q, isv, ach, bank, r = info[n]
                w0 = V("wdw", l_ * 132 + ach * 3 + 0)
                w2 = V("wdw", l_ * 132 + ach * 3 + 2)
                S.op("dve", lambda: nc.vector.scalar_tensor_tensor(Bb[r][:, 0:T], Ab[r][:, 0:T], w0,
                                                                  Bb[r][:, 0:T], ALU.mult, ALU.add),
                     reads=[("Ah", r), ("A", r), ("Bh", r), ("B", r), "vecs"], writes=[("Bh", r), ("B", r)])
                dst, dk = (cv, "cv") if isv else (cg, "cg")
                S.op("dve", lambda: nc.vector.scalar_tensor_tensor(dst[q % NCG][:], Ab[r][:, 2:2 + T], w2,
                                                                  Bb[r][:, 0:T], ALU.mult, ALU.add),
                     reads=[("A", r), ("At", r), ("Bh", r), ("B", r), "vecs"], writes=[(dk, q % NCG)])

            def st_silu(n):
                q, isv, ach, bank, r = info[n]
                if not isv:
                    S.op("act", lambda: nc.scalar.activation(out=cg[q % NCG][:], in_=cg[q % NCG][:], func=AF.Silu),
                         reads=[("cg", q % NCG)], writes=[("cg", q % NCG)])

            def st_mul(n):
                q, isv, ach, bank, r = info[n]
                if isv:
                    S.op("dve", lambda: nc.vector.tensor_tensor(uT[:, q, :], cg[q % NCG][:], cv[q % NCG][:], ALU.mult),
                         reads=[("cg", q % NCG), ("cv", q % NCG)], writes=[("u", q)])

            stages = [st_pe, st_evac, st_conv, st_silu, st_mul]
            for it in range(NA + len(stages) - 1):
                for k, fn in enumerate(stages):
                    n = it - k
                    if 0 <= n < NA:
                        fn(n)
                if deferred_casts:
                    fcast, gcast = deferred_casts.pop(0)
                    fcast(gcast)
                for _ in range(6):
                    next(diag_gen, None)
            while deferred_casts:
                fcast, gcast = deferred_casts.pop(0)
                fcast(gcast)
            for _ in diag_gen:
                pass
            for m in range(NCH):
                slot = WD.get(based + m)
                bank = next_bank()
                for fc in range(NQ):
                    mm(psb[bank][:], wd[slot][:, fc * 128:(fc + 1) * 128], uT[:, fc, :], fc == 0, fc == NQ - 1,
                       [("wd", slot), ("u", fc)], [("ps", bank)])
                post_chunk(s, m, bank, 1.0, 0.0)
                if m > 0:
                    post_stats(m - 1)
                for _ in range(5):
                    if extra_steps:
                        extra_steps.pop(0)()
            while extra_steps:
                extra_steps.pop(0)()
            post_stats(NCH - 1)
            post_finish(s, l_, 1)

        def conformer(i, s):
            pre_norm(s, 1, 0)
            base = i * 36 + 11
            cinfo = {}

            def c_pe(n):
                g, pos = n // 4, n % 4
                if n < 4:
                    if n == 0:
                        slot = W8.get(base)
                        banks = [next_bank() for _ in range(4)]
                        for kc in range(8):
                            for p4 in range(4):
                                o = kc * 512 + p4 * 128
                                mm(psb[banks[p4]][:], w8[slot][:, o:o + 128], hT[:, kc, HM:HM + T], kc == 0, kc == 7,
                                   [("w8", slot), ("h", kc)], [("ps", banks[p4])])
                        cinfo["banks"] = banks
                    cinfo[n] = cinfo["banks"][n]
                    return
                slot = W8.get(base + g)
                bank = next_bank()
                for kc in range(8):
                    o = kc * 512 + pos * 128
                    mm(psb[bank][:], w8[slot][:, o:o + 128], hT[:, kc, HM:HM + T], kc == 0, kc == 7,
                       [("w8", slot), ("h", kc)], [("ps", bank)])
                cinfo[n] = bank

            def c_evac(n):
                j, isg = n // 2, n % 2
                bank = cinfo[n]
                r = j % 2
                if not isg:
                    S.op("act", lambda: nc.scalar.activation(out=vh[r][:], in_=psb[bank][:], func=AF.Identity,
                                                             scale=0.5, bias=der[:, 64 + j:65 + j]),
                         reads=[("ps", bank), "der"], writes=[("vh", r)])
                else:
                    S.op("act", lambda: nc.scalar.activation(out=th[r][:], in_=psb[bank][:], func=AF.Tanh,
                                                             scale=0.5, bias=der[:, 72 + j:73 + j]),
                         reads=[("ps", bank), "der"], writes=[("th", r)])

            def c_glu(n):
                j, isg = n // 2, n % 2
                r = j % 2
                if isg:
                    S.op("dve", lambda: nc.vector.scalar_tensor_tensor(u31[:, j, CM:CM + T], th[r][:], 1.0, vh[r][:],
                                                                      ALU.add, ALU.mult),
                         reads=[("th", r), ("vh", r)], writes=[("u31", j)])

            cstages = [c_pe, c_evac, c_glu]
            for it in range(16 + len(cstages) - 1):
                for k, fn in enumerate(cstages):
                    n = it - k
                    if 0 <= n < 16:
                        fn(n)
            based = i * 36 + 15

            def ln_stats(j):
                stats_mm(6, Dt[:, j, :], j, [("D", j)])
                stats_mm(7, sq[:, j * T:(j + 1) * T], j, [("sq", j)])

            for j in range(NCH):
                bank = next_bank()
                for k in range(31):
                    idx = j * 31 + k
                    slot = W8.get(based + idx // 32)
                    p0 = (idx % 32) * 128
                    mm(psb[bank][:], w8[slot][:, p0:p0 + 128], u31[:, j, k:k + T], k == 0, k == 30,
                       [("w8", slot), ("u31", j), "u31m"], [("ps", bank)])
                S.op("act", lambda: nc.scalar.activation(out=Dt[:, j, :], in_=psb[bank][:], func=AF.Identity,
                                                         bias=V("bdw", j)),
                     reads=[("ps", bank), "vecs"], writes=[("D", j)])
                S.op("act", lambda: nc.scalar.activation(out=sq[:, j * T:(j + 1) * T], in_=Dt[:, j, :], func=AF.Square),
                     reads=[("D", j)], writes=[("sq", j)])
                if j > 0:
                    ln_stats(j - 1)
            ln_stats(NCH - 1)
            keep_warm(40)
            S.op("pool", lambda: nc.gpsimd.tensor_copy(u31[:, :, 0:CM], u31[:, :, T:T + CM]),
                 reads=[("u31", j) for j in range(NCH)], writes=["u31m"])
            S.op("dve", lambda: nc.vector.tensor_copy(mean[:], psb[6][:]), reads=[("ps", 6)], writes=["mean"])
            S.op("act", lambda: nc.scalar.activation(out=m2[:], in_=psb[6][:], func=AF.Square),
                 reads=[("ps", 6)], writes=["m2"])
            S.op("dve", lambda: nc.vector.tensor_tensor(ms[:], psb[7][:], m2[:], ALU.subtract),
                 reads=[("ps", 7), "m2"], writes=["ms"])
            S.op("dve", lambda: nc.vector.tensor_scalar(ms[:], ms[:], 0.0, EPS, ALU.max, ALU.add),
                 reads=["ms"], writes=["ms"])
            newton_rsqrt(ms[:], ["ms"], False, rstdA, "rstdA")
            for j in range(NCH):
                S.op("dve", lambda: nc.vector.tensor_tensor(t32[:, j, :], Dt[:, j, :], mean[:], ALU.subtract),
                     reads=[("D", j), "mean"], writes=[("t32", j)])
                S.op("dve", lambda: nc.vector.tensor_tensor(t32[:, j, :], t32[:, j, :], rstdA[:], ALU.mult),
                     reads=[("t32", j), "rstdA"], writes=[("t32", j)])
                S.op("act", lambda: nc.scalar.activation(out=sT[:, j, :], in_=t32[:, j, :], func=AF.Silu,
                                                         scale=V("lng", j), bias=V("lnb", j)),
                     reads=[("t32", j), "vecs"], writes=[("sT", j)])
            slot = W8.get(i * 36 + 23)
            banks0 = [next_bank() for _ in range(4)]
            for kc in range(8):
                for p4 in range(4):
                    o = kc * 512 + p4 * 128
                    mm(psb[banks0[p4]][:], w8[slot][:, o:o + 128], sT[:, kc, :], kc == 0, kc == 7,
                       [("w8", slot), ("sT", kc)], [("ps", banks0[p4])])
            for m in range(NCH):
                g, pos = m // 4, m % 4
                if m < 4:
                    bank = banks0[m]
                else:
                    slot = W8.get(i * 36 + 23 + g)
                    bank = next_bank()
                    for kc in range(8):
                        o = kc * 512 + pos * 128
                        mm(psb[bank][:], w8[slot][:, o:o + 128], sT[:, kc, :], kc == 0, kc == 7,
                           [("w8", slot), ("sT", kc)], [("ps", bank)])
                post_chunk(s, m, bank, 1.0, V("bpw2", m))
                if m > 0:
                    post_stats(m - 1)
            post_stats(NCH - 1)
            post_finish(s, 1, 0)

        for i in range(n_tiles):
            s = i % 2
            if i == 0:
                pool_front(i, s)
            pool_back(i, s)
            ffn(i, s, 0)
            conformer(i, s)
            if i + 1 < n_tiles:
                load_x(i + 1)
                ffn(i, s, 1, early_front_steps(i + 1, (i + 1) % 2))
            else:
                ffn(i, s, 1)
            store_x(i)
        for k in [("o", 0), ("o", 1)]:
            if k in S.dma_sems:
                S._wait("sp", ("d_%s" % (k,), S.dma_sems[k], S.dma_cnt[k]))
        S.flush_all()
    return nc, S


def _g8(Wc):
    return np.ascontiguousarray(Wc.reshape(8, 128, 512).transpose(1, 0, 2)).reshape(128, 4096)


def _cols(order):
    return np.concatenate([np.arange(ch * 128, (ch + 1) * 128) for ch in order])


def _fm(v):
    v = np.asarray(v, np.float32)
    return v.reshape(-1, 128).T


def prepare_inputs(inputs):
    f = lambda k: np.asarray(inputs[k], dtype=np.float32)
    x, c = f("x"), f("c")
    ada_w, ada_b = f("ada_w"), f("ada_b")
    ffn_w_up, ffn_w_dw, ffn_w_down = f("ffn_w_up"), f("ffn_w_dw"), f("ffn_w_down")
    w8f = np.empty((N_W8, 128, 4096), np.float32)
    up_order = []
    for q in range(NQ):
        up_order += [q, NQ + q]
    ucols = _cols(up_order)
    for l in range(2):
        Wp = ffn_w_up[l][:, ucols]
        for g in range(11):
            w8f[(0 if l == 0 else 17) + g] = _g8(Wp[:, g * 512:(g + 1) * 512])
    p1_order = []
    for j in range(8):
        p1_order += [j, 8 + j]
    Wp = f("cv_w_pw1")[0][:, _cols(p1_order)]
    for g in range(4):
        w8f[11 + g] = _g8(Wp[:, g * 512:(g + 1) * 512])
    Wp = f("cv_w_pw2")[0]
    for g in range(2):
        w8f[15 + g] = _g8(Wp[:, g * 512:(g + 1) * 512])
    wdf = np.empty((N_WD, 128, NQ * 128), np.float32)
    for l in range(2):
        for m in range(8):
            blk = ffn_w_down[l][:, m * 128:(m + 1) * 128]
            wdf[l * 8 + m] = np.ascontiguousarray(blk.reshape(NQ, 128, 128).transpose(1, 0, 2)).reshape(128, NQ * 128)
    adaf = np.empty((24, 128, 4096), np.float32)
    for hb in range(24):
        blk, half = hb // 2, hb % 2
        l, i = blk // 6, blk % 6
        adaf[hb] = _g8(ada_w[l][:, i * 1024 + half * 512:i * 1024 + half * 512 + 512])
    pw = f("pool_w")[0]
    poolwf = np.ascontiguousarray(pw.reshape(4, 2, 128, 256).transpose(2, 0, 1, 3)).reshape(128, 2048)

    vec_common = np.zeros((128, NV), np.float32)

    def put(name, arr):
        arr = np.asarray(arr, np.float32)
        vec_common[:, _VOFF[name]:_VOFF[name] + arr.shape[1]] = arr

    put("adab", np.concatenate([_fm(ada_b[l]) for l in range(2)], axis=1))
    put("preg", np.concatenate([_fm(f("pre_g")[l, j]) for l in range(2) for j in range(2)], axis=1))
    put("postg", np.concatenate([_fm(f("post_g")[l, j]) for l in range(2) for j in range(2)], axis=1))
    put("pscale", _fm(f("pool_scale")[0]))
    put("bpw1", _fm(f("cv_b_pw1")[0]))
    w31 = f("cv_w_dw")[0]
    put("w31", np.ascontiguousarray(w31.reshape(31, 8, 128).transpose(2, 1, 0)).reshape(128, 248))
    put("bdw", _fm(f("cv_b_dw")[0]))
    put("lng", _fm(f("cv_ln_g")[0]))
    put("lnb", _fm(f("cv_ln_b")[0]))
    put("bpw2", _fm(f("cv_b_pw2")[0]))
    put("wdw", np.concatenate(
        [np.ascontiguousarray(ffn_w_dw[l].reshape(3, NA, 128).transpose(2, 1, 0)).reshape(128, NA * 3)
         for l in range(2)], axis=1))
    put("ident", np.eye(128, dtype=np.float32))
    corr = np.zeros((4, 16), np.float32)
    for g in range(4):
        w = 2 << g
        for t in range(16):
            corr[g, t] = w / min(t + 1, w)
    put("corr", np.broadcast_to(corr.reshape(1, 64), (128, 64)))

    in_maps = []
    for b in range(NB):
        vb = vec_common.copy()
        vb[:, _VOFF["c"]:_VOFF["c"] + 8] = _fm(c[b])
        xTb = np.ascontiguousarray(x[b].T.reshape(NCH, 128, SEQ).transpose(1, 0, 2))
        in_maps.append({"xT": xTb, "vecs": vb, "w8f": w8f, "wdf": wdf, "adaf": adaf, "poolwf": poolwf})
    return in_maps


def kernel(**inputs):
    in_maps = prepare_inputs(inputs)
    nc, _ = build_program()
    res = run_bass_kernel_spmd(nc, in_maps, core_ids=list(range(NB)))
    out = np.empty((NB, SEQ, D), np.float32)
    for b in range(NB):
        o = np.asarray(res.results[b]["outT"], dtype=np.float32)
        out[b] = o.transpose(1, 0, 2).reshape(D, SEQ).T
    return out
```

```python
import contextlib
import numpy as np
import concourse.bass as bass
import concourse.mybir as mybir
from concourse.bass_utils import run_bass_kernel_spmd

F32 = mybir.dt.float32
BF16 = mybir.dt.bfloat16
AF = mybir.ActivationFunctionType
ALU = mybir.AluOpType

D = 1024
SEQ = 8192
NB = 8
FF = 2816
NCH = 8
NA = 44
NQ = 22
EPS = 1e-6
T = 512
NT = SEQ // T
HM = 16
CM = 30

_VOFF = {}
_cur = 0
for _n, _w in [("c", 8), ("adab", 96), ("preg", 32), ("postg", 32), ("pscale", 8),
               ("bpw1", 16), ("w31", 248), ("bdw", 8), ("lng", 8), ("lnb", 8),
               ("bpw2", 8), ("wdw", 264), ("ident", 128), ("corr", 64)]:
    _VOFF[_n] = _cur
    _cur += _w
NV = _cur

N_W8 = 28
N_DG = 8
N_WD = 16


class Sched:
    def __init__(self, nc, es):
        self.nc = nc
        self.es = es
        self.eng = {"pe": nc.tensor, "act": nc.scalar, "dve": nc.vector,
                    "pool": nc.gpsimd, "sp": nc.sync}
        self.sem = {k: es.enter_context(nc.semaphore("sem_" + k)) for k in self.eng}
        self.cnt = {k: 0 for k in self.eng}
        self.waited = {k: {} for k in self.eng}
        self.last_w = {}
        self.readers = {}
        self.dma_sems = {}
        self.dma_cnt = {}
        self.pending = {k: ([], []) for k in self.eng}
        self.last_ins = {k: None for k in self.eng}
        self.n_inst = 0

    def flush(self, e):
        pr, pw = self.pending[e]
        if not pr and not pw:
            return
        self.cnt[e] += 1
        self.last_ins[e].then_inc(self.sem[e], 1)
        self._commit((e, self.sem[e], self.cnt[e]), pr, pw)
        self.pending[e] = ([], [])

    def flush_all(self):
        for e in self.eng:
            self.flush(e)

    def _flush_conflicts(self, e, reads, writes):
        for e2 in self.eng:
            if e2 == e:
                continue
            pr, pw = self.pending[e2]
            if not pr and not pw:
                continue
            hit = any(r in pw for r in reads) or any((w in pw) or (w in pr) for w in writes)
            if hit:
                self.flush(e2)

    def _wait(self, e, ev):
        if ev is None:
            return
        sname, sem, val = ev
        w = self.waited[e]
        if w.get(sname, 0) >= val:
            return
        w[sname] = val
        self.eng[e].wait_ge(sem, val)
        self.n_inst += 1

    def _deps(self, e, reads, writes):
        self._flush_conflicts(e, reads, writes)
        for r in reads:
            self._wait(e, self.last_w.get(r))
        for w in writes:
            self._wait(e, self.last_w.get(w))
            for ev in self.readers.get(w, ()):
                self._wait(e, ev)

    def _commit(self, ev, reads, writes):
        for r in reads:
            self.readers.setdefault(r, []).append(ev)
        for w in writes:
            self.last_w[w] = ev
            self.readers[w] = []

    def op(self, e, fn, reads=(), writes=(), signal=True):
        reads = list(reads)
        writes = list(writes)
        writes += [r for r in reads if isinstance(r, tuple) and r[0] == "ps" and r not in writes]
        self._deps(e, reads, writes)
        ins = fn()
        self.last_ins[e] = ins
        self.n_inst += 1
        pr, pw = self.pending[e]
        if signal:
            self.cnt[e] += 1
            ins.then_inc(self.sem[e], 1)
            ev = (e, self.sem[e], self.cnt[e])
            self._commit(ev, pr + reads, pw + writes)
            self.pending[e] = ([], [])
        else:
            pr.extend(reads)
            pw.extend(writes)
        return ins

    def dma(self, q, out, in_, reads=(), writes=(), sem_key=None, **kw):
        reads = list(reads)
        writes = list(writes)
        self._deps(q, reads, writes)
        if sem_key not in self.dma_sems:
            self.dma_sems[sem_key] = self.es.enter_context(
                self.nc.semaphore("dsem_%d" % len(self.dma_sems)))
            self.dma_cnt[sem_key] = 0
        sem = self.dma_sems[sem_key]
        name = "d_%s" % (sem_key,)
        if self.dma_cnt[sem_key] > 0:
            self._wait(q, (name, sem, self.dma_cnt[sem_key]))
        self.dma_cnt[sem_key] += 16
        ins = self.eng[q].dma_start(out=out, in_=in_, **kw)
        ins.then_inc(sem, 16)
        self.n_inst += 1
        ev = (name, sem, self.dma_cnt[sem_key])
        self._commit(ev, reads, writes)
        return ev


class Stream:
    def __init__(self, S, name, slots, srcs, nslots):
        self.S, self.name, self.slots, self.srcs = S, name, slots, srcs
        self.ns = nslots
        self.issued = 0

    def get(self, n, ahead=None):
        ahead = self.ns - 1 if ahead is None else ahead
        upto = min(len(self.srcs) - 1, n + ahead)
        while self.issued <= upto:
            k = self.issued
            s = k % self.ns
            src_ap, src_key = self.srcs[k]
            self.S.dma("sp", self.slots[s][:], src_ap, reads=[src_key],
                       writes=[(self.name, s)], sem_key=(self.name, s))
            self.issued += 1
        return n % self.ns


def build_program(n_tiles=NT):
    nc = bass.Bass("TRN2", target_bir_lowering=False)
    dram = lambda n, sh, dt, kind: nc.dram_tensor(n, sh, dt, kind=kind).ap()
    xT = dram("xT", [128, NCH, SEQ], F32, "ExternalInput")
    vecs_d = dram("vecs", [128, NV], F32, "ExternalInput")
    w8f = dram("w8f", [N_W8, 128, 4096], F32, "ExternalInput")
    wdf = dram("wdf", [N_WD, 128, NQ * 128], F32, "ExternalInput")
    adaf = dram("adaf", [24, 128, 4096], F32, "ExternalInput")
    poolwf = dram("poolwf", [128, 2048], F32, "ExternalInput")
    outT = dram("outT", [128, NCH, SEQ], F32, "ExternalOutput")
    w8b = dram("w8b", [N_W8, 128, 4096], BF16, "Internal")
    dgb = dram("dgb", [N_DG, 128, 4096], BF16, "Internal")
    wdb = dram("wdb", [N_WD, 128, NQ * 128], BF16, "Internal")

    es = contextlib.ExitStack()
    with es:
        S = Sched(nc, es)
        sb = lambda n, sh, dt: es.enter_context(nc.sbuf_tensor(n, sh, dt))
        NS8, NSD, NAB, NCG = 3, 3, 4, 2
        POOL_UPD = ()
        vecs = sb("vecs_sb", [128, NV], F32)
        der = sb("der", [128, 80], F32)
        modT = sb("modT", [128, 96], F32)
        cact = sb("cact", [128, 8], BF16)
        ones = sb("ones", [128, 128], BF16)
        mh = sb("mh", [128, T], F32)
        cbuf = sb("cbuf", [128, 4, 16], F32)
        zz = sb("zz", [128, T], F32)
        xt = [sb("xt%d" % i, [128, NCH, T], F32) for i in range(2)]
        t32 = sb("t32", [128, NCH, T], F32)
        sq = sb("sq", [128, NCH * T], BF16)
        hT = sb("hT", [128, NCH, HM + T], BF16)
        ms = sb("ms", [128, T], F32)
        rstdA = sb("rstdA", [128, T], F32)
        rstdB = sb("rstdB", [128, T], F32)
        mean = sb("mean", [128, T], F32)
        m2 = sb("m2", [128, T], F32)
        Sa = sb("Sa", [128, NCH, HM + T], BF16)
        Sb_ = sb("Sb", [128, NCH, HM + T], BF16)
        Dt = sb("Dt", [128, NCH, T], BF16)
        sT = sb("sT", [128, NCH, T], BF16)
        Ab = [sb("Ab%d" % i, [128, 2 + T], BF16) for i in range(NAB)]
        Bb = [sb("Bb%d" % i, [128, 2 + T], BF16) for i in range(NAB)]
        cg = [sb("cg%d" % i, [128, T], BF16) for i in range(NCG)]
        cv = [sb("cv%d" % i, [128, T], BF16) for i in range(NCG)]
        uT = sb("uT", [128, NQ, T], BF16)
        u31 = sb("u31", [128, NCH, CM + T], BF16)
        vh = [sb("vh%d" % i, [128, T], BF16) for i in range(2)]
        th = [sb("th%d" % i, [128, T], BF16) for i in range(2)]
        HA = sb("HA", [128, 2, NA, 2], BF16)
        HB = sb("HB", [128, 2, NA, 2], BF16)
        w8 = [sb("w8_%d" % i, [128, 4096], BF16) for i in range(NS8)]
        wd = [sb("wd_%d" % i, [128, NQ * 128], BF16) for i in range(NSD)]
        poolw = sb("poolw", [128, 2048], BF16)
        psb = [es.enter_context(nc.psum_tensor("ps%d" % i, [128, T], F32)) for i in range(8)]
        NMAIN = 5
        state = {"bank": 0}

        def next_bank():
            b = state["bank"]
            state["bank"] = (b + 1) % NMAIN
            return b

        V = lambda name, a, n=1: vecs[:, _VOFF[name] + a:_VOFF[name] + a + n]
        XK = lambda s: [("x", s, c) for c in range(NCH)]

        S.dma("sp", vecs[:], vecs_d, writes=["vecs"], sem_key="vecs")
        S.dma("sp", xt[0][:], xT[:, :, 0:T], writes=XK(0), sem_key=("x", 0))
        S.op("pool", lambda: nc.gpsimd.memset(ones[:], 1.0 / D), writes=["ones"])
        S.op("pool", lambda: nc.gpsimd.memset(hT[:, :, 0:HM], 0.0), writes=["hm"])
        S.op("pool", lambda: nc.gpsimd.memset(u31[:, :, 0:CM], 0.0), writes=["u31m"])
        S.op("pool", lambda: nc.gpsimd.memset(HA[:], 0.0), writes=["HA0", "HA1"])
        S.op("pool", lambda: nc.gpsimd.memset(HB[:], 0.0), writes=["HB0", "HB1"])
        S.op("act", lambda: nc.scalar.activation(out=cact[:], in_=V("c", 0, 8), func=AF.Silu),
             reads=["vecs"], writes=["cact"])

        SQK = [("sq", c) for c in range(NCH)]
        HK = [("h", c) for c in range(NCH)]
        GS = lambda l_, j, c: der[:, (l_ * 2 + j) * 8 + c:(l_ * 2 + j) * 8 + c + 1]
        GG = lambda l_, j, c: der[:, 32 + (l_ * 2 + j) * 8 + c:32 + (l_ * 2 + j) * 8 + c + 1]
        SH = lambda l_, j, c: modT[:, l_ * 48 + 3 * j * 8 + c:l_ * 48 + 3 * j * 8 + c + 1]

        def mm(out, lhsT, rhs, start, stop, reads, writes, signal=None):
            S.op("pe", lambda: nc.tensor.matmul(out, lhsT, rhs, start=start, stop=stop),
                 reads=reads, writes=writes, signal=(stop if signal is None else signal))

        def keep_warm(n):
            for _ in range(n):
                S.op("pe", lambda: nc.tensor.matmul(psb[5][:], ones[:], w8[0][:, 0:T], start=True, stop=True),
                     signal=False)

        modps = psb[7]
        for hb in range(24):
            blk, half = hb // 2, hb % 2
            l_, i_ = blk // 6, blk % 6
            s = hb % NS8
            S.dma("pool", w8[s][:, :].rearrange("p (a n) -> p a n", n=2048),
                  adaf[hb].rearrange("p (a n) -> p a n", n=2048),
                  writes=[("w8", s)], sem_key=("w8", s))
            for cc in range(4):
                col = l_ * 48 + i_ * 8 + half * 4 + cc
                for kc in range(8):
                    o = kc * 512 + cc * 128
                    mm(modps[:, col:col + 1], w8[s][:, o:o + 128], cact[:, kc:kc + 1],
                       kc == 0, kc == 7, [("w8", s), "cact"], [("ps", 7)])
        S.op("dve", lambda: nc.vector.tensor_tensor(modT[:], modps[:, 0:96], V("adab", 0, 96), ALU.add),
             reads=[("ps", 7), "vecs"], writes=["modT"])
        for l_ in range(2):
            for j in range(2):
                o = (l_ * 2 + j) * 8
                sc0 = l_ * 48 + (1 + 3 * j) * 8
                gt0 = l_ * 48 + (2 + 3 * j) * 8
                S.op("dve", lambda: nc.vector.scalar_tensor_tensor(
                    der[:, o:o + 8], modT[:, sc0:sc0 + 8], 1.0, V("preg", l_ * 16 + j * 8, 8),
                    ALU.add, ALU.mult), reads=["modT", "vecs"], writes=["der"])
                S.op("dve", lambda: nc.vector.tensor_tensor(
                    der[:, 32 + o:32 + o + 8], modT[:, gt0:gt0 + 8], V("postg", l_ * 16 + j * 8, 8),
                    ALU.mult), reads=["modT", "vecs"], writes=["der"])
        S.op("dve", lambda: nc.vector.tensor_scalar(der[:, 64:80], V("bpw1", 0, 16), 0.5, None, ALU.mult),
             reads=["vecs"], writes=["der"])

        S.dma("pool", poolw[:], poolwf, writes=["poolw"], sem_key="poolw")

        def cast_w8(g):
            S.dma("pool", w8b[g].rearrange("p (a n) -> (p a) n", n=2048),
                  w8f[g].rearrange("p (a n) -> (p a) n", n=2048),
                  writes=[("w8b", g)], sem_key=("cv", g))

        def cast_wd(g):
            S.dma("pool", wdb[g].rearrange("p (a n) -> (p a) n", n=1408),
                  wdf[g].rearrange("p (a n) -> (p a) n", n=1408),
                  writes=[("wdb", g)], sem_key=("cw", g))

        for g in range(11):
            cast_w8(g)
        for g in range(8):
            cast_wd(g)
        deferred_casts = [(cast_w8, g) for g in range(11, 17)] + [(cast_w8, g) for g in range(17, 28)] \
            + [(cast_wd, g) for g in range(8, 16)]

        def diag_builder():
            for idx in range(NCH * 31):
                g = idx // 32
                stg, sk = (Sa, "Sa") if g % 2 == 0 else (Sb_, "Sb")
                keys = [(sk, c) for c in range(NCH)]
                st_c, st_o = (idx % 32) // 4, ((idx % 32) % 4) * 128
                S.op("dve", lambda: nc.vector.tensor_scalar(stg[:, st_c, st_o:st_o + 128], V("ident", 0, 128),
                                                            V("w31", idx), None, ALU.mult),
                     reads=["vecs"], writes=keys)
                if idx % 32 == 31 or idx == NCH * 31 - 1:
                    S.dma("sp", dgb[g].rearrange("p (q n) -> p q n", n=T), stg[:, :, 0:T], reads=keys,
                          writes=[("dgb", g)], sem_key=("dg", g % 2))
                yield

        diag_gen = diag_builder()

        w8_srcs, wd_srcs = [], []
        for i in range(n_tiles):
            for g in range(11):
                w8_srcs.append((w8b[g], ("w8b", g)))
            for g in range(11, 15):
                w8_srcs.append((w8b[g], ("w8b", g)))
            for g in range(N_DG):
                w8_srcs.append((dgb[g], ("dgb", g)))
            for g in range(15, 28):
                w8_srcs.append((w8b[g], ("w8b", g)))
            for g in range(N_WD):
                wd_srcs.append((wdb[g], ("wdb", g)))
        W8 = Stream(S, "w8", w8, w8_srcs, NS8)
        WD = Stream(S, "wd", wd, wd_srcs, NSD)

        def load_x(i):
            s = i % 2
            S.dma("sp", xt[s][:], xT[:, :, i * T:(i + 1) * T], writes=XK(s), sem_key=("x", s))

        def store_x(i):
            s = i % 2
            S.dma("sp", outT[:, :, i * T:(i + 1) * T], xt[s][:], reads=XK(s), sem_key=("o", s))

        def stats_mm(bank, src_ap, c, reads):
            mm(psb[bank][:], ones[:], src_ap, c == 0, c == NCH - 1, ["ones"] + reads, [("ps", bank)])

        I32 = mybir.dt.int32

        NBLK = T // 32

        def newton_rsqrt(src_ap, src_keys, add_eps, dst, dkey):
            S.op("dve", lambda: nc.vector.transpose(mh[:], src_ap), reads=src_keys, writes=["mh"])
            cview = mh[:].rearrange("p (b i) -> p b i", i=32)[:, :, 0:1]
            ca, cy, ch, cz = (cbuf[:, k, 0:NBLK] for k in range(4))
            S.op("dve", lambda: nc.vector.tensor_scalar(ca.unsqueeze(2), cview, EPS if add_eps else 0.0, None, ALU.add),
                 reads=["mh"], writes=["cbuf"])
            S.op("dve", lambda: nc.vector.tensor_scalar(cy.bitcast(I32), ca.bitcast(I32), 1, None, ALU.arith_shift_right),
                 reads=["cbuf"], writes=["cbuf"])
            S.op("dve", lambda: nc.vector.tensor_scalar(cy.bitcast(I32), cy.bitcast(I32), -1.0, 1597463007.0,
                                                        ALU.mult, ALU.add), reads=["cbuf"], writes=["cbuf"])
            S.op("dve", lambda: nc.vector.tensor_scalar(ch, ca, -0.5, None, ALU.mult), reads=["cbuf"], writes=["cbuf"])
            for _ in range(2):
                S.op("dve", lambda: nc.vector.tensor_tensor(cz, cy, cy, ALU.mult), reads=["cbuf"], writes=["cbuf"])
                S.op("dve", lambda: nc.vector.tensor_tensor(cz, cz, ch, ALU.mult), reads=["cbuf"], writes=["cbuf"])
                S.op("dve", lambda: nc.vector.scalar_tensor_tensor(cy, cz, 1.5, cy, ALU.add, ALU.mult),
                     reads=["cbuf"], writes=["cbuf"])
            S.op("dve", lambda: nc.vector.tensor_copy(zz[:].rearrange("p (b i) -> p b i", i=32),
                                                      cy.unsqueeze(2).broadcast_to([128, NBLK, 32])),
                 reads=["cbuf"], writes=["zz"])
            S.op("dve", lambda: nc.vector.transpose(dst[:], zz[:]), reads=["zz"], writes=[dkey])

        def rstd_from(bank, dst, dkey):
            newton_rsqrt(psb[bank][:], [("ps", bank)], True, dst, dkey)

        def pre_norm(s, l_, j, warm=True):
            for c in range(NCH):
                S.op("act", lambda: nc.scalar.activation(out=sq[:, c * T:(c + 1) * T], in_=xt[s][:, c, :],
                                                         func=AF.Square),
                     reads=[("x", s, c)], writes=[("sq", c)])
                stats_mm(6, sq[:, c * T:(c + 1) * T], c, [("sq", c)])
                if warm:
                    keep_warm(22 if c == NCH - 1 else 4)
            rstd_from(6, rstdA, "rstdA")
            for c in range(NCH):
                S.op("dve", lambda: nc.vector.tensor_tensor(t32[:, c, :], xt[s][:, c, :], rstdA[:], ALU.mult),
                     reads=[("x", s, c), "rstdA"], writes=[("t32", c)])
                S.op("act", lambda: nc.scalar.activation(out=hT[:, c, HM:HM + T], in_=t32[:, c, :],
                                                         func=AF.Identity, scale=GS(l_, j, c), bias=SH(l_, j, c)),
                     reads=[("t32", c), "der", "modT"], writes=[("h", c)])

        def post_chunk(s, m, bank, scale, bias, sq_on_dve=False):
            S.op("act", lambda: nc.scalar.activation(out=t32[:, m, :], in_=psb[bank][:], func=AF.Identity,
                                                     scale=scale, bias=bias),
                 reads=[("ps", bank), "vecs"], writes=[("t32", m)])
            if sq_on_dve:
                S.op("dve", lambda: nc.vector.tensor_tensor(sq[:, m * T:(m + 1) * T], t32[:, m, :], t32[:, m, :],
                                                            ALU.mult),
                     reads=[("t32", m)], writes=[("sq", m)])
            else:
                S.op("act", lambda: nc.scalar.activation(out=sq[:, m * T:(m + 1) * T], in_=t32[:, m, :],
                                                         func=AF.Square),
                     reads=[("t32", m)], writes=[("sq", m)])

        def post_stats(m):
            stats_mm(7, sq[:, m * T:(m + 1) * T], m, [("sq", m)])

        def post_finish(s, l_, j):
            keep_warm(22)
            rstd_from(7, rstdB, "rstdB")
            def upd_mul(m):
                S.op("dve", lambda: nc.vector.tensor_tensor(t32[:, m, :], t32[:, m, :], rstdB[:], ALU.mult),
                     reads=[("t32", m), "rstdB"], writes=[("t32", m)])

            def upd_add(m):
                S.op("dve", lambda: nc.vector.scalar_tensor_tensor(xt[s][:, m, :], t32[:, m, :], GG(l_, j, m),
                                                                  xt[s][:, m, :], ALU.mult, ALU.add),
                     reads=[("t32", m), "der", ("x", s, m)], writes=[("x", s, m)])

            upd_mul(0)
            for m in range(1, NCH):
                upd_mul(m)
                upd_add(m - 1)
            upd_add(NCH - 1)

        def pool_sums_steps(i, s):
            W = HM + T
            SaK = lambda a, b: [("Sa", c) for c in range(a, b)]
            SbK = lambda a, b: [("Sb", c) for c in range(a, b)]
            steps = []
            steps.append(lambda: S.op("dve", lambda: nc.vector.tensor_tensor(
                Sa[:, :, 1:W], hT[:, :, 1:W], hT[:, :, 0:W - 1], ALU.add), reads=HK + ["hm"], writes=SaK(0, 8)))
            steps.append(lambda: S.op("dve", lambda: nc.vector.tensor_tensor(
                Sb_[:, 2:8, 3:W], Sa[:, 2:8, 3:W], Sa[:, 2:8, 1:W - 2], ALU.add), reads=SaK(2, 8), writes=SbK(2, 8)))
            steps.append(lambda: S.op("dve", lambda: nc.vector.tensor_tensor(
                Sa[:, 4:8, 7:W], Sb_[:, 4:8, 7:W], Sb_[:, 4:8, 3:W - 4], ALU.add), reads=SbK(4, 8), writes=SaK(4, 8)))
            steps.append(lambda: S.op("dve", lambda: nc.vector.tensor_tensor(
                Sb_[:, 6:8, 15:W], Sa[:, 6:8, 15:W], Sa[:, 6:8, 7:W - 8], ALU.add), reads=SaK(6, 8), writes=SbK(6, 8)))

            def d_step(g):
                src, sk = (Sa, "Sa") if g in (0, 2) else (Sb_, "Sb")
                c0 = 2 * g
                keys = [(sk, c0), (sk, c0 + 1)]
                if i == 0:
                    S.op("dve", lambda: nc.vector.tensor_tensor(
                        src[:, c0:c0 + 2, HM:HM + 16], src[:, c0:c0 + 2, HM:HM + 16],
                        V("corr", g * 16, 16).unsqueeze(1).broadcast_to([128, 2, 16]), ALU.mult),
                        reads=keys + ["vecs"], writes=keys)
                S.op("dve", lambda: nc.vector.scalar_tensor_tensor(
                    Dt[:, c0:c0 + 2, :], src[:, c0:c0 + 2, HM:HM + T], 1.0 / (2 << g),
                    hT[:, c0:c0 + 2, HM:HM + T], ALU.mult, ALU.subtract),
                    reads=keys + [("h", c0), ("h", c0 + 1)], writes=[("D", c0), ("D", c0 + 1)])

            for g in range(4):
                steps.append(lambda g=g: d_step(g))
            steps.append(lambda: S.op("pool", lambda: nc.gpsimd.tensor_copy(hT[:, :, 0:HM], hT[:, :, T:T + HM]),
                                      reads=HK, writes=["hm"]))
            return steps

        def early_front_steps(i, s):
            steps = []

            def sq_step(c):
                S.op("act", lambda: nc.scalar.activation(out=Dt[:, c, :], in_=xt[s][:, c, :], func=AF.Square),
                     reads=[("x", s, c)], writes=[("D", c)])
                stats_mm(6, Dt[:, c, :], c, [("D", c)])

            for c in range(NCH):
                steps.append(lambda c=c: sq_step(c))
            steps.append(lambda: rstd_from(6, rstdA, "rstdA"))

            def h_step(c):
                S.op("dve", lambda: nc.vector.scalar_tensor_tensor(hT[:, c, HM:HM + T], xt[s][:, c, :], GS(0, 0, c),
                                                                  rstdA[:], ALU.mult, ALU.mult),
                     reads=[("x", s, c), "rstdA", "der"], writes=[("h", c)])
                S.op("act", lambda: nc.scalar.activation(out=hT[:, c, HM:HM + T], in_=hT[:, c, HM:HM + T],
                                                         func=AF.Identity, bias=SH(0, 0, c)),
                     reads=[("h", c), "modT"], writes=[("h", c)])

            for c in range(NCH):
                steps.append(lambda c=c: h_step(c))
            return steps + pool_sums_steps(i, s)

        def pool_front(i, s):
            pre_norm(s, 0, 0)
            for st in pool_sums_steps(i, s):
                st()

        def pool_back(i, s):
            for m in range(NCH):
                g = m // 2
                bank = next_bank()
                for kc in range(2):
                    o = (g * 2 + kc) * 256 + (m % 2) * 128
                    mm(psb[bank][:], poolw[:, o:o + 128], Dt[:, g * 2 + kc, :], kc == 0, kc == 1,
                       ["poolw", ("D", g * 2 + kc)], [("ps", bank)])
                post_chunk(s, m, bank, V("pscale", m), 0.0, sq_on_dve=True)
                if m > 0:
                    post_stats(m - 1)
            post_stats(NCH - 1)
            post_finish(s, 0, 0)

        def ffn(i, s, l_, extra_steps=()):
            extra_steps = list(extra_steps)
            pre_norm(s, l_, 1)
            base = i * 36 + (0 if l_ == 0 else 25)
            based = i * 16 + l_ * 8
            WD.get(based)
            info = {}

            def st_pe(n):
                q, isv = n // 2, n % 2
                ach = q + (NQ if isv else 0)
                g, pos = n // 4, n % 4
                r = n % NAB
                if n < 4:
                    if n == 0:
                        slot = W8.get(base)
                        banks = [next_bank() for _ in range(4)]
                        for kc in range(8):
                            for p4 in range(4):
                                o = kc * 512 + p4 * 128
                                mm(psb[banks[p4]][:], w8[slot][:, o:o + 128], hT[:, kc, HM:HM + T], kc == 0, kc == 7,
                                   [("w8", slot), ("h", kc)], [("ps", banks[p4])])
                        info["banks"] = banks
                    bank = info["banks"][n]
                else:
                    slot = W8.get(base + g)
                    bank = next_bank()
                    for kc in range(8):
                        o = kc * 512 + pos * 128
                        mm(psb[bank][:], w8[slot][:, o:o + 128], hT[:, kc, HM:HM + T], kc == 0, kc == 7,
                           [("w8", slot), ("h", kc)], [("ps", bank)])
                info[n] = (q, isv, ach, bank, r)
                S.op("pool", lambda: nc.gpsimd.tensor_copy(Ab[r][:, 0:2], HA[:, l_, ach, :]),
                     reads=[("HA", l_)], writes=[("Ah", r)])
                S.op("pool", lambda: nc.gpsimd.tensor_copy(Bb[r][:, 0:1], HB[:, l_, ach, 0:1]),
                     reads=[("HB", l_)], writes=[("Bh", r)])

            def st_evac(n):
                q, isv, ach, bank, r = info[n]
                w1 = V("wdw", l_ * 132 + ach * 3 + 1)
                S.op("act", lambda: nc.scalar.activation(out=Ab[r][:, 2:2 + T], in_=psb[bank][:], func=AF.Copy),
                     reads=[("ps", bank)], writes=[("A", r), ("At", r)])
                S.op("act", lambda: nc.scalar.activation(out=Bb[r][:, 1:1 + T], in_=psb[bank][:], func=AF.Identity,
                                                         scale=w1),
                     reads=[("ps", bank), "vecs"], writes=[("B", r), ("Bt", r)])
                S.op("pool", lambda: nc.gpsimd.tensor_copy(HA[:, l_, ach, :], Ab[r][:, T:T + 2]),
                     reads=[("At", r)], writes=[("HA", l_)])
                S.op("pool", lambda: nc.gpsimd.tensor_copy(HB[:, l_, ach, 0:1], Bb[r][:, T:T + 1]),
                     reads=[("Bt", r)], writes=[("HB", l_)])

            def st_conv1(n):
                q, isv, ach, bank, r = info[n]
                w0 = V("wdw", l_ * 132 + ach * 3 + 0)
                S.op("dve", lambda: nc.vector.scalar_tensor_tensor(Bb[r][:, 0:T], Ab[r][:, 0:T], w0,
                                                                  Bb[r][:, 0:T], ALU.mult, ALU.add),
                     reads=[("Ah", r), ("A", r), ("Bh", r), ("B", r), "vecs"], writes=[("Bh", r), ("B", r)])

            def st_conv2(n):
                q, isv, ach, bank, r = info[n]
                w2 = V("wdw", l_ * 132 + ach * 3 + 2)
                dst, dk = (cv, "cv") if isv else (cg, "cg")
                S.op("dve", lambda: nc.vector.scalar_tensor_tensor(dst[q % NCG][:], Ab[r][:, 2:2 + T], w2,
                                                                  Bb[r][:, 0:T], ALU.mult, ALU.add),
                     reads=[("A", r), ("At", r), ("Bh", r), ("B", r), "vecs"], writes=[(dk, q % NCG)])

            def st_silu(n):
                q, isv, ach, bank, r = info[n]
                if not isv:
                    S.op("act", lambda: nc.scalar.activation(out=cg[q % NCG][:], in_=cg[q % NCG][:], func=AF.Silu),
                         reads=[("cg", q % NCG)], writes=[("cg", q % NCG)])

            def st_mul(n):
                q, isv, ach, bank, r = info[n]
                if isv:
                    S.op("dve", lambda: nc.vector.tensor_tensor(uT[:, q, :], cg[q % NCG][:], cv[q % NCG][:], ALU.mult),
                         reads=[("cg", q % NCG), ("cv", q % NCG)], writes=[("u", q)])

            stages = [st_pe, st_evac, st_conv1, st_conv2, st_silu, st_mul]
            for it in range(NA + len(stages) - 1):
                for k, fn in enumerate(stages):
                    n = it - k
                    if 0 <= n < NA:
                        fn(n)
                if deferred_casts:
                    fcast, gcast = deferred_casts.pop(0)
                    fcast(gcast)
                for _ in range(6):
                    next(diag_gen, None)
            while deferred_casts:
                fcast, gcast = deferred_casts.pop(0)
                fcast(gcast)
            for _ in diag_gen:
                pass
            for m in range(NCH):
                slot = WD.get(based + m)
                bank = next_bank()
                for fc in range(NQ):
                    mm(psb[bank][:], wd[slot][:, fc * 128:(fc + 1) * 128], uT[:, fc, :], fc == 0, fc == NQ - 1,
                       [("wd", slot), ("u", fc)], [("ps", bank)])
                post_chunk(s, m, bank, 1.0, 0.0)
                if m > 0:
                    post_stats(m - 1)
                for _ in range(5):
                    if extra_steps:
                        extra_steps.pop(0)()
            while extra_steps:
                extra_steps.pop(0)()
            post_stats(NCH - 1)
            post_finish(s, l_, 1)

        def conformer(i, s):
            pre_norm(s, 1, 0)
            base = i * 36 + 11
            cinfo = {}

            def c_pe(n):
                g, pos = n // 4, n % 4
                if n < 4:
                    if n == 0:
                        slot = W8.get(base)
                        banks = [next_bank() for _ in range(4)]
                        for kc in range(8):
                            for p4 in range(4):
                                o = kc * 512 + p4 * 128
                                mm(psb[banks[p4]][:], w8[slot][:, o:o + 128], hT[:, kc, HM:HM + T], kc == 0, kc == 7,
                                   [("w8", slot), ("h", kc)], [("ps", banks[p4])])
                        cinfo["banks"] = banks
                    cinfo[n] = cinfo["banks"][n]
                    return
                slot = W8.get(base + g)
                bank = next_bank()
                for kc in range(8):
                    o = kc * 512 + pos * 128
                    mm(psb[bank][:], w8[slot][:, o:o + 128], hT[:, kc, HM:HM + T], kc == 0, kc == 7,
                       [("w8", slot), ("h", kc)], [("ps", bank)])
                cinfo[n] = bank

            def c_evac(n):
                j, isg = n // 2, n % 2
                bank = cinfo[n]
                r = j % 2
                if not isg:
                    S.op("act", lambda: nc.scalar.activation(out=vh[r][:], in_=psb[bank][:], func=AF.Identity,
                                                             scale=0.5, bias=der[:, 64 + j:65 + j]),
                         reads=[("ps", bank), "der"], writes=[("vh", r)])
                else:
                    S.op("act", lambda: nc.scalar.activation(out=th[r][:], in_=psb[bank][:], func=AF.Tanh,
                                                             scale=0.5, bias=der[:, 72 + j:73 + j]),
                         reads=[("ps", bank), "der"], writes=[("th", r)])

            def c_glu(n):
                j, isg = n // 2, n % 2
                r = j % 2
                if isg:
                    S.op("dve", lambda: nc.vector.scalar_tensor_tensor(u31[:, j, CM:CM + T], th[r][:], 1.0, vh[r][:],
                                                                      ALU.add, ALU.mult),
                         reads=[("th", r), ("vh", r)], writes=[("u31", j)])

            cstages = [c_pe, c_evac, c_glu]
            for it in range(16 + len(cstages) - 1):
                for k, fn in enumerate(cstages):
                    n = it - k
                    if 0 <= n < 16:
                        fn(n)
            based = i * 36 + 15

            def ln_stats(j):
                stats_mm(6, Dt[:, j, :], j, [("D", j)])
                stats_mm(7, sq[:, j * T:(j + 1) * T], j, [("sq", j)])

            for j in range(NCH):
                bank = next_bank()
                for k in range(31):
                    idx = j * 31 + k
                    slot = W8.get(based + idx // 32)
                    p0 = (idx % 32) * 128
                    mm(psb[bank][:], w8[slot][:, p0:p0 + 128], u31[:, j, k:k + T], k == 0, k == 30,
                       [("w8", slot), ("u31", j), "u31m"], [("ps", bank)])
                S.op("act", lambda: nc.scalar.activation(out=Dt[:, j, :], in_=psb[bank][:], func=AF.Identity,
                                                         bias=V("bdw", j)),
                     reads=[("ps", bank), "vecs"], writes=[("D", j)])
                S.op("act", lambda: nc.scalar.activation(out=sq[:, j * T:(j + 1) * T], in_=Dt[:, j, :], func=AF.Square),
                     reads=[("D", j)], writes=[("sq", j)])
                if j > 0:
                    ln_stats(j - 1)
            ln_stats(NCH - 1)
            keep_warm(40)
            S.op("pool", lambda: nc.gpsimd.tensor_copy(u31[:, :, 0:CM], u31[:, :, T:T + CM]),
                 reads=[("u31", j) for j in range(NCH)], writes=["u31m"])
            S.op("dve", lambda: nc.vector.tensor_copy(mean[:], psb[6][:]), reads=[("ps", 6)], writes=["mean"])
            S.op("act", lambda: nc.scalar.activation(out=m2[:], in_=psb[6][:], func=AF.Square),
                 reads=[("ps", 6)], writes=["m2"])
            S.op("dve", lambda: nc.vector.tensor_tensor(ms[:], psb[7][:], m2[:], ALU.subtract),
                 reads=[("ps", 7), "m2"], writes=["ms"])
            S.op("dve", lambda: nc.vector.tensor_scalar(ms[:], ms[:], 0.0, EPS, ALU.max, ALU.add),
                 reads=["ms"], writes=["ms"])
            newton_rsqrt(ms[:], ["ms"], False, rstdA, "rstdA")
            def ln_d(j):
                S.op("dve", lambda: nc.vector.tensor_tensor(t32[:, j, :], Dt[:, j, :], mean[:], ALU.subtract),
                     reads=[("D", j), "mean"], writes=[("t32", j)])

            def ln_e(j):
                S.op("dve", lambda: nc.vector.tensor_tensor(t32[:, j, :], t32[:, j, :], rstdA[:], ALU.mult),
                     reads=[("t32", j), "rstdA"], writes=[("t32", j)])
                S.op("act", lambda: nc.scalar.activation(out=sT[:, j, :], in_=t32[:, j, :], func=AF.Silu,
                                                         scale=V("lng", j), bias=V("lnb", j)),
                     reads=[("t32", j), "vecs"], writes=[("sT", j)])

            ln_d(0)
            for j in range(1, NCH):
                ln_d(j)
                ln_e(j - 1)
            ln_e(NCH - 1)
            slot = W8.get(i * 36 + 23)
            banks0 = [next_bank() for _ in range(4)]
            for kc in range(8):
                for p4 in range(4):
                    o = kc * 512 + p4 * 128
                    mm(psb[banks0[p4]][:], w8[slot][:, o:o + 128], sT[:, kc, :], kc == 0, kc == 7,
                       [("w8", slot), ("sT", kc)], [("ps", banks0[p4])])
            for m in range(NCH):
                g, pos = m // 4, m % 4
                if m < 4:
                    bank = banks0[m]
                else:
                    slot = W8.get(i * 36 + 23 + g)
                    bank = next_bank()
                    for kc in range(8):
                        o = kc * 512 + pos * 128
                        mm(psb[bank][:], w8[slot][:, o:o + 128], sT[:, kc, :], kc == 0, kc == 7,
                           [("w8", slot), ("sT", kc)], [("ps", bank)])
                post_chunk(s, m, bank, 1.0, V("bpw2", m))
                if m > 0:
                    post_stats(m - 1)
            post_stats(NCH - 1)
            post_finish(s, 1, 0)

        for i in range(n_tiles):
            s = i % 2
            if i == 0:
                pool_front(i, s)
            pool_back(i, s)
            ffn(i, s, 0)
            conformer(i, s)
            if i + 1 < n_tiles:
                load_x(i + 1)
                ffn(i, s, 1, early_front_steps(i + 1, (i + 1) % 2))
            else:
                ffn(i, s, 1)
            store_x(i)
        for k in [("o", 0), ("o", 1)]:
            if k in S.dma_sems:
                S._wait("sp", ("d_%s" % (k,), S.dma_sems[k], S.dma_cnt[k]))
        S.flush_all()
    return nc, S


def _g8(Wc):
    return np.ascontiguousarray(Wc.reshape(8, 128, 512).transpose(1, 0, 2)).reshape(128, 4096)


def _cols(order):
    return np.concatenate([np.arange(ch * 128, (ch + 1) * 128) for ch in order])


def _fm(v):
    v = np.asarray(v, np.float32)
    return v.reshape(-1, 128).T


def prepare_inputs(inputs):
    f = lambda k: np.asarray(inputs[k], dtype=np.float32)
    x, c = f("x"), f("c")
    ada_w, ada_b = f("ada_w"), f("ada_b")
    ffn_w_up, ffn_w_dw, ffn_w_down = f("ffn_w_up"), f("ffn_w_dw"), f("ffn_w_down")
    w8f = np.empty((N_W8, 128, 4096), np.float32)
    up_order = []
    for q in range(NQ):
        up_order += [q, NQ + q]
    ucols = _cols(up_order)
    for l in range(2):
        Wp = ffn_w_up[l][:, ucols]
        for g in range(11):
            w8f[(0 if l == 0 else 17) + g] = _g8(Wp[:, g * 512:(g + 1) * 512])
    p1_order = []
    for j in range(8):
        p1_order += [j, 8 + j]
    Wp = f("cv_w_pw1")[0][:, _cols(p1_order)]
    for g in range(4):
        w8f[11 + g] = _g8(Wp[:, g * 512:(g + 1) * 512])
    Wp = f("cv_w_pw2")[0]
    for g in range(2):
        w8f[15 + g] = _g8(Wp[:, g * 512:(g + 1) * 512])
    wdf = np.empty((N_WD, 128, NQ * 128), np.float32)
    for l in range(2):
        for m in range(8):
            blk = ffn_w_down[l][:, m * 128:(m + 1) * 128]
            wdf[l * 8 + m] = np.ascontiguousarray(blk.reshape(NQ, 128, 128).transpose(1, 0, 2)).reshape(128, NQ * 128)
    adaf = np.empty((24, 128, 4096), np.float32)
    for hb in range(24):
        blk, half = hb // 2, hb % 2
        l, i = blk // 6, blk % 6
        adaf[hb] = _g8(ada_w[l][:, i * 1024 + half * 512:i * 1024 + half * 512 + 512])
    pw = f("pool_w")[0]
    poolwf = np.ascontiguousarray(pw.reshape(4, 2, 128, 256).transpose(2, 0, 1, 3)).reshape(128, 2048)

    vec_common = np.zeros((128, NV), np.float32)

    def put(name, arr):
        arr = np.asarray(arr, np.float32)
        vec_common[:, _VOFF[name]:_VOFF[name] + arr.shape[1]] = arr

    put("adab", np.concatenate([_fm(ada_b[l]) for l in range(2)], axis=1))
    put("preg", np.concatenate([_fm(f("pre_g")[l, j]) for l in range(2) for j in range(2)], axis=1))
    put("postg", np.concatenate([_fm(f("post_g")[l, j]) for l in range(2) for j in range(2)], axis=1))
    put("pscale", _fm(f("pool_scale")[0]))
    put("bpw1", _fm(f("cv_b_pw1")[0]))
    w31 = f("cv_w_dw")[0]
    put("w31", np.ascontiguousarray(w31.reshape(31, 8, 128).transpose(2, 1, 0)).reshape(128, 248))
    put("bdw", _fm(f("cv_b_dw")[0]))
    put("lng", _fm(f("cv_ln_g")[0]))
    put("lnb", _fm(f("cv_ln_b")[0]))
    put("bpw2", _fm(f("cv_b_pw2")[0]))
    put("wdw", np.concatenate(
        [np.ascontiguousarray(ffn_w_dw[l].reshape(3, NA, 128).transpose(2, 1, 0)).reshape(128, NA * 3)
         for l in range(2)], axis=1))
    put("ident", np.eye(128, dtype=np.float32))
    corr = np.zeros((4, 16), np.float32)
    for g in range(4):
        w = 2 << g
        for t in range(16):
            corr[g, t] = w / min(t + 1, w)
    put("corr", np.broadcast_to(corr.reshape(1, 64), (128, 64)))

    in_maps = []
    for b in range(NB):
        vb = vec_common.copy()
        vb[:, _VOFF["c"]:_VOFF["c"] + 8] = _fm(c[b])
        xTb = np.ascontiguousarray(x[b].T.reshape(NCH, 128, SEQ).transpose(1, 0, 2))
        in_maps.append({"xT": xTb, "vecs": vb, "w8f": w8f, "wdf": wdf, "adaf": adaf, "poolwf": poolwf})
    return in_maps


def kernel(**inputs):
    in_maps = prepare_inputs(inputs)
    nc, _ = build_program()
    res = run_bass_kernel_spmd(nc, in_maps, core_ids=list(range(NB)))
    out = np.empty((NB, SEQ, D), np.float32)
    for b in range(NB):
        o = np.asarray(res.results[b]["outT"], dtype=np.float32)
        out[b] = o.transpose(1, 0, 2).reshape(D, SEQ).T
    return out
```
